# Optimizing a Trainium2 kernel written in Bass

```python
import math
import jax
import jax.numpy as jnp
from jax import lax
import numpy as np

D_MODEL = 1024
BATCH = 8
SEQ = 2048
DEPTH = 4

GRID_W = 64
CTX_LEN = 256
N_MOD = 9
FFN_HIDDEN = 2816
EPS = 1e-6
ROPE_THETA = 10000.0
Q_BLOCK = 128
GROUP_W = D_MODEL // 4
D_MIX = 4 * GROUP_W
HEAD_DIM = 64
ML_HEADS = GROUP_W // HEAD_DIM
ML_HEAD_DIM = HEAD_DIM
ML_CHUNK = 64
ML_CONV = 3
ML_N_GATES = 4 * ML_HEADS
GQA_HEADS = GROUP_W // HEAD_DIM
GQA_KV_HEADS = GQA_HEADS // 2
HY_CH = GROUP_W
HY_ORDER = 2
HY_SHORT = 3
HY_POS_BANDS = 16
HY_POS_DIM = 1 + 2 * HY_POS_BANDS
HY_FILT_HIDDEN = 64
HY_DECAY_TARGET = 1e-2
HY_FAST_DECAY = 0.3
HY_SLOW_DECAY = 1.5
DIFF_HEADS = GROUP_W // HEAD_DIM
DIFF_SUB_DIM = HEAD_DIM // 2
N_IN_ML = 4 * GROUP_W + ML_N_GATES
N_IN_GQA = (GQA_HEADS + 2 * GQA_KV_HEADS) * HEAD_DIM
N_IN_HY = (HY_ORDER + 1) * HY_CH
N_IN_DIFF = 3 * GROUP_W
N_IN = N_IN_ML + N_IN_GQA + N_IN_HY + N_IN_DIFF
IN_SPLITS = [N_IN_ML, N_IN_ML + N_IN_GQA, N_IN_ML + N_IN_GQA + N_IN_HY]

kernel_name = 'hybrid_mlstm_gqa_hyena_diffattn_dit_trunk'


def _rmsnorm(x, g):
    xf = x.astype(jnp.float32)
    y = xf * lax.rsqrt(jnp.mean(xf * xf, axis=-1, keepdims=True) + EPS)
    return (y * g.astype(jnp.float32)).astype(x.dtype)


def _modnorm(x, g, shift, scale):
    return _rmsnorm(x, g) * (1 + scale) + shift


def _swiglu(x, w13, w2):
    a, b = jnp.split(x @ w13, 2, axis=-1)
    return (jax.nn.silu(a) * b) @ w2


def _ffn_half(x, g, shift, scale, gate, w13, w2):
    return x + 0.5 * gate * _swiglu(_modnorm(x, g, shift, scale), w13, w2)


def _split_heads(x, n_heads):
    B, L, _ = x.shape
    return x.reshape(B, L, n_heads, -1).transpose(0, 2, 1, 3)


def _merge_heads(x):
    B, H, L, d = x.shape
    return x.transpose(0, 2, 1, 3).reshape(B, L, H * d)


def _dwconv_centred(x, w, b):
    C, K = x.shape[-1], w.shape[0]
    y = lax.conv_general_dilated(x, w[:, None, :].astype(x.dtype), window_strides=(1,),
                                 padding=[(K // 2, K // 2)],
                                 dimension_numbers=('NWC', 'WIO', 'NWC'), feature_group_count=C)
    return y + b


def _axial_rope_tables(length, dim):
    rows = length // GRID_W
    row = jnp.repeat(jnp.arange(rows), GRID_W).astype(jnp.float32)
    col = jnp.tile(jnp.arange(GRID_W), rows).astype(jnp.float32)
    n_freq = dim // 4
    inv = ROPE_THETA ** (-jnp.arange(n_freq, dtype=jnp.float32) / n_freq)
    ang = jnp.concatenate([row[:, None] * inv, col[:, None] * inv], axis=-1)
    return jnp.cos(ang), jnp.sin(ang)


def _rope(x, rope):
    cos, sin = rope
    x1, x2 = jnp.split(x, 2, axis=-1)
    return jnp.concatenate([x1 * cos - x2 * sin, x1 * sin + x2 * cos], axis=-1).astype(x.dtype)


def _sweep(fn, *qs):
    B, H, L, _ = qs[0].shape
    nb = L // Q_BLOCK
    blocks = tuple(q.reshape(B, H, nb, Q_BLOCK, q.shape[-1]).transpose(2, 0, 1, 3, 4) for q in qs)
    out = lax.map(lambda bq: fn(*bq), blocks)
    return out.transpose(1, 2, 0, 3, 4).reshape(B, H, L, out.shape[-1])


def _mlstm_scan(q, k, v, log_i, log_f, state):
    B, H, L, d = q.shape
    nc = L // ML_CHUNK

    def to_chunks(t):
        return jnp.moveaxis(t.reshape(B, H, nc, ML_CHUNK, *t.shape[3:]), 2, 0)

    tril = jnp.tril(jnp.ones((ML_CHUNK, ML_CHUNK), dtype=bool))

    def step(carry, xs):
        C, n, m = carry
        qc, kc, vc, ic, fc = xs
        b = jnp.cumsum(fc, axis=-1)
        log_w = jnp.where(tril, b[..., :, None] - b[..., None, :] + ic[..., None, :], -jnp.inf)
        log_inter = b + m[..., None]
        m_t = jnp.maximum(log_inter, jnp.max(log_w, axis=-1))
        w_intra = jnp.exp(log_w - m_t[..., None])
        w_inter = jnp.exp(log_inter - m_t)
        s = jnp.einsum('bhtd,bhsd->bhts', qc, kc) * w_intra
        num = (w_inter[..., None] * jnp.einsum('bhde,bhte->bhtd', C, qc)
               + jnp.einsum('bhts,bhsd->bhtd', s, vc))
        den = w_inter * jnp.einsum('bhd,bhtd->bht', n, qc) + jnp.sum(s, axis=-1)
        h = num / jnp.maximum(jnp.abs(den), jnp.exp(-m_t))[..., None]
        b_end = b[..., -1]
        log_g = b_end[..., None] - b + ic
        m_new = jnp.maximum(b_end + m, jnp.max(log_g, axis=-1))
        w_g = jnp.exp(log_g - m_new[..., None])
        w_c = jnp.exp(b_end + m - m_new)
        C = w_c[..., None, None] * C + jnp.einsum('bhs,bhsd,bhse->bhde', w_g, vc, kc)
        n = w_c[..., None] * n + jnp.einsum('bhs,bhsd->bhd', w_g, kc)
        return (C, n, m_new), h

    xs = tuple(to_chunks(t) for t in (q, k, v, log_i, log_f))
    state, hs = lax.scan(step, state, xs)
    return jnp.moveaxis(hs, 0, 2).reshape(B, H, L, d), state


def _mlstm_bidir(q, k, v, g, state_f, state_b):
    h_f, s_f = _mlstm_scan(q, k, v, g[0], jax.nn.log_sigmoid(g[1]), state_f)
    fl = lambda t: jnp.flip(t, axis=2)
    h_b, s_b = _mlstm_scan(fl(q), fl(k), fl(v), fl(g[2]), fl(jax.nn.log_sigmoid(g[3])), state_b)
    return h_f + fl(h_b), s_f, s_b


def _mlstm_mixer(pl, pc, conv_w, conv_b, gate_b, norm_g, need_ctx):
    def prep(p):
        B, L, _ = p.shape
        q, k, v, o, g = jnp.split(p, [GROUP_W, 2 * GROUP_W, 3 * GROUP_W, 4 * GROUP_W], axis=-1)
        qk = jax.nn.silu(_dwconv_centred(jnp.concatenate([q, k], axis=-1), conv_w, conv_b))
        q, k = jnp.split(qk, 2, axis=-1)
        g = (g.astype(jnp.float32) + gate_b).reshape(B, L, 4, ML_HEADS).transpose(2, 0, 3, 1)
        heads = lambda t: _split_heads(t, ML_HEADS).astype(jnp.float32)
        return heads(q) * ML_HEAD_DIM ** -0.5, heads(k), heads(v), o, g

    def out(h, o, dtype):
        hn = _rmsnorm(h, norm_g.reshape(ML_HEADS, 1, ML_HEAD_DIM))
        return (_merge_heads(hn) * jax.nn.sigmoid(o.astype(jnp.float32))).astype(dtype)

    qc, kc, vc, oc, gc = prep(pc)
    ql, kl, vl, ol, gl = prep(pl)
    B = pl.shape[0]
    zero = (jnp.zeros((B, ML_HEADS, ML_HEAD_DIM, ML_HEAD_DIM), jnp.float32),
            jnp.zeros((B, ML_HEADS, ML_HEAD_DIM), jnp.float32),
            jnp.zeros((B, ML_HEADS), jnp.float32))
    hc, s_f, s_b = _mlstm_bidir(qc, kc, vc, gc, zero, zero)
    hl, _, _ = _mlstm_bidir(ql, kl, vl, gl, s_f, s_b)
    yc = out(hc, oc, pc.dtype) if need_ctx else None
    return out(hl, ol, pl.dtype), yc


def _gqa_block(qb, k, v):
    B, Hq, Q, d = qb.shape
    Hkv = k.shape[1]
    qg = qb.reshape(B, Hkv, Hq // Hkv, Q, d)
    s = jnp.einsum('bhgqd,bhkd->bhgqk', qg, k, preferred_element_type=jnp.float32) * d ** -0.5
    p = jax.nn.softmax(s, axis=-1)
    o = jnp.einsum('bhgqk,bhkd->bhgqd', p.astype(v.dtype), v)
    return o.reshape(B, Hq, Q, d)


def _gqa_mixer(pl, pc, qk_g, rope, need_ctx):
    def prep(p):
        q, k, v = jnp.split(p, [GQA_HEADS * HEAD_DIM, (GQA_HEADS + GQA_KV_HEADS) * HEAD_DIM], axis=-1)
        return (_rmsnorm(_split_heads(q, GQA_HEADS), qk_g[0]),
                _rmsnorm(_split_heads(k, GQA_KV_HEADS), qk_g[1]),
                _split_heads(v, GQA_KV_HEADS))

    ql, kl, vl = prep(pl)
    qc, kc, vc = prep(pc)
    ql, kl = _rope(ql, rope), _rope(kl, rope)
    k_all = jnp.concatenate([kc, kl], axis=2)
    v_all = jnp.concatenate([vc, vl], axis=2)
    yl = _merge_heads(_sweep(lambda qb: _gqa_block(qb, k_all, v_all), ql))
    yc = _merge_heads(_sweep(lambda qb: _gqa_block(qb, kc, vc), qc)) if need_ctx else None
    return yl, yc


def _hyena_filters(length, w1, b1, w2, b2, w3, b3):
    t = jnp.arange(length, dtype=jnp.float32)
    tn = t / length
    bands = jnp.arange(1, HY_POS_BANDS + 1, dtype=jnp.float32)
    ang = 2.0 * math.pi * tn[:, None] * bands
    feats = jnp.concatenate([tn[:, None], jnp.cos(ang), jnp.sin(ang)], axis=-1)
    h = jnp.sin(feats @ w1.astype(jnp.float32) + b1.astype(jnp.float32))
    h = jnp.sin(h @ w2.astype(jnp.float32) + b2.astype(jnp.float32))
    h = h @ w3.astype(jnp.float32) + b3.astype(jnp.float32)
    dist = jnp.abs(t - length // 2) / (length / 2)
    deltas = jnp.abs(jnp.linspace(math.log(HY_DECAY_TARGET) / HY_SLOW_DECAY,
                                  math.log(HY_DECAY_TARGET) / HY_FAST_DECAY, HY_CH, dtype=jnp.float32))
    h = h * jnp.exp(-dist[:, None] * jnp.tile(deltas, HY_ORDER))
    return h / jnp.sum(jnp.abs(h), axis=0, keepdims=True)


def _fft_conv_centred(u, h, skip):
    L = u.shape[1]
    n = 2 * L
    y = jnp.fft.irfft(jnp.fft.rfft(u, n=n, axis=1) * jnp.fft.rfft(h, n=n, axis=0)[None], n=n, axis=1)
    return y[:, L // 2: L // 2 + L] + u * skip


def _hyena_seq(p, conv_w, conv_b, filt, skip):
    u = _dwconv_centred(p, conv_w, conv_b).astype(jnp.float32)
    v, x1, x2 = jnp.split(u, 3, axis=-1)
    h = _hyena_filters(p.shape[1], *filt)
    skip = skip.astype(jnp.float32)
    z = x1 * _fft_conv_centred(v, h[:, :HY_CH], skip[0])
    return (x2 * _fft_conv_centred(z, h[:, HY_CH:], skip[1])).astype(p.dtype)


def _diff_block(q1b, q2b, k1, k2, v, lam):
    scale = DIFF_SUB_DIM ** -0.5
    s1 = jnp.einsum('bhqd,bhkd->bhqk', q1b, k1, preferred_element_type=jnp.float32) * scale
    s2 = jnp.einsum('bhqd,bhkd->bhqk', q2b, k2, preferred_element_type=jnp.float32) * scale
    p = jax.nn.softmax(s1, axis=-1) - lam * jax.nn.softmax(s2, axis=-1)
    return jnp.einsum('bhqk,bhkd->bhqd', p.astype(v.dtype), v)


def _diff_mixer(pl, pc, qk_g, lam_p, subln_g, lam_init, rope, need_ctx):
    def prep(p):
        q, k, v = jnp.split(p, 3, axis=-1)
        q1, q2 = jnp.split(_split_heads(q, DIFF_HEADS), 2, axis=-1)
        k1, k2 = jnp.split(_split_heads(k, DIFF_HEADS), 2, axis=-1)
        return (_rmsnorm(q1, qk_g[0]), _rmsnorm(q2, qk_g[0]),
                _rmsnorm(k1, qk_g[1]), _rmsnorm(k2, qk_g[1]), _split_heads(v, DIFF_HEADS))

    lp = lam_p.astype(jnp.float32)
    lam = jnp.exp(jnp.sum(lp[0] * lp[1])) - jnp.exp(jnp.sum(lp[2] * lp[3])) + lam_init
    q1l, q2l, k1l, k2l, vl = prep(pl)
    q1l, q2l, k1l, k2l = (_rope(t, rope) for t in (q1l, q2l, k1l, k2l))
    q1c, q2c, k1c, k2c, vc = prep(pc)

    def out(q1, q2, k1, k2, v):
        y = _sweep(lambda a, b: _diff_block(a, b, k1, k2, v, lam), q1, q2)
        return _merge_heads(_rmsnorm(y, subln_g) * (1.0 - lam_init))

    cat = lambda a, b: jnp.concatenate([a, b], axis=2)
    yl = out(q1l, q2l, cat(k1c, k1l), cat(k2c, k2l), cat(vc, vl))
    yc = out(q1c, q2c, k1c, k2c, vc) if need_ctx else None
    return yl, yc


def setup_inputs(seed: int = 0) -> dict:
    key = jax.random.key(seed)
    it = iter(list(jax.random.split(key, 32)))
    nrm = lambda shape, scale: jax.random.normal(next(it), shape, jnp.float32) * scale
    f_bias = jnp.linspace(3.0, 6.0, ML_HEADS, dtype=jnp.float32)
    z = jnp.zeros((ML_HEADS,), jnp.float32)
    gate_struct = jnp.concatenate([z, f_bias, z, f_bias])
    return {
        'x': nrm((BATCH, SEQ, D_MODEL), 1.0),
        'c': nrm((BATCH, D_MODEL), 1.0),
        'ctx': nrm((BATCH, CTX_LEN, D_MODEL), 1.0),
        'c_ctx': nrm((D_MODEL,), 1.0),
        'ada_w': nrm((DEPTH, D_MODEL, N_MOD * D_MODEL), 0.5 * D_MODEL ** -0.5),
        'ada_b': nrm((DEPTH, N_MOD * D_MODEL), 0.01),
        'norm_g': 1.0 + nrm((DEPTH, 3, D_MODEL), 0.02),
        'ffn_w13': nrm((DEPTH, 2, D_MODEL, 2 * FFN_HIDDEN), D_MODEL ** -0.5),
        'ffn_w2': nrm((DEPTH, 2, FFN_HIDDEN, D_MODEL), FFN_HIDDEN ** -0.5),
        'w_in': nrm((DEPTH, D_MODEL, N_IN), D_MODEL ** -0.5),
        'w_out': nrm((DEPTH, D_MIX, D_MODEL), D_MIX ** -0.5),
        'ml_gate_b': gate_struct[None] + nrm((DEPTH, ML_N_GATES), 0.1),
        'ml_conv_w': nrm((DEPTH, ML_CONV, 2 * GROUP_W), ML_CONV ** -0.5),
        'ml_conv_b': nrm((DEPTH, 2 * GROUP_W), 0.02),
        'ml_norm_g': 1.0 + nrm((DEPTH, GROUP_W), 0.02),
        'gqa_qk_g': 1.0 + nrm((DEPTH, 2, HEAD_DIM), 0.02),
        'hy_conv_w': nrm((DEPTH, HY_SHORT, N_IN_HY), HY_SHORT ** -0.5),
        'hy_conv_b': nrm((DEPTH, N_IN_HY), 0.02),
        'hy_filt_w1': nrm((DEPTH, HY_POS_DIM, HY_FILT_HIDDEN), 1.0),
        'hy_filt_b1': nrm((DEPTH, HY_FILT_HIDDEN), 0.1),
        'hy_filt_w2': nrm((DEPTH, HY_FILT_HIDDEN, HY_FILT_HIDDEN), HY_FILT_HIDDEN ** -0.5),
        'hy_filt_b2': nrm((DEPTH, HY_FILT_HIDDEN), 0.1),
        'hy_filt_w3': nrm((DEPTH, HY_FILT_HIDDEN, HY_ORDER * HY_CH), HY_FILT_HIDDEN ** -0.5),
        'hy_filt_b3': nrm((DEPTH, HY_ORDER * HY_CH), 0.1),
        'hy_skip': nrm((DEPTH, HY_ORDER, HY_CH), 0.1),
        'diff_qk_g': 1.0 + nrm((DEPTH, 2, DIFF_SUB_DIM), 0.02),
        'diff_lambda': nrm((DEPTH, 4, DIFF_SUB_DIM), 0.1),
        'diff_subln_g': 1.0 + nrm((DEPTH, 2 * DIFF_SUB_DIM), 0.02),
    }


def reference(x, c, ctx, c_ctx, ada_w, ada_b, norm_g, ffn_w13, ffn_w2, w_in, w_out,
              ml_gate_b, ml_conv_w, ml_conv_b, ml_norm_g, gqa_qk_g,
              hy_conv_w, hy_conv_b, hy_filt_w1, hy_filt_b1, hy_filt_w2, hy_filt_b2, hy_filt_w3, hy_filt_b3,
              hy_skip, diff_qk_g, diff_lambda, diff_subln_g):
    B, L, D = x.shape
    rope_gqa = _axial_rope_tables(L, HEAD_DIM)
    rope_diff = _axial_rope_tables(L, DIFF_SUB_DIM)
    sc, scc = jax.nn.silu(c), jax.nn.silu(c_ctx)
    h, hc = x, ctx
    for l in range(DEPTH):
        need_ctx = l < DEPTH - 1
        mod_l = (sc @ ada_w[l] + ada_b[l]).reshape(B, 1, N_MOD, D)
        mod_c = (scc @ ada_w[l] + ada_b[l]).reshape(N_MOD, D)
        lam_init = 0.8 - 0.6 * math.exp(-0.3 * l)
        filt = (hy_filt_w1[l], hy_filt_b1[l], hy_filt_w2[l], hy_filt_b2[l], hy_filt_w3[l], hy_filt_b3[l])

        h = _ffn_half(h, norm_g[l, 0], mod_l[:, :, 0], mod_l[:, :, 1], mod_l[:, :, 2], ffn_w13[l, 0], ffn_w2[l, 0])
        hc = _ffn_half(hc, norm_g[l, 0], mod_c[0], mod_c[1], mod_c[2], ffn_w13[l, 0], ffn_w2[l, 0])

        pl = _modnorm(h, norm_g[l, 1], mod_l[:, :, 3], mod_l[:, :, 4]) @ w_in[l]
        pc = _modnorm(hc, norm_g[l, 1], mod_c[3], mod_c[4]) @ w_in[l]
        pl_ml, pl_gqa, pl_hy, pl_diff = jnp.split(pl, IN_SPLITS, axis=-1)
        pc_ml, pc_gqa, pc_hy, pc_diff = jnp.split(pc, IN_SPLITS, axis=-1)

        ym_l, ym_c = _mlstm_mixer(pl_ml, pc_ml, ml_conv_w[l], ml_conv_b[l], ml_gate_b[l], ml_norm_g[l], need_ctx)
        yg_l, yg_c = _gqa_mixer(pl_gqa, pc_gqa, gqa_qk_g[l], rope_gqa, need_ctx)
        yh_l = _hyena_seq(pl_hy, hy_conv_w[l], hy_conv_b[l], filt, hy_skip[l])
        yd_l, yd_c = _diff_mixer(pl_diff, pc_diff, diff_qk_g[l], diff_lambda[l], diff_subln_g[l],
                                 lam_init, rope_diff, need_ctx)

        h = h + mod_l[:, :, 5] * (jnp.concatenate([ym_l, yg_l, yh_l, yd_l], axis=-1) @ w_out[l])
        h = _ffn_half(h, norm_g[l, 2], mod_l[:, :, 6], mod_l[:, :, 7], mod_l[:, :, 8], ffn_w13[l, 1], ffn_w2[l, 1])

        if need_ctx:
            yh_c = _hyena_seq(pc_hy, hy_conv_w[l], hy_conv_b[l], filt, hy_skip[l])
            hc = hc + mod_c[5] * (jnp.concatenate([ym_c, yg_c, yh_c, yd_c], axis=-1) @ w_out[l])
            hc = _ffn_half(hc, norm_g[l, 2], mod_c[6], mod_c[7], mod_c[8], ffn_w13[l, 1], ffn_w2[l, 1])
    return h
```

```python
import math
import collections
import numpy as np
import ml_dtypes
import concourse.bass as bass
import concourse.mybir as mybir
from concourse.bass_utils import run_bass_kernel_spmd

F32 = mybir.dt.float32
BF16 = mybir.dt.bfloat16
AF = mybir.ActivationFunctionType
ALU = mybir.AluOpType
NPBF = ml_dtypes.bfloat16

D = 1024
L = 2048
CL = 256
T = CL + L
KC = 8
NL = 4
HID = 2816
NJ = 22
EPS = 1e-6
TBS = [(0, 256, 1), (256, 512, 0), (768, 512, 0), (1280, 512, 0), (1792, 512, 0)]
NSC = 18
EPOCH = 30000


def tbi_of_sc(sc):
    return 0 if sc < 2 else 1 + (sc - 2) // 4


class Buf:
    __slots__ = ("name", "w", "r")

    def __init__(self, name):
        self.name = name
        self.w = None
        self.r = []


class BufMap(dict):
    def __missing__(self, k):
        b = Buf(k)
        self[k] = b
        return b


class Sched:
    def __init__(self, nc, n_dma_sems=28):
        self.nc = nc
        self.h = {"pe": nc.tensor, "dve": nc.vector, "act": nc.scalar, "pool": nc.gpsimd, "sp": nc.sync}
        self.cnt = {k: 0 for k in self.h}
        self.sems = {k: [] for k in self.h}
        self.known = {k: {} for k in self.h}
        self.dma_sems = []
        self.dma_val = []
        self.n_dma_sems = n_dma_sems
        self.dma_rr = 0
        self._ctx = []
        self.pending = {}
        self.fuse_waits = True

    def _new_sem(self, name):
        g = self.nc.semaphore(name)
        s = g.__enter__()
        self._ctx.append(g)
        return s

    def close(self):
        for g in reversed(self._ctx):
            g.__exit__(None, None, None)
        self._ctx = []

    def _wait(self, eng, ev):
        if ev is None:
            return
        if ev[0] == "e":
            _, src, n = ev
            if src == eng and eng == "pe":
                return
            if self.known[eng].get(src, 0) >= n:
                return
            self.known[eng][src] = n
            ep, v = (n - 1) // EPOCH, (n - 1) % EPOCH + 1
            self._emit_wait(eng, self.sems[src][ep], v)
        else:
            _, si, val = ev
            key = ("d", si)
            if self.known[eng].get(key, 0) >= val:
                return
            self.known[eng][key] = val
            self._emit_wait(eng, self.dma_sems[si], val)

    def _emit_wait(self, eng, sem, val):
        if self.pending.get(eng) is not None:
            ps, pv = self.pending[eng]
            self.h[eng].wait_ge(ps, pv)
        self.pending[eng] = (sem, val)

    def _flush(self, eng, ins=None):
        p = self.pending.get(eng)
        if p is None:
            return
        self.pending[eng] = None
        if ins is not None and self.fuse_waits:
            ins._wait_ge(p[0], p[1])
        else:
            self.h[eng].wait_ge(p[0], p[1])

    def _deps(self, eng, reads, writes):
        for b in reads:
            self._wait(eng, b.w)
        for b in writes:
            self._wait(eng, b.w)
            for ev in b.r:
                if ev[0] == "e" and ev[1] == eng:
                    continue
                self._wait(eng, ev)

    def _mark(self, ev, reads, writes):
        for b in reads:
            b.r = [e for e in b.r if not (e[0] == ev[0] and e[1] == ev[1])] + [ev]
        for b in writes:
            b.w = ev
            b.r = []

    def op(self, eng, fn, reads=(), writes=()):
        self._deps(eng, reads, writes)
        if eng != "pe":
            self._flush(eng)
        ins = fn()
        if eng == "pe":
            self._flush(eng, ins)
        self.cnt[eng] += 1
        n = self.cnt[eng]
        ep = (n - 1) // EPOCH
        while len(self.sems[eng]) <= ep:
            self.sems[eng].append(self._new_sem("s_%s_%d" % (eng, len(self.sems[eng]))))
        ins.then_inc(self.sems[eng][ep], 1)
        self._mark(("e", eng, n), reads, writes)
        return ins

    def dma(self, out, in_, reads=(), writes=(), q="sp"):
        if len(self.dma_sems) < self.n_dma_sems:
            self.dma_sems.append(self._new_sem("s_dma_%d" % len(self.dma_sems)))
            self.dma_val.append(0)
        si = self.dma_rr % self.n_dma_sems
        self.dma_rr += 1
        if self.dma_val[si] > 0:
            self._wait(q, ("d", si, self.dma_val[si]))
        self._deps(q, reads, writes)
        self._flush(q)
        ins = self.h[q].dma_start(out=out, in_=in_)
        self.dma_val[si] += 16
        ins.then_inc(self.dma_sems[si], 16)
        ev = ("d", si, self.dma_val[si])
        self._mark(ev, reads, writes)
        return ev

    def barrier(self):
        for eng in self.h:
            for src in self.h:
                if src != eng and self.cnt[src] > 0:
                    self._wait(eng, ("e", src, self.cnt[src]))
            for si in range(len(self.dma_sems)):
                if self.dma_val[si] > 0:
                    self._wait(eng, ("d", si, self.dma_val[si]))
            self._flush(eng)


class Ring:
    def __init__(self, items):
        self.items = items
        self.i = 0

    def next(self):
        it = self.items[self.i % len(self.items)]
        self.i += 1
        return it


class Arena:
    def __init__(self, nc, lo, hi):
        self.nc, self.lo, self.hi, self.p = nc, lo, hi, lo
        self.n = 0

    def alloc(self, shape, dt, at=None):
        nbytes = int(np.prod(shape[1:])) * (2 if dt == BF16 else 4)
        nbytes = (nbytes + 63) // 64 * 64
        if at is None:
            at = self.p
            self.p += nbytes
            assert self.p <= self.hi, ("SBUF arena overflow", self.p, self.hi)
        self.n += 1
        return self.nc.alloc_sbuf_tensor_at("a%d" % self.n, list(shape), dt, offset=at)

    def mark(self):
        return self.p

    def release(self, m):
        self.p = m


def _rope_tables(dim):
    rows = L // 64
    row = np.repeat(np.arange(rows), 64).astype(np.float32)
    col = np.tile(np.arange(64), rows).astype(np.float32)
    nf = dim // 4
    inv = (np.float32(10000.0) ** (-np.arange(nf, dtype=np.float32) / np.float32(nf))).astype(np.float32)
    ang = np.concatenate([row[:, None] * inv, col[:, None] * inv], axis=-1).astype(np.float32)
    half = dim // 2
    idx = np.arange(128) % half
    cosT = np.cos(ang)[:, idx].T.astype(np.float32)
    sinT = np.sin(ang)[:, idx].T.astype(np.float32)
    return np.ascontiguousarray(cosT), np.ascontiguousarray(sinT)


def _rot_mat(dim):
    half = dim // 2
    m = np.zeros((128, 128), np.float32)
    for o in range(128):
        if o % dim < half:
            m[o + half, o] = -1.0
        else:
            m[o - half, o] = 1.0
    return m


def _dft_consts(Ln):
    n = 2 * Ln
    t = np.arange(Ln, dtype=np.float64)
    f = np.arange(Ln, dtype=np.float64)
    ang = 2.0 * np.pi * np.outer(t, f) / n
    FC = np.cos(ang)
    FS = -np.sin(ang)
    cf = np.full(Ln, 2.0)
    cf[0] = 1.0
    angi = 2.0 * np.pi * np.outer(f, t + Ln // 2) / n
    IC = cf[:, None] / n * np.cos(angi)
    IS = -cf[:, None] / n * np.sin(angi)
    alt = (-1.0) ** t
    icny = alt / n
    return FC, FS, IC, IS, alt, icny


def _hy_feats(Ln):
    t = np.arange(Ln, dtype=np.float32)
    tn = (t / np.float32(Ln)).astype(np.float32)
    bands = np.arange(1, 17, dtype=np.float32)
    ang = (np.float32(2.0 * math.pi) * tn[:, None] * bands).astype(np.float32)
    feats = np.concatenate([tn[:, None], np.cos(ang), np.sin(ang)], axis=-1).astype(np.float32)
    dist = (np.abs(t - Ln // 2) / np.float32(Ln / 2)).astype(np.float32)
    deltas = np.abs(np.linspace(math.log(1e-2) / 1.5, math.log(1e-2) / 0.3, 256, dtype=np.float32))
    win = np.exp(-dist[:, None] * np.tile(deltas, 2)).astype(np.float32)
    return np.ascontiguousarray(feats.T), win


_CONST_CACHE = {}


def host_consts():
    if _CONST_CACHE:
        return _CONST_CACHE
    c = {}
    c["ones_bf"] = np.ones((128, 128), NPBF)
    c["ones_f"] = np.ones((128, 128), np.float32)
    b64 = np.zeros((128, 128), np.float32)
    b64[:64, :64] = 1
    b64[64:, 64:] = 1
    c["blk64"] = b64.astype(NPBF)
    b32 = np.zeros((128, 128), np.float32)
    for i in range(4):
        b32[32 * i:32 * i + 32, 32 * i:32 * i + 32] = 1
    c["blk32"] = b32.astype(NPBF)
    c["ident_bf"] = np.eye(128, dtype=np.float32).astype(NPBF)
    c["ident_f"] = np.eye(128, dtype=np.float32)
    sel = np.zeros((4, 4, 128), np.float32)
    for hh in range(4):
        sel[hh, hh, :] = 1
    c["sel4"] = sel
    c["rot64"] = _rot_mat(64)
    c["rot32"] = _rot_mat(32)
    c["cos64"], c["sin64"] = _rope_tables(64)
    c["cos32"], c["sin32"] = _rope_tables(32)
    p = np.arange(128)[:, None]
    xx = np.arange(896)[None, :] - 384
    c["mask_f"] = ((xx - p) >= 0).astype(np.float32).astype(NPBF)
    c["mask_b"] = ((xx - p) <= 0).astype(np.float32).astype(NPBF)
    hm = np.zeros((128, 2), np.float32)
    hm[(np.arange(128) % 64) < 32, 0] = 1
    hm[(np.arange(128) % 64) >= 32, 1] = 1
    c["halfmask"] = hm
    h64 = np.zeros((128, 2), np.float32)
    h64[:64, 0] = 1
    h64[64:, 1] = 1
    c["hm64"] = h64
    q32 = np.zeros((128, 4), np.float32)
    for i in range(4):
        q32[32 * i:32 * i + 32, i] = 1
    c["qm32"] = q32
    for nm, Ln in (("l", L), ("c", CL)):
        FC, FS, IC, IS, alt, icny = _dft_consts(Ln)
        ntc = Ln // 128
        F = np.stack([FC, FS], 0).reshape(2, ntc, 128, ntc, 128)
        c["F_" + nm] = np.ascontiguousarray(F.transpose(3, 2, 1, 0, 4)).astype(NPBF)
        c["alt_" + nm] = np.ascontiguousarray(alt.reshape(ntc, 128).T).astype(NPBF)
        I = np.stack([IC, IS], 0).reshape(2, ntc, 128, Ln)
        c["I_" + nm] = np.ascontiguousarray(I.transpose(2, 1, 0, 3)).astype(NPBF)
        c["icny_" + nm] = icny.reshape(1, Ln).astype(NPBF)
        ft, win = _hy_feats(Ln)
        c["feats_" + nm] = ft
        c["win_" + nm] = np.ascontiguousarray(win.reshape(ntc, 128, 512).transpose(1, 0, 2))
    _CONST_CACHE.update(c)
    return c


def _npdt(a):
    return BF16 if a.dtype == NPBF else F32


def build(consts, n_layers=NL, dbg=None):
    nc = bass.Bass("TRN2", target_bir_lowering=False)
    S = Sched(nc)
    B = BufMap()
    V, ACT, PE = nc.vector, nc.scalar, nc.tensor

    def din(name, shape, dt=F32):
        return nc.dram_tensor(name, list(shape), dt, kind="ExternalInput").ap()

    xT = din("xT", [D, L])
    ctxT = din("ctxT", [D, CL])
    cc = din("cc", [128, KC, 2])
    ada_w = din("ada_w", [NL, D, 9 * D])
    adabT = din("adabT", [128, NL * 72])
    normgT = din("normgT", [128, NL * 3 * KC])
    w13 = din("ffn_w13", [NL, 2, D, 2 * HID])
    w2 = din("ffn_w2", [NL, 2, HID, D])
    w_in = din("w_in", [NL, D, 3088])
    w_out = din("w_out", [NL, D, D])
    mlp = din("mlp", [128, NL, 24])
    mlgb = din("mlgb", [4, NL, 4])
    gqp = din("gqp", [128, NL, 2])
    hyp = din("hyp", [128, NL, 28])
    hyf1 = din("hyf1", [NL, 33, 64])
    hyf2 = din("hyf2", [NL, 64, 64])
    hyf3 = din("hyf3", [NL, 64, 512])
    hyfb = din("hyfb", [64, NL, 2])
    hyb3 = din("hyb3", [NL, 1, 512])
    dfp = din("dfp", [128, NL, 3])
    dlam = din("dlam", [128, NL, 128])
    cd = {k: din("c_" + k, v.shape, _npdt(v)) for k, v in consts.items()}
    outT = nc.dram_tensor("outT", [D, L], F32, kind="ExternalOutput").ap()
    dbgT = nc.dram_tensor("dbgT", [D, T], F32, kind=("ExternalOutput" if dbg is not None else "Internal")).ap()
    Hl = nc.dram_tensor("Hl", [NL, 16, 128, 2, 2, 256], BF16, kind="Internal").ap()
    Hlny = nc.dram_tensor("Hlny", [NL, 1, 512], BF16, kind="Internal").ap()
    Hc = nc.dram_tensor("Hc", [NL, 2, 128, 2, 2, 256], BF16, kind="Internal").ap()
    Hcny = nc.dram_tensor("Hcny", [NL, 1, 512], BF16, kind="Internal").ap()

    A = Arena(nc, 16640, 229376 - 256)
    psF = [nc.alloc_psum_tensor("psf%d" % i, [128, 512], F32) for i in range(8)]
    psA = Ring([(psF[i], B[("ps", i)]) for i in range(4)])
    psD = psF[3]
    psR = Ring([(psF[i], B[("ps", i)]) for i in range(4, 8)])

    def mm(out, lhsT, rhs, start, stop, reads, writes):
        S.op("pe", lambda: PE.matmul(out, lhsT=lhsT, rhs=rhs, start=start, stop=stop), reads, writes)

    def ld_const(name, q="sp"):
        a = consts[name]
        t = A.alloc(a.shape, _npdt(a))
        S.dma(t[:], cd[name], writes=[B[("c", name)]], q=q)
        return t

    ones_bf = ld_const("ones_bf")
    ones_f = ld_const("ones_f")
    blk64 = ld_const("blk64")
    blk32 = ld_const("blk32")
    ident_bf = ld_const("ident_bf")
    ident_f = ld_const("ident_f")
    sel4 = ld_const("sel4")
    halfmask = ld_const("halfmask")
    hm64 = ld_const("hm64")
    qm32 = ld_const("qm32")
    CB = [B[("c", n)] for n in ("ones_bf", "ones_f", "blk64", "blk32", "ident_bf", "ident_f", "sel4", "halfmask", "hm64", "qm32")]
    epsT = A.alloc([128, 4], F32)
    S.op("dve", lambda: V.memset(epsT[:, 0:1], EPS), writes=[B["eps"]])
    S.op("dve", lambda: V.memset(epsT[:, 1:2], -math.pi), writes=[B["eps"]])
    S.op("dve", lambda: V.memset(epsT[:, 2:3], 1.0), writes=[B["eps"]])
    S.op("dve", lambda: V.memset(epsT[:, 3:4], 0.0), writes=[B["eps"]])
    CB.append(B["eps"])

    dconst = A.alloc([128, 512], BF16)
    S.op("dve", lambda: V.memset(dconst[:], 1.0), writes=[B["dconst"]])
    CB.append(B["dconst"])

    def keep_warm(n):
        S.op("pe", lambda: PE.matmul(psD[:, 0:n], lhsT=ones_bf[:, :], rhs=dconst[:, 0:n], start=True, stop=True))

    def small(name, ap, shape):
        t = A.alloc(shape, F32)
        S.dma(t[:], ap, writes=[B[("c", name)]])
        CB.append(B[("c", name)])
        return t

    adab = small("adab", adabT, [128, NL * 72])
    normg = small("normg", normgT, [128, NL * 3 * KC])
    mlp_t = small("mlp", mlp, [128, NL, 24])
    mlgb_t = small("mlgb", mlgb, [4, NL, 4])
    gqp_t = small("gqp", gqp, [128, NL, 2])
    hyp_t = small("hyp", hyp, [128, NL, 28])
    hyfb_t = small("hyfb", hyfb, [64, NL, 2])
    dfp_t = small("dfp", dfp, [128, NL, 3])
    dlam_t = small("dlam", dlam, [128, NL, 128])
    cc_t = small("cc", cc, [128, KC, 2])
    modT = A.alloc([128, NL, 72, 2], F32)
    modA = A.alloc([128, NL, 3, KC, 2], F32)
    modG = A.alloc([128, NL, 3, KC, 2], F32)
    lamT = A.alloc([128, NL, 4], F32)
    dsub = A.alloc([128, NL], F32)
    hT = A.alloc([128, KC, T], F32)
    base_mark = A.mark()

    def dump_fm(t, nchunks, bufs, ncols=T, col0=0, row0=0):
        evs = []
        for kc in range(nchunks):
            evs.append(S.dma(dbgT[row0 + kc * 128:row0 + (kc + 1) * 128, col0:col0 + ncols], t[:, kc, 0:ncols], reads=bufs))
        S.barrier()
        return evs

    out_evs = []

    for kc in range(KC):
        S.dma(hT[:, kc, 0:CL], ctxT[kc * 128:(kc + 1) * 128, :], writes=[B[("h", kc, 0)]])
        S.dma(hT[:, kc, CL:T], xT[kc * 128:(kc + 1) * 128, :], writes=[B[("h", kc, i)] for i in range(1, 5)])

    m0 = A.mark()
    sT = A.alloc([128, KC, 2], F32)
    S.op("act", lambda: ACT.activation(out=sT[:], in_=cc_t[:], func=AF.Silu), reads=[B[("c", "cc")]], writes=[B["sT"]])
    NST = 1152
    stg = [A.alloc([128, KC, NST], F32) for _ in range(2)]
    stgR = Ring([(stg[i], B[("adastg", i)]) for i in range(2)])
    for l in range(n_layers):
        for si in range(9 * D // NST):
            st, sb = stgR.next()
            S.dma(st[:], ada_w[l].rearrange("(k p) n -> p k n", p=128)[:, :, si * NST:(si + 1) * NST], writes=[sb])
            ps, pb = psR.next()
            nm = NST // 128
            for m in range(nm):
                for kc in range(KC):
                    mm(ps[:, 2 * m:2 * m + 2], st[:, kc, m * 128:(m + 1) * 128], sT[:, kc, :], kc == 0, kc == KC - 1,
                       [sb, B["sT"]], [pb])
            for j in range(2):
                S.op("dve", lambda: V.tensor_tensor(
                    out=modT[:, l, si * nm:(si + 1) * nm, j], in0=ps[:, 0:2 * nm].rearrange("p (m j) -> p m j", j=2)[:, :, j],
                    in1=adab[:, l * 72 + si * nm: l * 72 + (si + 1) * nm], op=ALU.add),
                    reads=[pb, B[("c", "adab")]], writes=[B["mod"]])
    for l in range(n_layers):
        for i in range(3):
            for j in range(2):
                S.op("dve", lambda: V.scalar_tensor_tensor(
                    out=modA[:, l, i, :, j], in0=modT[:, l, (3 * i + 1) * 8:(3 * i + 2) * 8, j], scalar=1.0,
                    in1=normg[:, (l * 3 + i) * KC:(l * 3 + i + 1) * KC], op0=ALU.add, op1=ALU.mult),
                    reads=[B["mod"], B[("c", "normg")]], writes=[B["modA"]])
                S.op("dve", lambda: V.tensor_scalar(
                    out=modG[:, l, i, :, j], in0=modT[:, l, (3 * i + 2) * 8:(3 * i + 3) * 8, j],
                    scalar1=(1.0 if i == 1 else 0.5), scalar2=None, op0=ALU.mult),
                    reads=[B["mod"]], writes=[B["modG"]])
        lam_init = 0.8 - 0.6 * math.exp(-0.3 * l)
        S.op("dve", lambda: V.tensor_tensor(out=dlam_t[:, l, 0:32], in0=dlam_t[:, l, 0:32], in1=dlam_t[:, l, 32:64], op=ALU.mult),
             reads=[B[("c", "dlam")]], writes=[B[("c", "dlam")]])
        S.op("dve", lambda: V.tensor_tensor(out=dlam_t[:, l, 64:96], in0=dlam_t[:, l, 64:96], in1=dlam_t[:, l, 96:128], op=ALU.mult),
             reads=[B[("c", "dlam")]], writes=[B[("c", "dlam")]])
        S.op("dve", lambda: V.reduce_sum(out=lamT[:, l, 0:1], in_=dlam_t[:, l, 0:32], axis=mybir.AxisListType.X),
             reads=[B[("c", "dlam")]], writes=[B["lam"]])
        S.op("dve", lambda: V.reduce_sum(out=lamT[:, l, 1:2], in_=dlam_t[:, l, 64:96], axis=mybir.AxisListType.X),
             reads=[B[("c", "dlam")]], writes=[B["lam"]])
        S.op("act", lambda: ACT.activation(out=lamT[:, l, 0:2], in_=lamT[:, l, 0:2], func=AF.Exp), reads=[B["lam"]], writes=[B["lam"]])
        S.op("dve", lambda: V.tensor_tensor(out=lamT[:, l, 2:3], in0=lamT[:, l, 1:2], in1=lamT[:, l, 0:1], op=ALU.subtract),
             reads=[B["lam"]], writes=[B["lam"]])
        S.op("dve", lambda: V.tensor_scalar(out=lamT[:, l, 2:3], in0=lamT[:, l, 2:3], scalar1=-lam_init, scalar2=None, op0=ALU.add),
             reads=[B["lam"]], writes=[B["lam"]])
        S.op("dve", lambda: V.tensor_scalar(out=dsub[:, l:l + 1], in0=dfp_t[:, l, 2:3], scalar1=(1.0 - lam_init), scalar2=None, op0=ALU.mult),
             reads=[B[("c", "dfp")]], writes=[B["lam"]])
    S.barrier()
    A.release(m0)

    def hyena_filters(l, nm):
        Ln = L if nm == "l" else CL
        ntc = Ln // 128
        m1 = A.mark()
        feats = A.alloc([33, Ln], F32)
        winst = [A.alloc([128, 512], F32) for _ in range(2)]
        winR = Ring([(winst[k], B[("hf_win", k)]) for k in range(2)])
        S.dma(feats[:], cd["feats_" + nm], writes=[B["hf_feats"]])
        w1 = A.alloc([33, 64], F32)
        w2_ = A.alloc([64, 64], F32)
        w3 = A.alloc([64, 512], F32)
        b3 = A.alloc([1, 512], F32)
        S.dma(w1[:], hyf1[l], writes=[B["hf_w"]])
        S.dma(w2_[:], hyf2[l], writes=[B["hf_w"]])
        S.dma(w3[:], hyf3[l], writes=[B["hf_w"]])
        S.dma(b3[:], hyb3[l], writes=[B["hf_w"]])
        bb = A.alloc([64, 2], F32)
        S.op("dve", lambda: V.tensor_scalar(out=bb[:], in0=hyfb_t[:, l, :], scalar1=1.0 / (2 * math.pi), scalar2=0.0,
                                            op0=ALU.mult, op1=ALU.add), reads=[B[("c", "hyfb")]], writes=[B["hf_bb"]])
        h1 = A.alloc([64, Ln], F32)
        h2 = A.alloc([64, Ln], F32)
        tmp = A.alloc([64, 512], F32)
        tmpi = A.alloc([64, 512], mybir.dt.int32)
        tmpk = A.alloc([64, 512], F32)
        nb = max(1, Ln // 512)
        bw = Ln // nb
        for (src, wt, kk, dst, bi, dname, sname) in ((feats, w1, 33, h1, 0, "hf_h1", "hf_feats"), (h1, w2_, 64, h2, 1, "hf_h2", "hf_h1")):
            for b_ in range(nb):
                ps, pb = psR.next()
                mm(ps[0:64, 0:bw], wt[0:kk, :], src[0:kk, b_ * bw:(b_ + 1) * bw], True, True, [B["hf_w"], B[sname]], [pb])
                S.op("dve", lambda: V.tensor_scalar(out=tmp[:, 0:bw], in0=ps[0:64, 0:bw], scalar1=1.0 / (2 * math.pi),
                                                    scalar2=bb[:, bi:bi + 1], op0=ALU.mult, op1=ALU.add),
                     reads=[pb, B["hf_bb"]], writes=[B["hf_tmp"]])
                S.op("dve", lambda: V.tensor_copy(out=tmpi[:, 0:bw], in_=tmp[:, 0:bw]), reads=[B["hf_tmp"]], writes=[B["hf_tmpi"]])
                S.op("dve", lambda: V.tensor_copy(out=tmpk[:, 0:bw], in_=tmpi[:, 0:bw]), reads=[B["hf_tmpi"]], writes=[B["hf_tmpk"]])
                S.op("dve", lambda: V.tensor_tensor(out=tmp[:, 0:bw], in0=tmp[:, 0:bw], in1=tmpk[:, 0:bw], op=ALU.subtract),
                     reads=[B["hf_tmp"], B["hf_tmpk"]], writes=[B["hf_tmp"]])
                S.op("act", lambda: ACT.activation(out=dst[:, b_ * bw:(b_ + 1) * bw], in_=tmp[:, 0:bw], func=AF.Sin,
                                                   bias=epsT[0:64, 3:4], scale=2 * math.pi * (1.0 - 1e-6)),
                     reads=[B["hf_tmp"], B["eps"]], writes=[B[dname]])
        hw = A.alloc([128, ntc, 512], F32)
        ab = A.alloc([128, 512], F32)
        psum_s, pbs = psA.next()
        for tc in range(ntc):
            ps, pb = psR.next()
            wn_, wnb = winR.next()
            S.dma(wn_[:], cd["win_" + nm][:, tc, :], writes=[wnb])
            mm(ps[:, :], h2[:, tc * 128:(tc + 1) * 128], w3[:, :], True, False, [B["hf_h2"], B["hf_w"]], [pb])
            mm(ps[:, :], ones_f[0:1, :], b3[0:1, :], False, True, [B[("c", "ones_f")], B["hf_w"]], [pb])
            S.op("dve", lambda: V.tensor_tensor(out=hw[:, tc, :], in0=ps[:, :], in1=wn_[:], op=ALU.mult),
                 reads=[pb, wnb], writes=[B[("hf_hw", tc)]])
            S.op("act", lambda: ACT.activation(out=ab[:], in_=hw[:, tc, :], func=AF.Abs),
                 reads=[B[("hf_hw", tc)]], writes=[B["hf_ab"]])
            mm(psum_s[:, :], ones_f[:, :], ab[:, :], tc == 0, tc == ntc - 1, [B[("c", "ones_f")], B["hf_ab"]], [pbs])
        rs = A.alloc([128, 512], F32)
        S.op("dve", lambda: V.reciprocal(out=rs[:], in_=psum_s[:, :]), reads=[pbs], writes=[B["hf_rs"]])
        hb = A.alloc([128, ntc, 512], BF16)
        for tc in range(ntc):
            S.op("dve", lambda: V.tensor_tensor(out=hb[:, tc, :], in0=hw[:, tc, :], in1=rs[:], op=ALU.mult),
                 reads=[B[("hf_hw", tc)], B["hf_rs"]], writes=[B["hf_hb"]])
        Fst = [A.alloc([128, ntc, 2, 128], BF16) for _ in range(2)]
        FR = Ring([(Fst[i], B[("hf_F", i)]) for i in range(2)])
        Hst = [A.alloc([128, 2, 2, 256], BF16) for _ in range(2)]
        HR = Ring([(Hst[i], B[("hf_H", i)]) for i in range(2)])
        Hd = Hl if nm == "l" else Hc
        for fc in range(ntc):
            ft, fb = FR.next()
            S.dma(ft[:], cd["F_" + nm][fc], writes=[fb])
            ht, hbuf = HR.next()
            for ri in range(2):
                ps, pb = psR.next()
                for tc in range(ntc):
                    mm(ps[:, :], ft[:, tc, ri, :], hb[:, tc, :], tc == 0, tc == ntc - 1, [fb, B["hf_hb"]], [pb])
                S.op("act", lambda: ACT.copy(out=ht[:, :, ri, :], in_=ps[:, :].rearrange("p (o c) -> p o c", o=2)),
                     reads=[pb], writes=[hbuf])
            S.dma(Hd[l, fc], ht[:], reads=[hbuf], writes=[B[("Hd", nm, l)]], q="pool")
        alt = A.alloc([128, ntc], BF16)
        S.dma(alt[:], cd["alt_" + nm], writes=[B["hf_alt"]])
        ps, pb = psR.next()
        for tc in range(ntc):
            mm(ps[0:1, :], alt[:, tc:tc + 1], hb[:, tc, :], tc == 0, tc == ntc - 1, [B["hf_alt"], B["hf_hb"]], [pb])
        ny = A.alloc([1, 512], BF16)
        S.op("act", lambda: ACT.copy(out=ny[:], in_=ps[0:1, :]), reads=[pb], writes=[B["hf_ny"]])
        S.dma((Hlny if nm == "l" else Hcny)[l], ny[:], reads=[B["hf_ny"]], writes=[B[("Hd", nm, l)]], q="pool")
        if dbg == "hfilt" and l == 0 and nm == "l":
            tmpd = A.alloc([128, ntc, 512], F32)
            S.op("dve", lambda: V.tensor_copy(out=tmpd[:], in_=hb[:]), reads=[B["hf_hb"]], writes=[B["dbgtmp"]])
            for tc in range(4):
                out_evs.append(S.dma(dbgT[tc * 128:(tc + 1) * 128, 0:512], tmpd[:, tc, :], reads=[B["dbgtmp"]]))
        S.barrier()
        A.release(m1)

    for l in range(n_layers):
        hyena_filters(l, "l")
        if l < NL - 1:
            hyena_filters(l, "c")

    xn = A.alloc([128, KC, T], BF16)
    layer_mark = A.mark()

    def modnorm(l, i):
        m1 = A.mark()
        sq = [A.alloc([128, 512], BF16) for _ in range(3)]
        sqR = Ring([(sq[k], B[("mn_sq", k)]) for k in range(3)])
        rstd = [A.alloc([128, 512], F32) for _ in range(2)]
        rsR = Ring([(rstd[k], B[("mn_rs", k)]) for k in range(2)])
        tm = [A.alloc([128, 512], F32) for _ in range(3)]
        tmR = Ring([(tm[k], B[("mn_tm", k)]) for k in range(3)])
        for tbi, (t0, n, j) in enumerate(TBS):
            ps, pb = psR.next()
            for kc in range(KC):
                q, qb = sqR.next()
                S.op("act", lambda: ACT.activation(out=q[:, 0:n], in_=hT[:, kc, t0:t0 + n], func=AF.Square),
                     reads=[B[("h", kc, tbi)]], writes=[qb])
                mm(ps[:, 0:n], ones_bf[:, :], q[:, 0:n], kc == 0, kc == KC - 1, [qb, B[("c", "ones_bf")]], [pb])
            r, rb = rsR.next()
            S.op("act", lambda: ACT.activation(out=r[:, 0:n], in_=ps[:, 0:n], func=AF.Sqrt, bias=epsT[:, 0:1], scale=1.0 / D),
                 reads=[pb, B["eps"]], writes=[rb])
            S.op("dve", lambda: V.reciprocal(out=r[:, 0:n], in_=r[:, 0:n]), reads=[rb], writes=[rb])
            for kc in range(KC):
                t_, tb_ = tmR.next()
                S.op("dve", lambda: V.tensor_tensor(out=t_[:, 0:n], in0=hT[:, kc, t0:t0 + n], in1=r[:, 0:n], op=ALU.mult),
                     reads=[B[("h", kc, tbi)], rb], writes=[tb_])
                S.op("act", lambda: ACT.activation(out=xn[:, kc, t0:t0 + n], in_=t_[:, 0:n], func=AF.Identity,
                                                   bias=modT[:, l, 3 * i * 8 + kc, j:j + 1], scale=modA[:, l, i, kc, j:j + 1]),
                     reads=[tb_, B["mod"], B["modA"]], writes=[B[("xn", kc, tbi)]])
        S.barrier()
        A.release(m1)

    GROUPS = [(0, 6), (6, 12), (12, 17), (17, 22)]

    def ffn(l, f, i, skip_ctx=False):
        m1 = A.mark()
        tbs = [(tbi, tb) for tbi, tb in enumerate(TBS) if not (skip_ctx and tbi == 0)]
        g = A.alloc([128, 6, T], BF16)
        wab = [A.alloc([128, 2, KC, 256], BF16) for _ in range(2)]
        wR = Ring([(wab[k], B[("ffn_wab", k)]) for k in range(2)])
        w2t = [A.alloc([128, 6, D], BF16) for _ in range(2)]
        w2R = Ring([(w2t[k], B[("ffn_w2", k)]) for k in range(2)])
        sa = [A.alloc([128, 512], F32) for _ in range(2)]
        saR = Ring([(sa[k], B[("ffn_sa", k)]) for k in range(2)])
        w13v = w13[l, f].rearrange("(k p) n -> p k n", p=128)
        for (j0, j1) in GROUPS:
            nj = j1 - j0
            wt2, wb2 = w2R.next()
            S.dma(wt2[:, 0:nj, :], w2[l, f, j0 * 128:j1 * 128, :].rearrange("(j p) n -> p j n", p=128), writes=[wb2], q="pool")
            jj = j0
            while jj < j1:
                npair = min(2, j1 - jj)
                wt, wb = wR.next()
                S.dma(wt[:, 0, :, 0:npair * 128], w13v[:, :, jj * 128:(jj + npair) * 128], writes=[wb], q="pool")
                S.dma(wt[:, 1, :, 0:npair * 128], w13v[:, :, HID + jj * 128:HID + (jj + npair) * 128], writes=[wb], q="pool")
                for u in range(npair):
                    for tbi, (t0, n, j) in tbs:
                        pa, pab = psR.next()
                        pbb, pbbb = psR.next()
                        for kc in range(KC):
                            mm(pa[:, 0:n], wt[:, 0, kc, u * 128:(u + 1) * 128], xn[:, kc, t0:t0 + n], kc == 0, kc == KC - 1,
                               [wb, B[("xn", kc, tbi)]], [pab])
                        for kc in range(KC):
                            mm(pbb[:, 0:n], wt[:, 1, kc, u * 128:(u + 1) * 128], xn[:, kc, t0:t0 + n], kc == 0, kc == KC - 1,
                               [wb, B[("xn", kc, tbi)]], [pbbb])
                        s_, sb_ = saR.next()
                        S.op("act", lambda: ACT.activation(out=s_[:, 0:n], in_=pa[:, 0:n], func=AF.Silu), reads=[pab], writes=[sb_])
                        S.op("dve", lambda: V.tensor_tensor(out=g[:, jj - j0 + u, t0:t0 + n], in0=pbb[:, 0:n], in1=s_[:, 0:n], op=ALU.mult),
                             reads=[pbbb, sb_], writes=[B[("ffn_g", jj - j0 + u, tbi)]])
                jj += npair
            for tbi, (t0, n, j) in tbs:
                for oc in range(KC):
                    ps, pb = psR.next()
                    for q in range(nj):
                        mm(ps[:, 0:n], wt2[:, q, oc * 128:(oc + 1) * 128], g[:, q, t0:t0 + n], q == 0, q == nj - 1,
                           [wb2, B[("ffn_g", q, tbi)]], [pb])
                    S.op("dve", lambda: V.scalar_tensor_tensor(out=hT[:, oc, t0:t0 + n], in0=ps[:, 0:n], scalar=modG[:, l, i, oc, j:j + 1],
                                                               in1=hT[:, oc, t0:t0 + n], op0=ALU.mult, op1=ALU.add),
                         reads=[pb, B["modG"], B[("h", oc, tbi)]], writes=[B[("h", oc, tbi)]])
        S.barrier()
        A.release(m1)

    def load_w_cols(l, cols_list, name):
        ncols = sum(n for _, n in cols_list)
        t = A.alloc([128, KC, ncols], BF16)
        wv = w_in[l].rearrange("(k p) n -> p k n", p=128)
        o = 0
        for (c0, n) in cols_list:
            S.dma(t[:, :, o:o + n], wv[:, :, c0:c0 + n], writes=[B[("win", name)]], q="pool")
            o += n
        return t, B[("win", name)]

    def proj_fm(wt, wb, c0, m, tbi, skip=None):
        t0, n, j = TBS[tbi]
        ps, pb = psR.next()
        for kc in range(KC):
            mm(ps[0:m, 0:n], wt[:, kc, c0:c0 + m], xn[:, kc, t0:t0 + n], kc == 0, kc == KC - 1, [wb, B[("xn", kc, tbi)]], [pb])
        return ps, pb

    def proj_tm(wt, wb, c0, ncols, sc):
        ps, pb = psR.next()
        tbi = tbi_of_sc(sc)
        for kc in range(KC):
            mm(ps[:, 0:ncols], xn[:, kc, sc * 128:(sc + 1) * 128], wt[:, kc, c0:c0 + ncols], kc == 0, kc == KC - 1,
               [wb, B[("xn", kc, tbi)]], [pb])
        return ps, pb

    def out_proj(l, y, ybufs, g, skip_ctx=False):
        m1 = A.mark()
        wt = A.alloc([128, 2, D], BF16)
        S.dma(wt[:], w_out[l, g * 256:(g + 1) * 256, :].rearrange("(k p) n -> p k n", p=128), writes=[B["wout"]], q="pool")
        for tbi, (t0, n, j) in enumerate(TBS):
            if skip_ctx and tbi == 0:
                continue
            for oc in range(KC):
                ps, pb = psR.next()
                for k in range(2):
                    mm(ps[:, 0:n], wt[:, k, oc * 128:(oc + 1) * 128], y[:, k, t0:t0 + n], k == 0, k == 1, [B["wout"]] + ybufs(k, tbi), [pb])
                S.op("dve", lambda: V.scalar_tensor_tensor(out=hT[:, oc, t0:t0 + n], in0=ps[:, 0:n], scalar=modG[:, l, 1, oc, j:j + 1],
                                                           in1=hT[:, oc, t0:t0 + n], op0=ALU.mult, op1=ALU.add),
                     reads=[pb, B["modG"], B[("h", oc, tbi)]], writes=[B[("h", oc, tbi)]])
        S.barrier()
        A.release(m1)

    def head_rms(src, srcb, n, blk, blkname, dim, dst_rstd, dstb):
        sqt = A.alloc([128, 512], BF16)
        S.op("act", lambda: ACT.activation(out=sqt[:, 0:n], in_=src, func=AF.Square), reads=srcb, writes=[B["hr_sq"]])
        ps, pb = psR.next()
        mm(ps[:, 0:n], blk[:, :], sqt[:, 0:n], True, True, [B["hr_sq"], B[("c", blkname)]], [pb])
        S.op("act", lambda: ACT.activation(out=dst_rstd, in_=ps[:, 0:n], func=AF.Sqrt, bias=epsT[:, 0:1], scale=1.0 / dim),
             reads=[pb, B["eps"]], writes=dstb)
        S.op("dve", lambda: V.reciprocal(out=dst_rstd, in_=dst_rstd), reads=dstb, writes=dstb)

    def attn_core(pairs, kT_of, kbufs, q_of, qbufs, vx_of, vbufs, scale, epilogue, nq=1, depth=3):
        E = [A.alloc([128, 512], BF16) for _ in range(depth + 2)]
        ER = Ring([(E[k], B[("at_E", k)]) for k in range(depth + 2)])
        items = []
        for tbi, scs in pairs:
            accs = None
            for si, sc in enumerate(scs):
                for v in range(nq):
                    items.append((tbi, sc, v, si == 0, si == len(scs) - 1, si == len(scs) - 1 and v == nq - 1))
        state = {}
        pend = []

        def stage_a(it):
            tbi, sc, v, first, last, fin = it
            t0, n, j = TBS[tbi]
            if first and v == 0:
                state[tbi] = [psA.next() for _ in range(nq)]
            ps, pb = psR.next()
            mm(ps[:, 0:n], kT_of(sc), q_of(v, tbi), True, True, kbufs(sc) + qbufs(v, tbi), [pb])
            e, eb = ER.next()
            S.op("act", lambda: ACT.activation(out=e[:, 0:n], in_=ps[:, 0:n], func=AF.Exp, scale=scale), reads=[pb], writes=[eb])
            return (it, e, eb)

        def stage_b(c):
            (tbi, sc, v, first, last, fin), e, eb = c
            t0, n, j = TBS[tbi]
            acc, ab = state[tbi][v]
            mm(acc[:, 0:n], vx_of(sc), e[:, 0:n], first, last, vbufs(sc) + [eb], [ab])
            if fin:
                epilogue(tbi, state[tbi])

        for idx in range(len(items) + depth):
            if idx < len(items):
                pend.append(stage_a(items[idx]))
            if idx >= depth:
                stage_b(pend.pop(0))

    def qk_tmp_ring(nsets=2):
        sets = []
        for k in range(nsets):
            sets.append(dict(raw=A.alloc([128, 512], F32), rs=A.alloc([128, 512], F32), qn=A.alloc([128, 512], F32),
                             cs=A.alloc([128, 2, 512], F32), t1=A.alloc([128, 512], F32), sq=A.alloc([128, 512], BF16), k=k))
        return Ring(sets)

    def qk_norm_rope(l, ps, pb, tbi, gain_ap, blk, blkname, dim, rot, csname, dst, dstb, extra=None, tmpR=None):
        t0, n, j = TBS[tbi]
        ts = tmpR.next()
        k_ = ts["k"]
        raw, rs, qn, cs, t1, sq = ts["raw"], ts["rs"], ts["qn"], ts["cs"], ts["t1"], ts["sq"]
        braw, brs, bqn, bcs, bt1, bsq = (B[("qn_" + nm, k_)] for nm in ("raw", "rs", "qn", "cs", "t1", "sq"))
        S.op("act", lambda: ACT.copy(out=raw[:, 0:n], in_=ps[:, 0:n]), reads=[pb], writes=[braw])
        S.op("act", lambda: ACT.activation(out=sq[:, 0:n], in_=raw[:, 0:n], func=AF.Square), reads=[braw], writes=[bsq])
        pq, pqb = psR.next()
        mm(pq[:, 0:n], blk[:, :], sq[:, 0:n], True, True, [bsq, B[("c", blkname)]], [pqb])
        S.op("act", lambda: ACT.activation(out=rs[:, 0:n], in_=pq[:, 0:n], func=AF.Sqrt, bias=epsT[:, 0:1], scale=1.0 / dim),
             reads=[pqb, B["eps"]], writes=[brs])
        S.op("dve", lambda: V.reciprocal(out=rs[:, 0:n], in_=rs[:, 0:n]), reads=[brs], writes=[brs])
        S.op("dve", lambda: V.scalar_tensor_tensor(out=qn[:, 0:n], in0=raw[:, 0:n], scalar=gain_ap, in1=rs[:, 0:n], op0=ALU.mult, op1=ALU.mult),
             reads=[braw, brs, B[("c", "gqp")], B[("c", "dfp")]], writes=[bqn])
        if tbi != 0:
            pr, prb = psR.next()
            mm(pr[:, 0:n], rot[:, :], qn[:, 0:n], True, True, [bqn, B["rope_c"]], [prb])
            l0 = t0 - CL
            S.dma(cs[:, 0, 0:n], cd["cos" + csname][:, l0:l0 + n], writes=[bcs])
            S.dma(cs[:, 1, 0:n], cd["sin" + csname][:, l0:l0 + n], writes=[bcs])
            S.op("dve", lambda: V.tensor_tensor(out=t1[:, 0:n], in0=pr[:, 0:n], in1=cs[:, 1, 0:n], op=ALU.mult),
                 reads=[prb, bcs], writes=[bt1])
            S.op("dve", lambda: V.tensor_tensor(out=qn[:, 0:n], in0=qn[:, 0:n], in1=cs[:, 0, 0:n], op=ALU.mult),
                 reads=[bqn, bcs], writes=[bqn])
            S.op("dve", lambda: V.tensor_tensor(out=qn[:, 0:n], in0=qn[:, 0:n], in1=t1[:, 0:n], op=ALU.add),
                 reads=[bqn, bt1], writes=[bqn])
        if extra is None:
            S.op("act", lambda: ACT.copy(out=dst, in_=qn[:, 0:n]), reads=[bqn], writes=dstb)
        else:
            extra(qn, bqn, n)

    def vx_build(l, wv, wvb, c0, nh, vx, name):
        S.op("dve", lambda: V.memset(vx[:], 1.0), writes=[B[(name, t)] for t in range(5)])
        for sc in range(NSC):
            ps, pb = proj_tm(wv, wvb, c0, nh * 64, sc)
            S.op("act", lambda: ACT.copy(out=vx[:, sc, :, 0:64], in_=ps[:, 0:nh * 64].rearrange("p (h d) -> p h d", d=64)),
                 reads=[pb], writes=[B[(name, tbi_of_sc(sc))]])

    def lat_pairs(need_ctx):
        pairs = []
        if need_ctx:
            pairs.append((0, [0, 1]))
        for tbi in range(1, 5):
            pairs.append((tbi, list(range(NSC))))
        return pairs

    def gqa_mixer(l, need_ctx):
        ROW0 = 256
        m1 = A.mark()
        y = A.alloc([128, 2, T], BF16)
        rot = A.alloc([128, 128], F32)
        S.dma(rot[:], cd["rot64"], writes=[B["rope_c"]])
        b0 = 1040
        wq, wqb = load_w_cols(l, [(b0, 64), (b0 + 128, 64), (b0 + 64, 64), (b0 + 192, 64)], "gq_q")
        wk, wkb = load_w_cols(l, [(b0 + 256, 128)], "gq_k")
        wv, wvb = load_w_cols(l, [(b0 + 384, 128)], "gq_v")
        qT = A.alloc([128, 2, 2, T], BF16)
        kT = A.alloc([128, T], BF16)
        vx = A.alloc([128, NSC, 2, 128], BF16)
        vx_build(l, wv, wvb, 0, 2, vx, "gq_vx")
        mtmp = A.mark()
        tmpR = qk_tmp_ring()
        for tbi, (t0, n, j) in enumerate(TBS):
            for ch in range(2):
                ps, pb = proj_fm(wq, wqb, ch * 128, 128, tbi)

                def extra_q(fin, finb, n, ch=ch, t0=t0, tbi=tbi):
                    for hf in range(2):
                        S.op("dve", lambda: V.tensor_scalar(out=qT[:, ch, hf, t0:t0 + n], in0=fin[:, 0:n], scalar1=hm64[:, hf:hf + 1],
                                                            scalar2=None, op0=ALU.mult),
                             reads=[finb, B[("c", "hm64")]], writes=[B[("gq_q", ch, tbi)]])

                qk_norm_rope(l, ps, pb, tbi, gqp_t[:, l, 0:1], blk64, "blk64", 64, rot, "64",
                             None, None, extra=extra_q, tmpR=tmpR)
            ps, pb = proj_fm(wk, wkb, 0, 128, tbi)
            qk_norm_rope(l, ps, pb, tbi, gqp_t[:, l, 1:2], blk64, "blk64", 64, rot, "64",
                         kT[:, t0:t0 + n], [B[("gq_k", tbi)]], tmpR=tmpR)
        S.barrier()
        A.release(mtmp)
        rc = A.alloc([128, 512], F32)
        for h in range(4):
            ch, r0 = h % 2, (h // 2) * 64
            kvh = h // 2

            def epi(tbi, accs, h=h):
                t0, n, j = TBS[tbi]
                acc, ab = accs[0]
                S.op("dve", lambda: V.reciprocal(out=rc[64:128, 0:n], in_=acc[64:128, 0:n]), reads=[ab], writes=[B["gq_rc"]])
                o0 = (h % 2) * 64
                S.op("dve", lambda: V.tensor_tensor(out=y[o0:o0 + 64, h // 2, t0:t0 + n], in0=acc[0:64, 0:n], in1=rc[64:128, 0:n], op=ALU.mult),
                     reads=[ab, B["gq_rc"]], writes=[B[("gq_y", h // 2, tbi)]])

            attn_core(lat_pairs(need_ctx),
                      lambda sc: kT[:, sc * 128:(sc + 1) * 128], lambda sc: [B[("gq_k", tbi_of_sc(sc))]],
                      lambda v, tbi: qT[:, ch, kvh, TBS[tbi][0]:TBS[tbi][0] + TBS[tbi][1]], lambda v, tbi: [B[("gq_q", ch, tbi)]],
                      lambda sc: vx[:, sc, kvh, :], lambda sc: [B[("gq_vx", tbi_of_sc(sc))]],
                      0.125, epi)
        S.barrier()
        A.release(m1 + 2 * T * 2)
        if dbg in ("gqa", "allmix") and l == 0:
            yf = A.alloc([128, 2, T], F32)
            S.op("dve", lambda: V.tensor_copy(out=yf[:], in_=y[:]), reads=[B[("gq_y", k, t)] for k in range(2) for t in range(5)], writes=[B["dbgtmp"]])
            out_evs.extend(dump_fm(yf, 2, [B["dbgtmp"]], row0=(ROW0 if dbg == "allmix" else 0)))
        out_proj(l, y, lambda k, tbi: [B[("gq_y", k, tbi)]], 1, skip_ctx=not need_ctx)
        A.release(m1)

    def diff_mixer(l, need_ctx):
        ROW0 = 768
        m1 = A.mark()
        y = A.alloc([128, 2, T], BF16)
        rot = A.alloc([128, 128], F32)
        S.dma(rot[:], cd["rot32"], writes=[B["rope_c"]])
        b0 = 2320
        for ch in range(2):
            mch = A.mark()
            wq, wqb = load_w_cols(l, [(b0 + ch * 128, 128)], "df_q")
            wk, wkb = load_w_cols(l, [(b0 + 256 + ch * 128, 128)], "df_k")
            wv, wvb = load_w_cols(l, [(b0 + 512 + ch * 128, 128)], "df_v")
            qz = A.alloc([128, 4, T], BF16)
            kT = A.alloc([128, T], BF16)
            vx = A.alloc([128, NSC, 2, 128], BF16)
            vx_build(l, wv, wvb, 0, 2, vx, "df_vx")
            mtmp = A.mark()
            tmpR = qk_tmp_ring()
            for tbi, (t0, n, j) in enumerate(TBS):
                ps, pb = proj_fm(wq, wqb, 0, 128, tbi)

                def extra(fin, finb, n, t0=t0, tbi=tbi):
                    for v in range(4):
                        S.op("dve", lambda: V.tensor_scalar(out=qz[:, v, t0:t0 + n], in0=fin[:, 0:n], scalar1=qm32[:, v:v + 1],
                                                            scalar2=None, op0=ALU.mult),
                             reads=[finb, B[("c", "qm32")]], writes=[B[("df_q", v, tbi)]])

                qk_norm_rope(l, ps, pb, tbi, dfp_t[:, l, 0:1], blk32, "blk32", 32, rot, "32", None, None, extra=extra, tmpR=tmpR)
                ps, pb = proj_fm(wk, wkb, 0, 128, tbi)
                qk_norm_rope(l, ps, pb, tbi, dfp_t[:, l, 1:2], blk32, "blk32", 32, rot, "32",
                             kT[:, t0:t0 + n], [B[("df_k", tbi)]], tmpR=tmpR)
            S.barrier()
            A.release(mtmp)
            yraw = A.alloc([128, T], F32)
            r1 = A.alloc([128, 512], F32)
            r2 = A.alloc([128, 512], F32)
            t1 = A.alloc([128, 512], F32)
            t2 = A.alloc([128, 512], F32)
            for hh in range(2):
                r0 = hh * 64

                def epi(tbi, accs, r0=r0):
                    t0, n, j = TBS[tbi]
                    (a1, ab1), (a2, ab2) = accs
                    S.op("dve", lambda: V.reciprocal(out=r1[64:128, 0:n], in_=a1[64:128, 0:n]), reads=[ab1], writes=[B["df_r1"]])
                    S.op("dve", lambda: V.reciprocal(out=r2[64:128, 0:n], in_=a2[64:128, 0:n]), reads=[ab2], writes=[B["df_r2"]])
                    S.op("dve", lambda: V.tensor_tensor(out=t1[0:64, 0:n], in0=a1[0:64, 0:n], in1=r1[64:128, 0:n], op=ALU.mult),
                         reads=[ab1, B["df_r1"]], writes=[B["df_t1"]])
                    S.op("dve", lambda: V.tensor_tensor(out=t2[0:64, 0:n], in0=a2[0:64, 0:n], in1=r2[64:128, 0:n], op=ALU.mult),
                         reads=[ab2, B["df_r2"]], writes=[B["df_t2"]])
                    if r0 == 0:
                        S.op("dve", lambda: V.scalar_tensor_tensor(out=yraw[0:64, t0:t0 + n], in0=t2[0:64, 0:n], scalar=lamT[0:64, l, 2:3],
                                                                   in1=t1[0:64, 0:n], op0=ALU.mult, op1=ALU.add),
                             reads=[B["df_t1"], B["df_t2"], B["lam"]], writes=[B[("df_yraw", tbi)]])
                    else:
                        S.op("dve", lambda: V.scalar_tensor_tensor(out=t1[0:64, 0:n], in0=t2[0:64, 0:n], scalar=lamT[0:64, l, 2:3],
                                                                   in1=t1[0:64, 0:n], op0=ALU.mult, op1=ALU.add),
                             reads=[B["df_t1"], B["df_t2"], B["lam"]], writes=[B["df_t1"]])
                        S.op("act", lambda: ACT.copy(out=yraw[64:128, t0:t0 + n], in_=t1[0:64, 0:n]), reads=[B["df_t1"]], writes=[B[("df_yraw", tbi)]])

                attn_core(lat_pairs(need_ctx),
                          lambda sc: kT[:, sc * 128:(sc + 1) * 128], lambda sc: [B[("df_k", tbi_of_sc(sc))]],
                          lambda v, tbi: qz[:, hh * 2 + v, TBS[tbi][0]:TBS[tbi][0] + TBS[tbi][1]], lambda v, tbi: [B[("df_q", hh * 2 + v, tbi)]],
                          lambda sc: vx[:, sc, hh, :], lambda sc: [B[("df_vx", tbi_of_sc(sc))]],
                          32 ** -0.5, epi, nq=2)
            rs = A.alloc([128, 512], F32)
            for tbi, (t0, n, j) in enumerate(TBS):
                if tbi == 0 and not need_ctx:
                    continue
                m2 = A.mark()
                head_rms(yraw[:, t0:t0 + n], [B[("df_yraw", tbi)]], n, blk64, "blk64", 64, rs[:, 0:n], [B["df_rs"]])
                S.op("dve", lambda: V.scalar_tensor_tensor(out=y[:, ch, t0:t0 + n], in0=yraw[:, t0:t0 + n], scalar=dsub[:, l:l + 1],
                                                           in1=rs[:, 0:n], op0=ALU.mult, op1=ALU.mult),
                     reads=[B[("df_yraw", tbi)], B["df_rs"], B["lam"]], writes=[B[("df_y", ch, tbi)]])
                A.release(m2)
            S.barrier()
            A.release(mch)
        A.release(m1 + 2 * T * 2)
        if dbg in ("diff", "allmix") and l == 0:
            yf = A.alloc([128, 2, T], F32)
            S.op("dve", lambda: V.tensor_copy(out=yf[:], in_=y[:]), reads=[B[("df_y", k, t)] for k in range(2) for t in range(5)], writes=[B["dbgtmp"]])
            out_evs.extend(dump_fm(yf, 2, [B["dbgtmp"]], row0=(ROW0 if dbg == "allmix" else 0)))
        out_proj(l, y, lambda k, tbi: [B[("df_y", k, tbi)]], 3, skip_ctx=not need_ctx)
        A.release(m1)

    PW = 2308

    def pcol(t):
        return 1 + t if t < CL else 3 + t

    def dwconv(praw, prb, wcols, bcol, out_fn):
        for (t0, n) in ((0, CL), (CL, 512), (CL + 512, 512), (CL + 1024, 512), (CL + 1536, 512)):
            c = pcol(t0)
            acc = A.alloc([128, 512], F32)
            S.op("dve", lambda: V.tensor_scalar(out=acc[:, 0:n], in0=praw[:, c - 1:c - 1 + n], scalar1=wcols[0], scalar2=None, op0=ALU.mult),
                 reads=prb, writes=[B["dw_acc"]])
            S.op("dve", lambda: V.scalar_tensor_tensor(out=acc[:, 0:n], in0=praw[:, c:c + n], scalar=wcols[1], in1=acc[:, 0:n],
                                                       op0=ALU.mult, op1=ALU.add), reads=prb + [B["dw_acc"]], writes=[B["dw_acc"]])
            S.op("dve", lambda: V.scalar_tensor_tensor(out=acc[:, 0:n], in0=praw[:, c + 1:c + 1 + n], scalar=wcols[2], in1=acc[:, 0:n],
                                                       op0=ALU.mult, op1=ALU.add), reads=prb + [B["dw_acc"]], writes=[B["dw_acc"]])
            out_fn(t0, n, acc, B["dw_acc"])
            A.release(A.mark() - 512 * 4)

    def proj_to_praw(wt, wb, c0, praw, prbuf):
        S.op("dve", lambda: V.memset(praw[:, 0:1], 0.0), writes=[prbuf])
        S.op("dve", lambda: V.memset(praw[:, 257:259], 0.0), writes=[prbuf])
        S.op("dve", lambda: V.memset(praw[:, 2307:2308], 0.0), writes=[prbuf])
        for tbi, (t0, n, j) in enumerate(TBS):
            ps, pb = proj_fm(wt, wb, c0, 128, tbi)
            c = pcol(t0)
            S.op("act", lambda: ACT.copy(out=praw[:, c:c + n], in_=ps[:, 0:n]), reads=[pb], writes=[prbuf])

    def mlstm_mixer(l, need_ctx):
        ROW0 = 0
        m1 = A.mark()
        y = A.alloc([128, 2, T], BF16)
        potA = [A.alloc([4, T], F32) for _ in range(2)]
        CT = A.alloc([128, NSC, 8], F32)
        mg = A.mark()
        wg, wgb = load_w_cols(l, [(1024, 16)], "ml_g")
        gi = A.alloc([4, T], F32)
        gf = A.alloc([4, T], F32)
        onesr = A.alloc([4, T], F32)
        tot = A.alloc([4, 1], F32)
        S.op("dve", lambda: V.memset(onesr[:], 1.0), writes=[B["ml_ones"]])
        for d_ in range(2):
            ty_i, ty_f = 2 * d_, 2 * d_ + 1
            for (ty, dstt, bn) in ((ty_i, gi, "ml_gi"), (ty_f, gf, "ml_gf")):
                for tbi, (t0, n, j) in enumerate(TBS):
                    ps, pb = proj_fm(wg, wgb, ty * 4, 4, tbi)
                    S.op("act", lambda: ACT.activation(out=dstt[:, t0:t0 + n], in_=ps[0:4, 0:n], func=AF.Identity,
                                                       bias=mlgb_t[:, l, ty:ty + 1], scale=1.0),
                         reads=[pb, B[("c", "mlgb")]], writes=[B[bn]])
            lfb = B["ml_gf"]
            S.op("act", lambda: ACT.activation(out=gf[:], in_=gf[:], func=AF.Exp, scale=-1.0), reads=[lfb], writes=[lfb])
            S.op("act", lambda: ACT.activation(out=gf[:], in_=gf[:], func=AF.Ln, bias=epsT[0:4, 2:3], scale=1.0), reads=[lfb, B["eps"]], writes=[lfb])
            S.op("dve", lambda: V.tensor_scalar(out=gf[:], in0=gf[:], scalar1=-1.0, scalar2=None, op0=ALU.mult), reads=[lfb], writes=[lfb])
            Ap, Ab = potA[d_], B[("ml_potA", d_)]
            if d_ == 0:
                S.op("dve", lambda: V.tensor_tensor_scan(out=Ap[:], data0=onesr[:], data1=gf[:], initial=0.0, op0=ALU.mult, op1=ALU.add),
                     reads=[lfb, B["ml_ones"]], writes=[Ab])
            else:
                for (c0, n) in ((0, CL), (CL, L)):
                    S.op("dve", lambda: V.tensor_tensor_scan(out=Ap[:, c0:c0 + n], data0=onesr[:, c0:c0 + n], data1=gf[:, c0:c0 + n], initial=0.0,
                                                             op0=ALU.mult, op1=ALU.add), reads=[lfb, B["ml_ones"]], writes=[Ab])
                S.op("dve", lambda: V.tensor_copy(out=tot[:], in_=Ap[:, T - 1:T]), reads=[Ab], writes=[B["ml_tot"]])
                S.op("dve", lambda: V.tensor_tensor(out=Ap[:], in0=gf[:], in1=Ap[:], op=ALU.subtract), reads=[lfb, Ab, B["ml_tot"]], writes=[Ab])
                S.op("dve", lambda: V.tensor_scalar(out=Ap[:, CL:T], in0=Ap[:, CL:T], scalar1=tot[:, 0:1], scalar2=None, op0=ALU.add),
                     reads=[Ab, B["ml_tot"]], writes=[Ab])
            S.op("dve", lambda: V.scalar_tensor_tensor(out=gi[:], in0=gi[:], scalar=math.log(0.125), in1=Ap[:], op0=ALU.add, op1=ALU.subtract),
                 reads=[B["ml_gi"], Ab], writes=[B["ml_gi"]])
            ps, pb = psR.next()
            for sc in range(NSC):
                mm(ps[:, sc * 4:sc * 4 + 4], gi[0:4, sc * 128:(sc + 1) * 128], ident_f[0:4, 0:4], True, True,
                   [B["ml_gi"], B[("c", "ident_f")]], [pb])
            S.op("dve", lambda: V.tensor_copy(out=CT[:, :, d_ * 4:(d_ + 1) * 4], in_=ps[:, 0:NSC * 4].rearrange("p (s e) -> p s e", e=4)),
                 reads=[pb], writes=[B["ml_CT"]])
        S.barrier()
        A.release(mg)
        mf = A.alloc([128, 128], BF16)
        mb = A.alloc([128, 128], BF16)
        S.dma(mf[:], cd["mask_f"][:, 384:512], writes=[B["ml_mask"]])
        S.dma(mb[:], cd["mask_b"][:, 384:512], writes=[B["ml_mask"]])
        et = A.alloc([128, 512], F32)
        negr = A.alloc([128, 1], F32)
        rpos = A.alloc([128, 1], F32)
        es = A.alloc([128, NSC], F32)
        Wt = [A.alloc([128, 512], BF16) for _ in range(4)]
        WR = Ring([(Wt[k], B[("ml_W", k)]) for k in range(4)])
        q1 = A.alloc([128, 512], F32)
        q3 = A.alloc([128, 512], F32)
        hd = A.alloc([128, 512], F32)
        sets = []
        for k in range(2):
            st_ = (A.alloc([128, 512], F32), A.alloc([128, 512], F32), A.alloc([128, 4, 4], F32), A.alloc([128, 8], F32), A.alloc([128, 8], F32))
            S.op("dve", lambda: V.memset(st_[3][:], 0.0), writes=[B[("ml_set", k)]])
            sets.append((st_, B[("ml_set", k)]))
        setR = Ring(sets)
        rs, t3, sgt = q1, q3, hd
        for ch in range(2):
            mch = A.mark()
            wv, wvb = load_w_cols(l, [(512 + ch * 128, 128)], "ml_wv")
            wo, wob = load_w_cols(l, [(768 + ch * 128, 128)], "ml_wo")
            qzm = A.alloc([128, 2, T], BF16)
            kc_ = A.alloc([128, T], BF16)
            m2 = A.mark()
            wq, wqb = load_w_cols(l, [(ch * 128, 128)], "ml_wq")
            wk, wkb = load_w_cols(l, [(256 + ch * 128, 128)], "ml_wk")
            qc = A.alloc([128, T], BF16)
            praw = A.alloc([128, PW], F32)
            for (wt_, wtb_, cidx, dstt, bn) in ((wq, wqb, ch, qc, "ml_q"), (wk, wkb, 2 + ch, kc_, "ml_k")):
                proj_to_praw(wt_, wtb_, 0, praw, B["ml_praw"])

                def fin(t0, n, acc, ab, cidx=cidx, dstt=dstt, bn=bn):
                    S.op("act", lambda: ACT.activation(out=dstt[:, t0:t0 + n], in_=acc[:, 0:n], func=AF.Silu,
                                                       bias=mlp_t[:, l, 12 + cidx:13 + cidx], scale=1.0),
                         reads=[ab, B[("c", "mlp")]], writes=[B[bn]])

                dwconv(praw, [B["ml_praw"]], [mlp_t[:, l, k * 4 + cidx:k * 4 + cidx + 1] for k in range(3)], None, fin)
            for (c0_, n_) in ((0, CL), (CL, 1024), (CL + 1024, 1024)):
                for hf in range(2):
                    S.op("dve", lambda: V.tensor_scalar(out=qzm[:, hf, c0_:c0_ + n_], in0=qc[:, c0_:c0_ + n_], scalar1=hm64[:, hf:hf + 1],
                                                        scalar2=None, op0=ALU.mult),
                         reads=[B["ml_q"], B[("c", "hm64")]], writes=[B["ml_qz"]])
            S.barrier()
            A.release(m2)
            vx = A.alloc([128, NSC, 2, 128], BF16)
            vx_build(l, wv, wvb, 0, 2, vx, "ml_vx")
            hsum = A.alloc([128, T], F32)
            jobs = []
            for hh in range(2):
                for d_ in range(2):
                    for tbi in range(5):
                        if tbi == 0 and not need_ctx:
                            continue
                        jobs.append((hh, d_, tbi))
            DEPTH = 2
            pend = []

            def setup(job):
                hh, d_, tbi = job
                t0, n, j = TBS[tbi]
                r0 = hh * 64
                h = 2 * ch + hh
                sc_lo = t0 // 128
                sc_hi = (t0 + n) // 128
                nsub = n // 128
                if tbi == 0:
                    off_scs = []
                elif d_ == 0:
                    off_scs = list(range(0, sc_lo))
                else:
                    off_scs = [0, 1] + list(range(sc_hi, NSC))
                (etd, etm, esd, rr, nrr), sbuf_ = setR.next()
                pbc, pbcb = psR.next()
                mm(pbc[:, 0:n], sel4[0:4, h, :], potA[d_][0:4, t0:t0 + n], True, True, [B[("c", "sel4")], B[("ml_potA", d_)]], [pbcb])
                rcol = 0 if d_ == 0 else n - 1
                S.op("dve", lambda: V.tensor_copy(out=rr[:, 0:1], in_=pbc[:, rcol:rcol + 1]), reads=[pbcb], writes=[sbuf_])
                for tt in range(nsub):
                    cc_ = tt * 128 + (0 if d_ == 0 else 127)
                    S.op("dve", lambda: V.tensor_copy(out=rr[:, 1 + tt:2 + tt], in_=pbc[:, cc_:cc_ + 1]), reads=[pbcb], writes=[sbuf_])
                S.op("dve", lambda: V.tensor_scalar(out=nrr[:, 0:5], in0=rr[:, 0:5], scalar1=-1.0, scalar2=None, op0=ALU.mult),
                     reads=[sbuf_], writes=[sbuf_])
                if off_scs:
                    S.op("act", lambda: ACT.activation(out=et[:, 0:n], in_=pbc[:, 0:n], func=AF.Exp, bias=nrr[:, 0:1], scale=1.0),
                         reads=[pbcb, sbuf_], writes=[B["ml_et"]])
                    S.op("act", lambda: ACT.activation(out=es[:, :], in_=CT[:, :, d_ * 4 + h], func=AF.Exp, bias=rr[:, 0:1], scale=1.0),
                         reads=[B["ml_CT"], sbuf_], writes=[B["ml_es"]])
                tri = mf if d_ == 0 else mb
                for tt in range(nsub):
                    S.op("act", lambda: ACT.activation(out=etd[:, tt * 128:(tt + 1) * 128], in_=pbc[:, tt * 128:(tt + 1) * 128], func=AF.Exp,
                                                       bias=nrr[:, 1 + tt:2 + tt], scale=1.0),
                         reads=[pbcb, sbuf_], writes=[sbuf_])
                    S.op("dve", lambda: V.tensor_tensor(out=etm[:, tt * 128:(tt + 1) * 128], in0=etd[:, tt * 128:(tt + 1) * 128], in1=tri[:, :], op=ALU.mult),
                         reads=[sbuf_, B["ml_mask"]], writes=[sbuf_])
                    S.op("act", lambda: ACT.activation(out=esd[:, tt, 0:nsub], in_=CT[:, sc_lo:sc_lo + nsub, d_ * 4 + h], func=AF.Exp,
                                                       bias=rr[:, 1 + tt:2 + tt], scale=1.0),
                         reads=[B["ml_CT"], sbuf_], writes=[sbuf_])
                acc, accb = psA.next()
                items = []
                first = True
                for sc in off_scs:
                    items.append(dict(sc=sc, c0=0, w=n, es=es[:, sc:sc + 1], tgt=et[:, 0:n], rd=[B["ml_es"], B["ml_et"]], start=first, stop=False))
                    first = False
                for tt in range(nsub):
                    ks = list(range(0, tt + 1)) if d_ == 0 else list(range(tt, nsub))
                    for ki, k in enumerate(ks):
                        items.append(dict(sc=sc_lo + k, c0=tt * 128, w=128, es=esd[:, tt, k:k + 1],
                                          tgt=(etm if k == tt else etd)[:, tt * 128:(tt + 1) * 128], rd=[sbuf_],
                                          start=(first and ki == 0), stop=(ki == len(ks) - 1)))
                for it in items:
                    it.update(job=job, acc=acc, accb=accb, r0=r0, hh=hh, t0=t0, n=n, fin=False)
                items[-1]["fin"] = True
                return items

            def stage_a(it):
                r0, t0, c0, w, sc = it["r0"], it["t0"], it["c0"], it["w"], it["sc"]
                ps, pb = psR.next()
                mm(ps[:, 0:w], kc_[:, sc * 128:(sc + 1) * 128], qzm[:, it["hh"], t0 + c0:t0 + c0 + w], True, True,
                   [B["ml_k"], B["ml_qz"]], [pb])
                w_, wb_ = WR.next()
                S.op("dve", lambda: V.scalar_tensor_tensor(out=w_[:, 0:w], in0=ps[:, 0:w], scalar=it["es"], in1=it["tgt"],
                                                           op0=ALU.mult, op1=ALU.mult),
                     reads=[pb] + it["rd"], writes=[wb_])
                it["w_"], it["wb_"] = w_, wb_
                return it

            def stage_b(it):
                acc, accb, c0, w, sc, hh = it["acc"], it["accb"], it["c0"], it["w"], it["sc"], it["hh"]
                mm(acc[:, c0:c0 + w], vx[:, sc, hh, :], it["w_"][:, 0:w], it["start"], it["stop"], [B[("ml_vx", tbi_of_sc(sc))], it["wb_"]], [accb])
                if not it["fin"]:
                    return
                hh_, d_, tbi = it["job"]
                t0, n, r0 = it["t0"], it["n"], it["r0"]
                S.op("dve", lambda: V.tensor_scalar(out=q3[64:128, 0:n], in0=acc[64:128, 0:n], scalar1=-1.0, scalar2=1.0, op0=ALU.mult, op1=ALU.max),
                     reads=[accb], writes=[B["ml_q3"]])
                S.op("dve", lambda: V.tensor_tensor(out=q3[64:128, 0:n], in0=acc[64:128, 0:n], in1=q3[64:128, 0:n], op=ALU.max),
                     reads=[accb, B["ml_q3"]], writes=[B["ml_q3"]])
                S.op("act", lambda: ACT.activation(out=q3[64:128, 0:n], in_=q3[64:128, 0:n], func=AF.Ln), reads=[B["ml_q3"]], writes=[B["ml_q3"]])
                S.op("act", lambda: ACT.activation(out=q3[64:128, 0:n], in_=q3[64:128, 0:n], func=AF.Exp, scale=-1.0), reads=[B["ml_q3"]], writes=[B["ml_q3"]])
                if d_ == 0:
                    S.op("dve", lambda: V.tensor_tensor(out=hsum[r0:r0 + 64, t0:t0 + n], in0=acc[0:64, 0:n], in1=q3[64:128, 0:n], op=ALU.mult),
                         reads=[accb, B["ml_q3"]], writes=[B[("ml_hs", tbi)]])
                else:
                    S.op("dve", lambda: V.tensor_tensor(out=hd[r0:r0 + 64, 0:n], in0=acc[0:64, 0:n], in1=q3[64:128, 0:n], op=ALU.mult),
                         reads=[accb, B["ml_q3"]], writes=[B["ml_hd"]])
                    S.op("dve", lambda: V.tensor_tensor(out=hsum[r0:r0 + 64, t0:t0 + n], in0=hsum[r0:r0 + 64, t0:t0 + n], in1=hd[r0:r0 + 64, 0:n], op=ALU.add),
                         reads=[B["ml_hd"], B[("ml_hs", tbi)]], writes=[B[("ml_hs", tbi)]])

            for job in jobs:
                for it in setup(job):
                    pend.append(stage_a(it))
                    if len(pend) > DEPTH:
                        stage_b(pend.pop(0))
            while pend:
                stage_b(pend.pop(0))
            S.barrier()
            for tbi, (t0, n, j) in enumerate(TBS):
                if tbi == 0 and not need_ctx:
                    continue
                m3 = A.mark()
                head_rms(hsum[:, t0:t0 + n], [B[("ml_hs", tbi)]], n, blk64, "blk64", 64, rs[:, 0:n], [B["ml_rs"]])
                S.op("dve", lambda: V.scalar_tensor_tensor(out=t3[:, 0:n], in0=hsum[:, t0:t0 + n], scalar=mlp_t[:, l, 16 + ch:17 + ch],
                                                           in1=rs[:, 0:n], op0=ALU.mult, op1=ALU.mult),
                     reads=[B[("ml_hs", tbi)], B["ml_rs"], B[("c", "mlp")]], writes=[B["ml_t3"]])
                ps, pb = proj_fm(wo, wob, 0, 128, tbi)
                S.op("act", lambda: ACT.activation(out=sgt[:, 0:n], in_=ps[:, 0:n], func=AF.Sigmoid), reads=[pb], writes=[B["ml_sg"]])
                S.op("dve", lambda: V.tensor_tensor(out=y[:, ch, t0:t0 + n], in0=t3[:, 0:n], in1=sgt[:, 0:n], op=ALU.mult),
                     reads=[B["ml_t3"], B["ml_sg"]], writes=[B[("ml_y", ch, tbi)]])
                A.release(m3)
            S.barrier()
            A.release(mch)
        S.barrier()
        A.release(m1 + 2 * T * 2)
        if dbg in ("mlstm", "allmix") and l == 0:
            yf = A.alloc([128, 2, T], F32)
            S.op("dve", lambda: V.tensor_copy(out=yf[:], in_=y[:]), reads=[B[("ml_y", k, t)] for k in range(2) for t in range(5)], writes=[B["dbgtmp"]])
            out_evs.extend(dump_fm(yf, 2, [B["dbgtmp"]], row0=(ROW0 if dbg == "allmix" else 0)))
        out_proj(l, y, lambda k, tbi: [B[("ml_y", k, tbi)]], 0, skip_ctx=not need_ctx)
        A.release(m1)

    def hyena_mixer(l, need_ctx):
        ROW0 = 512
        m1 = A.mark()
        y = A.alloc([128, 2, T], BF16)
        b0 = 1552
        vf = A.alloc([128, 2, T], BF16)
        zf = A.alloc([128, 2, T], BF16)
        x1 = A.alloc([128, 2, T], BF16)
        x2 = A.alloc([128, 2, T], BF16)
        m2 = A.mark()
        wh, whb = load_w_cols(l, [(b0, 768)], "hy_w")
        praw = A.alloc([128, PW], F32)
        dsts = [vf, vf, x1, x1, x2, x2]
        for c6 in range(6):
            proj_to_praw(wh, whb, c6 * 128, praw, B["hy_praw"])

            def fin(t0, n, acc, ab, c6=c6):
                S.op("act", lambda: ACT.activation(out=dsts[c6][:, c6 % 2, t0:t0 + n], in_=acc[:, 0:n], func=AF.Identity,
                                                   bias=hyp_t[:, l, 18 + c6:19 + c6], scale=1.0),
                     reads=[ab, B[("c", "hyp")]], writes=[B[("hy_u", c6)]])

            dwconv(praw, [B["hy_praw"]], [hyp_t[:, l, k * 6 + c6:k * 6 + c6 + 1] for k in range(3)], None, fin)
        S.barrier()
        A.release(m2)
        xn_off = 16640 + (base_mark - 16640)
        A2 = Arena(nc, base_mark, base_mark + KC * T * 2)
        A2.n = 100000 + l * 1000

        def conv_seg(nm, order, src, srcb, t_lo, Ln, epi):
            ntc = Ln // 128
            a2m = A2.mark()
            am = A.mark()
            utok = A.alloc([128, ntc, 256], BF16)
            for tc in range(ntc):
                pt, ptb = psR.next()
                for cc_ in range(2):
                    mm(pt[:, cc_ * 128:(cc_ + 1) * 128], src[:, cc_, t_lo + tc * 128:t_lo + (tc + 1) * 128], ident_bf[:, :], True, True,
                       srcb + [B[("c", "ident_bf")]], [ptb])
                S.op("act", lambda: ACT.copy(out=utok[:, tc, :], in_=pt[:, 0:256]), reads=[ptb], writes=[B["hy_utok"]])
            Y = A2.alloc([128, ntc, 2, 256], BF16)
            Yny = A2.alloc([1, 256], BF16)
            Fst = [A2.alloc([128, ntc, 2, 128], BF16) for _ in range(2)]
            FR = Ring([(Fst[k], B[("hy_F", k)]) for k in range(2)])
            Hst = [A2.alloc([128, 2, 256], BF16) for _ in range(2)]
            HR = Ring([(Hst[k], B[("hy_H", k)]) for k in range(2)])
            Hd = Hl if nm == "l" else Hc
            tt = [A.alloc([128, 256], F32) for _ in range(4)]
            for fc in range(ntc):
                ft, fb = FR.next()
                S.dma(ft[:], cd["F_" + nm][fc], writes=[fb])
                ht, hb_ = HR.next()
                S.dma(ht[:], Hd[l, fc, :, order, :, :], reads=[B[("Hd", nm, l)]], writes=[hb_])
                ps, pb = psR.next()
                for ri in range(2):
                    for tc in range(ntc):
                        mm(ps[:, ri * 256:(ri + 1) * 256], ft[:, tc, ri, :], utok[:, tc, :], tc == 0, tc == ntc - 1, [fb, B["hy_utok"]], [pb])
                ure, uim = ps[:, 0:256], ps[:, 256:512]
                S.op("dve", lambda: V.tensor_tensor(out=tt[0][:], in0=ure, in1=ht[:, 0, :], op=ALU.mult), reads=[pb, hb_], writes=[B["hy_tt0"]])
                S.op("dve", lambda: V.tensor_tensor(out=tt[1][:], in0=uim, in1=ht[:, 1, :], op=ALU.mult), reads=[pb, hb_], writes=[B["hy_tt1"]])
                S.op("dve", lambda: V.tensor_tensor(out=Y[:, fc, 0, :], in0=tt[0][:], in1=tt[1][:], op=ALU.subtract),
                     reads=[B["hy_tt0"], B["hy_tt1"]], writes=[B["hy_Y"]])
                S.op("dve", lambda: V.tensor_tensor(out=tt[2][:], in0=ure, in1=ht[:, 1, :], op=ALU.mult), reads=[pb, hb_], writes=[B["hy_tt2"]])
                S.op("dve", lambda: V.tensor_tensor(out=tt[3][:], in0=uim, in1=ht[:, 0, :], op=ALU.mult), reads=[pb, hb_], writes=[B["hy_tt3"]])
                S.op("dve", lambda: V.tensor_tensor(out=Y[:, fc, 1, :], in0=tt[2][:], in1=tt[3][:], op=ALU.add),
                     reads=[B["hy_tt2"], B["hy_tt3"]], writes=[B["hy_Y"]])
            alt = A2.alloc([128, ntc], BF16)
            S.dma(alt[:], cd["alt_" + nm], writes=[B["hy_alt"]])
            hny = A2.alloc([1, 256], BF16)
            S.dma(hny[:], (Hlny if nm == "l" else Hcny)[l, :, order * 256:(order + 1) * 256], reads=[B[("Hd", nm, l)]], writes=[B["hy_hny"]])
            icny = A.alloc([1, Ln], BF16)
            S.dma(icny[:], cd["icny_" + nm], writes=[B["hy_icny"]])
            ps, pb = psR.next()
            for tc in range(ntc):
                mm(ps[0:1, 0:256], alt[:, tc:tc + 1], utok[:, tc, :], tc == 0, tc == ntc - 1, [B["hy_alt"], B["hy_utok"]], [pb])
            S.op("dve", lambda: V.tensor_tensor(out=Yny[:], in0=ps[0:1, 0:256], in1=hny[:], op=ALU.mult), reads=[pb, B["hy_hny"]], writes=[B["hy_Yny"]])
            nbk = max(1, Ln // 512)
            bw = Ln // nbk
            nhalf = 4 if ntc >= 8 else 1
            fh = ntc // nhalf
            Ist = [A.alloc([128, fh, 2, bw], BF16) for _ in range(2)]
            IR = Ring([(Ist[k], B[("hy_I", k)]) for k in range(2)])
            for bk in range(nbk):
                accs = [psA.next() for _ in range(2)]
                for hf in range(nhalf):
                    it, ib = IR.next()
                    S.dma(it[:], cd["I_" + nm][:, hf * fh:(hf + 1) * fh, :, bk * bw:(bk + 1) * bw], writes=[ib])
                    for cc_ in range(2):
                        for fq in range(fh):
                            fc = hf * fh + fq
                            for ri in range(2):
                                mm(accs[cc_][0][:, 0:bw], Y[:, fc, ri, cc_ * 128:(cc_ + 1) * 128], it[:, fq, ri, :],
                                   (fc == 0 and ri == 0), False, [B["hy_Y"], ib], [accs[cc_][1]])
                for cc_ in range(2):
                    mm(accs[cc_][0][:, 0:bw], Yny[0:1, cc_ * 128:(cc_ + 1) * 128], icny[0:1, bk * bw:(bk + 1) * bw], False, True,
                       [B["hy_Yny"], B["hy_icny"]], [accs[cc_][1]])
                    epi(cc_, t_lo + bk * bw, bw, accs[cc_][0], accs[cc_][1])
            S.barrier()
            A.release(am)
            A2.release(a2m)

        tmpc = A.alloc([128, 512], F32)
        for (nm, t_lo, Ln) in ((("c", 0, CL),) if need_ctx else ()) + (("l", CL, L),):
            def epi1(cc_, t0, n, ps, pb):
                S.op("dve", lambda: V.scalar_tensor_tensor(out=tmpc[:, 0:n], in0=vf[:, cc_, t0:t0 + n], scalar=hyp_t[:, l, 24 + cc_:25 + cc_],
                                                           in1=ps[:, 0:n], op0=ALU.mult, op1=ALU.add),
                     reads=[pb, B[("hy_u", cc_)], B[("c", "hyp")]], writes=[B["hy_tmpc"]])
                S.op("dve", lambda: V.tensor_tensor(out=zf[:, cc_, t0:t0 + n], in0=tmpc[:, 0:n], in1=x1[:, cc_, t0:t0 + n], op=ALU.mult),
                     reads=[B["hy_tmpc"], B[("hy_u", 2 + cc_)]], writes=[B["hy_z"]])

            conv_seg(nm, 0, vf, [B[("hy_u", 0)], B[("hy_u", 1)]], t_lo, Ln, epi1)

            def epi2(cc_, t0, n, ps, pb):
                S.op("dve", lambda: V.scalar_tensor_tensor(out=tmpc[:, 0:n], in0=zf[:, cc_, t0:t0 + n], scalar=hyp_t[:, l, 26 + cc_:27 + cc_],
                                                           in1=ps[:, 0:n], op0=ALU.mult, op1=ALU.add),
                     reads=[pb, B["hy_z"], B[("c", "hyp")]], writes=[B["hy_tmpc"]])
                S.op("dve", lambda: V.tensor_tensor(out=y[:, cc_, t0:t0 + n], in0=tmpc[:, 0:n], in1=x2[:, cc_, t0:t0 + n], op=ALU.mult),
                     reads=[B["hy_tmpc"], B[("hy_u", 4 + cc_)]], writes=[B["hy_y"]])

            conv_seg(nm, 1, zf, [B["hy_z"]], t_lo, Ln, epi2)
        if dbg in ("hyena", "allmix") and l == 0:
            yf = A.alloc([128, 2, T], F32)
            S.op("dve", lambda: V.tensor_copy(out=yf[:], in_=y[:]), reads=[B["hy_y"]], writes=[B["dbgtmp"]])
            out_evs.extend(dump_fm(yf, 2, [B["dbgtmp"]], row0=(ROW0 if dbg == "allmix" else 0)))
        S.barrier()
        A.release(m1 + 2 * T * 2)
        out_proj(l, y, lambda k, tbi: [B["hy_y"]], 2, skip_ctx=not need_ctx)
        A.release(m1)

    def finish_dbg(t, nchunks, bufs):
        out_evs.extend(dump_fm(t, nchunks, bufs))

    stop = False

    def scoped(name, fn, *a, **kw):
        with nc.named_scope(name):
            return fn(*a, **kw)

    for l in range(n_layers):
        need_ctx = l < NL - 1
        scoped("L%d_norm0" % l, modnorm, l, 0)
        scoped("L%d_ffn0" % l, ffn, l, 0, 0)
        if dbg == "ffn1" and l == 0:
            finish_dbg(hT, KC, [B[("h", kc, t)] for kc in range(KC) for t in range(5)])
            stop = True
            break
        scoped("L%d_norm1" % l, modnorm, l, 1)
        scoped("L%d_gqa" % l, gqa_mixer, l, need_ctx)
        scoped("L%d_diff" % l, diff_mixer, l, need_ctx)
        scoped("L%d_mlstm" % l, mlstm_mixer, l, need_ctx)
        scoped("L%d_hyena" % l, hyena_mixer, l, need_ctx)
        if dbg == "allmix":
            stop = True
            break
        scoped("L%d_norm2" % l, modnorm, l, 2)
        scoped("L%d_ffn1" % l, ffn, l, 1, 2, skip_ctx=not need_ctx)

    for kc in range(KC):
        out_evs.append(S.dma(outT[kc * 128:(kc + 1) * 128, :], hT[:, kc, CL:T], reads=[B[("h", kc, t)] for t in range(1, 5)]))
    for ev in out_evs:
        S._wait("sp", ev)
    for e_ in S.h:
        S._flush(e_)
    S.close()
    return nc


def _tile_rows(v, reps):
    return np.tile(np.asarray(v, np.float32), reps)


def make_in_maps(inputs, consts):
    f = lambda a: np.ascontiguousarray(np.asarray(a, dtype=np.float32))
    x, c, ctx, c_ctx = f(inputs["x"]), f(inputs["c"]), f(inputs["ctx"]), f(inputs["c_ctx"])
    shared = {}
    shared["ada_w"] = f(inputs["ada_w"])
    shared["adabT"] = np.ascontiguousarray(f(inputs["ada_b"]).reshape(NL, 72, 128).transpose(2, 0, 1).reshape(128, NL * 72))
    shared["normgT"] = np.ascontiguousarray(f(inputs["norm_g"]).reshape(NL, 3, KC, 128).transpose(3, 0, 1, 2).reshape(128, NL * 3 * KC))
    shared["ffn_w13"] = f(inputs["ffn_w13"])
    shared["ffn_w2"] = f(inputs["ffn_w2"])
    shared["w_in"] = f(inputs["w_in"])
    shared["w_out"] = f(inputs["w_out"])
    mlp = np.zeros((128, NL, 24), np.float32)
    cw = f(inputs["ml_conv_w"]).reshape(NL, 3, 4, 128)
    mlp[:, :, 0:12] = cw.transpose(3, 0, 1, 2).reshape(128, NL, 12)
    mlp[:, :, 12:16] = f(inputs["ml_conv_b"]).reshape(NL, 4, 128).transpose(2, 0, 1)
    mlp[:, :, 16:18] = f(inputs["ml_norm_g"]).reshape(NL, 2, 128).transpose(2, 0, 1)
    shared["mlp"] = mlp
    shared["mlgb"] = np.ascontiguousarray(f(inputs["ml_gate_b"]).reshape(NL, 4, 4).transpose(2, 0, 1))
    gq = f(inputs["gqa_qk_g"])
    shared["gqp"] = np.ascontiguousarray(np.tile(gq, (1, 1, 2)).transpose(2, 0, 1))
    hyp = np.zeros((128, NL, 28), np.float32)
    hw = f(inputs["hy_conv_w"]).reshape(NL, 3, 6, 128)
    hyp[:, :, 0:18] = hw.transpose(3, 0, 1, 2).reshape(128, NL, 18)
    hyp[:, :, 18:24] = f(inputs["hy_conv_b"]).reshape(NL, 6, 128).transpose(2, 0, 1)
    hyp[:, :, 24:28] = f(inputs["hy_skip"]).reshape(NL, 2, 2, 128).transpose(3, 0, 1, 2).reshape(128, NL, 4)
    shared["hyp"] = hyp
    shared["hyf1"] = f(inputs["hy_filt_w1"])
    shared["hyf2"] = f(inputs["hy_filt_w2"])
    shared["hyf3"] = f(inputs["hy_filt_w3"])
    shared["hyfb"] = np.ascontiguousarray(np.stack([f(inputs["hy_filt_b1"]), f(inputs["hy_filt_b2"])], -1).transpose(1, 0, 2))
    shared["hyb3"] = f(inputs["hy_filt_b3"]).reshape(NL, 1, 512)
    dq = f(inputs["diff_qk_g"])
    dfp = np.zeros((128, NL, 3), np.float32)
    dfp[:, :, 0:2] = np.tile(dq, (1, 1, 4)).transpose(2, 0, 1)
    dfp[:, :, 2] = np.tile(f(inputs["diff_subln_g"]), (1, 2)).T
    shared["dfp"] = dfp
    shared["dlam"] = np.ascontiguousarray(np.broadcast_to(f(inputs["diff_lambda"]).reshape(1, NL, 128), (128, NL, 128)))
    for k, v in consts.items():
        shared["c_" + k] = v
    maps = []
    for b in range(8):
        m = dict(shared)
        m["xT"] = np.ascontiguousarray(x[b].T)
        m["ctxT"] = np.ascontiguousarray(ctx[b].T)
        ccb = np.stack([c[b].reshape(KC, 128).T, c_ctx.reshape(KC, 128).T], -1)
        m["cc"] = np.ascontiguousarray(ccb)
        maps.append(m)
    return maps


def run(inputs, n_layers=NL, dbg=None, cores=8):
    consts = host_consts()
    nc = build(consts, n_layers=n_layers, dbg=dbg)
    maps = make_in_maps(inputs, consts)[:cores]
    res = run_bass_kernel_spmd(nc, maps, core_ids=list(range(cores)))
    return res


def kernel(**inputs):
    res = run(inputs)
    out = np.stack([np.ascontiguousarray(r["outT"].T) for r in res.results], 0)
    return out.astype(np.float32)
```

```python
import math
import collections
import numpy as np
import ml_dtypes
import concourse.bass as bass
import concourse.mybir as mybir
from concourse.bass_utils import run_bass_kernel_spmd

F32 = mybir.dt.float32
BF16 = mybir.dt.bfloat16
AF = mybir.ActivationFunctionType
ALU = mybir.AluOpType
NPBF = ml_dtypes.bfloat16

D = 1024
L = 2048
CL = 256
T = CL + L
KC = 8
NL = 4
HID = 2816
NJ = 22
EPS = 1e-6
TBS = [(0, 256, 1), (256, 512, 0), (768, 512, 0), (1280, 512, 0), (1792, 512, 0)]
NSC = 18
EPOCH = 30000


def tbi_of_sc(sc):
    return 0 if sc < 2 else 1 + (sc - 2) // 4


class Buf:
    __slots__ = ("name", "w", "r")

    def __init__(self, name):
        self.name = name
        self.w = None
        self.r = []


class BufMap(dict):
    def __missing__(self, k):
        b = Buf(k)
        self[k] = b
        return b


class Sched:
    def __init__(self, nc, n_dma_sems=28):
        self.nc = nc
        self.h = {"pe": nc.tensor, "dve": nc.vector, "act": nc.scalar, "pool": nc.gpsimd, "sp": nc.sync}
        self.cnt = {k: 0 for k in self.h}
        self.sems = {k: [] for k in self.h}
        self.known = {k: {} for k in self.h}
        self.dma_sems = []
        self.dma_val = []
        self.n_dma_sems = n_dma_sems
        self.dma_rr = 0
        self._ctx = []
        self.pending = {}
        self.fuse_waits = True

    def _new_sem(self, name):
        g = self.nc.semaphore(name)
        s = g.__enter__()
        self._ctx.append(g)
        return s

    def close(self):
        for g in reversed(self._ctx):
            g.__exit__(None, None, None)
        self._ctx = []

    def _wait(self, eng, ev):
        if ev is None:
            return
        if ev[0] == "e":
            _, src, n = ev
            if src == eng and eng == "pe":
                return
            if self.known[eng].get(src, 0) >= n:
                return
            self.known[eng][src] = n
            ep, v = (n - 1) // EPOCH, (n - 1) % EPOCH + 1
            self._emit_wait(eng, self.sems[src][ep], v)
        else:
            _, si, val = ev
            key = ("d", si)
            if self.known[eng].get(key, 0) >= val:
                return
            self.known[eng][key] = val
            self._emit_wait(eng, self.dma_sems[si], val)

    def _emit_wait(self, eng, sem, val):
        if self.pending.get(eng) is not None:
            ps, pv = self.pending[eng]
            self.h[eng].wait_ge(ps, pv)
        self.pending[eng] = (sem, val)

    def _flush(self, eng, ins=None):
        p = self.pending.get(eng)
        if p is None:
            return
        self.pending[eng] = None
        if ins is not None and self.fuse_waits:
            ins._wait_ge(p[0], p[1])
        else:
            self.h[eng].wait_ge(p[0], p[1])

    def _deps(self, eng, reads, writes):
        for b in reads:
            self._wait(eng, b.w)
        for b in writes:
            self._wait(eng, b.w)
            for ev in b.r:
                if ev[0] == "e" and ev[1] == eng:
                    continue
                self._wait(eng, ev)

    def _mark(self, ev, reads, writes):
        for b in reads:
            b.r = [e for e in b.r if not (e[0] == ev[0] and e[1] == ev[1])] + [ev]
        for b in writes:
            b.w = ev
            b.r = []

    def op(self, eng, fn, reads=(), writes=()):
        self._deps(eng, reads, writes)
        if eng != "pe":
            self._flush(eng)
        ins = fn()
        if eng == "pe":
            self._flush(eng, ins)
        self.cnt[eng] += 1
        n = self.cnt[eng]
        ep = (n - 1) // EPOCH
        while len(self.sems[eng]) <= ep:
            self.sems[eng].append(self._new_sem("s_%s_%d" % (eng, len(self.sems[eng]))))
        ins.then_inc(self.sems[eng][ep], 1)
        self._mark(("e", eng, n), reads, writes)
        return ins

    def dma(self, out, in_, reads=(), writes=(), q="sp"):
        if len(self.dma_sems) < self.n_dma_sems:
            self.dma_sems.append(self._new_sem("s_dma_%d" % len(self.dma_sems)))
            self.dma_val.append(0)
        si = self.dma_rr % self.n_dma_sems
        self.dma_rr += 1
        if self.dma_val[si] > 0:
            self._wait(q, ("d", si, self.dma_val[si]))
        self._deps(q, reads, writes)
        self._flush(q)
        ins = self.h[q].dma_start(out=out, in_=in_)
        self.dma_val[si] += 16
        ins.then_inc(self.dma_sems[si], 16)
        ev = ("d", si, self.dma_val[si])
        self._mark(ev, reads, writes)
        return ev

    def barrier(self):
        for eng in self.h:
            for src in self.h:
                if src != eng and self.cnt[src] > 0:
                    self._wait(eng, ("e", src, self.cnt[src]))
            for si in range(len(self.dma_sems)):
                if self.dma_val[si] > 0:
                    self._wait(eng, ("d", si, self.dma_val[si]))
            self._flush(eng)


class Ring:
    def __init__(self, items):
        self.items = items
        self.i = 0

    def next(self):
        it = self.items[self.i % len(self.items)]
        self.i += 1
        return it


class Arena:
    def __init__(self, nc, lo, hi):
        self.nc, self.lo, self.hi, self.p = nc, lo, hi, lo
        self.n = 0

    def alloc(self, shape, dt, at=None):
        nbytes = int(np.prod(shape[1:])) * (2 if dt == BF16 else 4)
        nbytes = (nbytes + 63) // 64 * 64
        if at is None:
            at = self.p
            self.p += nbytes
            assert self.p <= self.hi, ("SBUF arena overflow", self.p, self.hi)
        self.n += 1
        return self.nc.alloc_sbuf_tensor_at("a%d" % self.n, list(shape), dt, offset=at)

    def mark(self):
        return self.p

    def release(self, m):
        self.p = m


def _rope_tables(dim):
    rows = L // 64
    row = np.repeat(np.arange(rows), 64).astype(np.float32)
    col = np.tile(np.arange(64), rows).astype(np.float32)
    nf = dim // 4
    inv = (np.float32(10000.0) ** (-np.arange(nf, dtype=np.float32) / np.float32(nf))).astype(np.float32)
    ang = np.concatenate([row[:, None] * inv, col[:, None] * inv], axis=-1).astype(np.float32)
    half = dim // 2
    idx = np.arange(128) % half
    cosT = np.cos(ang)[:, idx].T.astype(np.float32)
    sinT = np.sin(ang)[:, idx].T.astype(np.float32)
    return np.ascontiguousarray(cosT), np.ascontiguousarray(sinT)


def _rot_mat(dim):
    half = dim // 2
    m = np.zeros((128, 128), np.float32)
    for o in range(128):
        if o % dim < half:
            m[o + half, o] = -1.0
        else:
            m[o - half, o] = 1.0
    return m


def _dft_consts(Ln):
    n = 2 * Ln
    t = np.arange(Ln, dtype=np.float64)
    f = np.arange(Ln, dtype=np.float64)
    ang = 2.0 * np.pi * np.outer(t, f) / n
    FC = np.cos(ang)
    FS = -np.sin(ang)
    cf = np.full(Ln, 2.0)
    cf[0] = 1.0
    angi = 2.0 * np.pi * np.outer(f, t + Ln // 2) / n
    IC = cf[:, None] / n * np.cos(angi)
    IS = -cf[:, None] / n * np.sin(angi)
    alt = (-1.0) ** t
    icny = alt / n
    return FC, FS, IC, IS, alt, icny


def _hy_feats(Ln):
    t = np.arange(Ln, dtype=np.float32)
    tn = (t / np.float32(Ln)).astype(np.float32)
    bands = np.arange(1, 17, dtype=np.float32)
    ang = (np.float32(2.0 * math.pi) * tn[:, None] * bands).astype(np.float32)
    feats = np.concatenate([tn[:, None], np.cos(ang), np.sin(ang)], axis=-1).astype(np.float32)
    dist = (np.abs(t - Ln // 2) / np.float32(Ln / 2)).astype(np.float32)
    deltas = np.abs(np.linspace(math.log(1e-2) / 1.5, math.log(1e-2) / 0.3, 256, dtype=np.float32))
    win = np.exp(-dist[:, None] * np.tile(deltas, 2)).astype(np.float32)
    return np.ascontiguousarray(feats.T), win


_CONST_CACHE = {}


def host_consts():
    if _CONST_CACHE:
        return _CONST_CACHE
    c = {}
    c["ones_bf"] = np.ones((128, 128), NPBF)
    c["ones_f"] = np.ones((128, 128), np.float32)
    b64 = np.zeros((128, 128), np.float32)
    b64[:64, :64] = 1
    b64[64:, 64:] = 1
    c["blk64"] = b64.astype(NPBF)
    b32 = np.zeros((128, 128), np.float32)
    for i in range(4):
        b32[32 * i:32 * i + 32, 32 * i:32 * i + 32] = 1
    c["blk32"] = b32.astype(NPBF)
    c["ident_bf"] = np.eye(128, dtype=np.float32).astype(NPBF)
    c["ident_f"] = np.eye(128, dtype=np.float32)
    sel = np.zeros((4, 4, 128), np.float32)
    for hh in range(4):
        sel[hh, hh, :] = 1
    c["sel4"] = sel
    c["rot64"] = _rot_mat(64)
    c["rot32"] = _rot_mat(32)
    c["cos64"], c["sin64"] = _rope_tables(64)
    c["cos32"], c["sin32"] = _rope_tables(32)
    p = np.arange(128)[:, None]
    xx = np.arange(896)[None, :] - 384
    c["mask_f"] = ((xx - p) >= 0).astype(np.float32).astype(NPBF)
    c["mask_b"] = ((xx - p) <= 0).astype(np.float32).astype(NPBF)
    hm = np.zeros((128, 2), np.float32)
    hm[(np.arange(128) % 64) < 32, 0] = 1
    hm[(np.arange(128) % 64) >= 32, 1] = 1
    c["halfmask"] = hm
    h64 = np.zeros((128, 2), np.float32)
    h64[:64, 0] = 1
    h64[64:, 1] = 1
    c["hm64"] = h64
    q32 = np.zeros((128, 4), np.float32)
    for i in range(4):
        q32[32 * i:32 * i + 32, i] = 1
    c["qm32"] = q32
    for nm, Ln in (("l", L), ("c", CL)):
        FC, FS, IC, IS, alt, icny = _dft_consts(Ln)
        ntc = Ln // 128
        F = np.stack([FC, FS], 0).reshape(2, ntc, 128, ntc, 128)
        c["F_" + nm] = np.ascontiguousarray(F.transpose(3, 2, 1, 0, 4)).astype(NPBF)
        c["alt_" + nm] = np.ascontiguousarray(alt.reshape(ntc, 128).T).astype(NPBF)
        I = np.stack([IC, IS], 0).reshape(2, ntc, 128, Ln)
        c["I_" + nm] = np.ascontiguousarray(I.transpose(2, 1, 0, 3)).astype(NPBF)
        c["icny_" + nm] = icny.reshape(1, Ln).astype(NPBF)
        ft, win = _hy_feats(Ln)
        c["feats_" + nm] = ft
        c["win_" + nm] = np.ascontiguousarray(win.reshape(ntc, 128, 512).transpose(1, 0, 2))
    _CONST_CACHE.update(c)
    return c


def _npdt(a):
    return BF16 if a.dtype == NPBF else F32


def build(consts, n_layers=NL, dbg=None):
    nc = bass.Bass("TRN2", target_bir_lowering=False)
    S = Sched(nc)
    B = BufMap()
    V, ACT, PE = nc.vector, nc.scalar, nc.tensor

    def din(name, shape, dt=F32):
        return nc.dram_tensor(name, list(shape), dt, kind="ExternalInput").ap()

    xT = din("xT", [D, L])
    ctxT = din("ctxT", [D, CL])
    cc = din("cc", [128, KC, 2])
    ada_w = din("ada_w", [NL, D, 9 * D])
    adabT = din("adabT", [128, NL * 72])
    normgT = din("normgT", [128, NL * 3 * KC])
    w13 = din("ffn_w13", [NL, 2, D, 2 * HID])
    w2 = din("ffn_w2", [NL, 2, HID, D])
    w_in = din("w_in", [NL, D, 3088])
    w_out = din("w_out", [NL, D, D])
    mlp = din("mlp", [128, NL, 24])
    mlgb = din("mlgb", [4, NL, 4])
    gqp = din("gqp", [128, NL, 2])
    hyp = din("hyp", [128, NL, 28])
    hyf1 = din("hyf1", [NL, 33, 64])
    hyf2 = din("hyf2", [NL, 64, 64])
    hyf3 = din("hyf3", [NL, 64, 512])
    hyfb = din("hyfb", [64, NL, 2])
    hyb3 = din("hyb3", [NL, 1, 512])
    dfp = din("dfp", [128, NL, 3])
    dlam = din("dlam", [128, NL, 128])
    cd = {k: din("c_" + k, v.shape, _npdt(v)) for k, v in consts.items()}
    outT = nc.dram_tensor("outT", [D, L], F32, kind="ExternalOutput").ap()
    dbgT = nc.dram_tensor("dbgT", [D, T], F32, kind=("ExternalOutput" if dbg is not None else "Internal")).ap()
    Hl = nc.dram_tensor("Hl", [NL, 16, 128, 2, 2, 256], BF16, kind="Internal").ap()
    Hlny = nc.dram_tensor("Hlny", [NL, 1, 512], BF16, kind="Internal").ap()
    Hc = nc.dram_tensor("Hc", [NL, 2, 128, 2, 2, 256], BF16, kind="Internal").ap()
    Hcny = nc.dram_tensor("Hcny", [NL, 1, 512], BF16, kind="Internal").ap()

    A = Arena(nc, 16640, 229376 - 256)
    psF = [nc.alloc_psum_tensor("psf%d" % i, [128, 512], F32) for i in range(8)]
    psA = Ring([(psF[i], B[("ps", i)]) for i in range(4)])
    psD = psF[3]
    psR = Ring([(psF[i], B[("ps", i)]) for i in range(4, 8)])

    def mm(out, lhsT, rhs, start, stop, reads, writes):
        S.op("pe", lambda: PE.matmul(out, lhsT=lhsT, rhs=rhs, start=start, stop=stop), reads, writes)

    def ld_const(name, q="sp"):
        a = consts[name]
        t = A.alloc(a.shape, _npdt(a))
        S.dma(t[:], cd[name], writes=[B[("c", name)]], q=q)
        return t

    ones_bf = ld_const("ones_bf")
    ones_f = ld_const("ones_f")
    blk64 = ld_const("blk64")
    blk32 = ld_const("blk32")
    ident_bf = ld_const("ident_bf")
    ident_f = ld_const("ident_f")
    sel4 = ld_const("sel4")
    halfmask = ld_const("halfmask")
    hm64 = ld_const("hm64")
    qm32 = ld_const("qm32")
    CB = [B[("c", n)] for n in ("ones_bf", "ones_f", "blk64", "blk32", "ident_bf", "ident_f", "sel4", "halfmask", "hm64", "qm32")]
    epsT = A.alloc([128, 4], F32)
    S.op("dve", lambda: V.memset(epsT[:, 0:1], EPS), writes=[B["eps"]])
    S.op("dve", lambda: V.memset(epsT[:, 1:2], -math.pi), writes=[B["eps"]])
    S.op("dve", lambda: V.memset(epsT[:, 2:3], 1.0), writes=[B["eps"]])
    S.op("dve", lambda: V.memset(epsT[:, 3:4], 0.0), writes=[B["eps"]])
    CB.append(B["eps"])

    dconst = A.alloc([128, 512], BF16)
    S.op("dve", lambda: V.memset(dconst[:], 1.0), writes=[B["dconst"]])
    CB.append(B["dconst"])

    def keep_warm(n):
        S.op("pe", lambda: PE.matmul(psD[:, 0:n], lhsT=ones_bf[:, :], rhs=dconst[:, 0:n], start=True, stop=True))

    def small(name, ap, shape):
        t = A.alloc(shape, F32)
        S.dma(t[:], ap, writes=[B[("c", name)]])
        CB.append(B[("c", name)])
        return t

    adab = small("adab", adabT, [128, NL * 72])
    normg = small("normg", normgT, [128, NL * 3 * KC])
    mlp_t = small("mlp", mlp, [128, NL, 24])
    mlgb_t = small("mlgb", mlgb, [4, NL, 4])
    gqp_t = small("gqp", gqp, [128, NL, 2])
    hyp_t = small("hyp", hyp, [128, NL, 28])
    hyfb_t = small("hyfb", hyfb, [64, NL, 2])
    dfp_t = small("dfp", dfp, [128, NL, 3])
    dlam_t = small("dlam", dlam, [128, NL, 128])
    cc_t = small("cc", cc, [128, KC, 2])
    modT = A.alloc([128, NL, 72, 2], F32)
    modA = A.alloc([128, NL, 3, KC, 2], F32)
    modG = A.alloc([128, NL, 3, KC, 2], F32)
    lamT = A.alloc([128, NL, 4], F32)
    dsub = A.alloc([128, NL], F32)
    hT = A.alloc([128, KC, T], F32)
    base_mark = A.mark()

    def dump_fm(t, nchunks, bufs, ncols=T, col0=0, row0=0):
        evs = []
        for kc in range(nchunks):
            evs.append(S.dma(dbgT[row0 + kc * 128:row0 + (kc + 1) * 128, col0:col0 + ncols], t[:, kc, 0:ncols], reads=bufs))
        S.barrier()
        return evs

    out_evs = []

    for kc in range(KC):
        S.dma(hT[:, kc, 0:CL], ctxT[kc * 128:(kc + 1) * 128, :], writes=[B[("h", kc, 0)]])
        S.dma(hT[:, kc, CL:T], xT[kc * 128:(kc + 1) * 128, :], writes=[B[("h", kc, i)] for i in range(1, 5)])

    m0 = A.mark()
    sT = A.alloc([128, KC, 2], F32)
    S.op("act", lambda: ACT.activation(out=sT[:], in_=cc_t[:], func=AF.Silu), reads=[B[("c", "cc")]], writes=[B["sT"]])
    NST = 1152
    stg = [A.alloc([128, KC, NST], F32) for _ in range(2)]
    stgR = Ring([(stg[i], B[("adastg", i)]) for i in range(2)])
    for l in range(n_layers):
        for si in range(9 * D // NST):
            st, sb = stgR.next()
            S.dma(st[:], ada_w[l].rearrange("(k p) n -> p k n", p=128)[:, :, si * NST:(si + 1) * NST], writes=[sb])
            ps, pb = psR.next()
            nm = NST // 128
            for m in range(nm):
                for kc in range(KC):
                    mm(ps[:, 2 * m:2 * m + 2], st[:, kc, m * 128:(m + 1) * 128], sT[:, kc, :], kc == 0, kc == KC - 1,
                       [sb, B["sT"]], [pb])
            for j in range(2):
                S.op("dve", lambda: V.tensor_tensor(
                    out=modT[:, l, si * nm:(si + 1) * nm, j], in0=ps[:, 0:2 * nm].rearrange("p (m j) -> p m j", j=2)[:, :, j],
                    in1=adab[:, l * 72 + si * nm: l * 72 + (si + 1) * nm], op=ALU.add),
                    reads=[pb, B[("c", "adab")]], writes=[B["mod"]])
    for l in range(n_layers):
        for i in range(3):
            for j in range(2):
                S.op("dve", lambda: V.scalar_tensor_tensor(
                    out=modA[:, l, i, :, j], in0=modT[:, l, (3 * i + 1) * 8:(3 * i + 2) * 8, j], scalar=1.0,
                    in1=normg[:, (l * 3 + i) * KC:(l * 3 + i + 1) * KC], op0=ALU.add, op1=ALU.mult),
                    reads=[B["mod"], B[("c", "normg")]], writes=[B["modA"]])
                S.op("dve", lambda: V.tensor_scalar(
                    out=modG[:, l, i, :, j], in0=modT[:, l, (3 * i + 2) * 8:(3 * i + 3) * 8, j],
                    scalar1=(1.0 if i == 1 else 0.5), scalar2=None, op0=ALU.mult),
                    reads=[B["mod"]], writes=[B["modG"]])
        lam_init = 0.8 - 0.6 * math.exp(-0.3 * l)
        S.op("dve", lambda: V.tensor_tensor(out=dlam_t[:, l, 0:32], in0=dlam_t[:, l, 0:32], in1=dlam_t[:, l, 32:64], op=ALU.mult),
             reads=[B[("c", "dlam")]], writes=[B[("c", "dlam")]])
        S.op("dve", lambda: V.tensor_tensor(out=dlam_t[:, l, 64:96], in0=dlam_t[:, l, 64:96], in1=dlam_t[:, l, 96:128], op=ALU.mult),
             reads=[B[("c", "dlam")]], writes=[B[("c", "dlam")]])
        S.op("dve", lambda: V.reduce_sum(out=lamT[:, l, 0:1], in_=dlam_t[:, l, 0:32], axis=mybir.AxisListType.X),
             reads=[B[("c", "dlam")]], writes=[B["lam"]])
        S.op("dve", lambda: V.reduce_sum(out=lamT[:, l, 1:2], in_=dlam_t[:, l, 64:96], axis=mybir.AxisListType.X),
             reads=[B[("c", "dlam")]], writes=[B["lam"]])
        S.op("act", lambda: ACT.activation(out=lamT[:, l, 0:2], in_=lamT[:, l, 0:2], func=AF.Exp), reads=[B["lam"]], writes=[B["lam"]])
        S.op("dve", lambda: V.tensor_tensor(out=lamT[:, l, 2:3], in0=lamT[:, l, 1:2], in1=lamT[:, l, 0:1], op=ALU.subtract),
             reads=[B["lam"]], writes=[B["lam"]])
        S.op("dve", lambda: V.tensor_scalar(out=lamT[:, l, 2:3], in0=lamT[:, l, 2:3], scalar1=-lam_init, scalar2=None, op0=ALU.add),
             reads=[B["lam"]], writes=[B["lam"]])
        S.op("dve", lambda: V.tensor_scalar(out=dsub[:, l:l + 1], in0=dfp_t[:, l, 2:3], scalar1=(1.0 - lam_init), scalar2=None, op0=ALU.mult),
             reads=[B[("c", "dfp")]], writes=[B["lam"]])
    S.barrier()
    A.release(m0)

    def hyena_filters(l, nm):
        Ln = L if nm == "l" else CL
        ntc = Ln // 128
        m1 = A.mark()
        feats = A.alloc([33, Ln], F32)
        winst = [A.alloc([128, 512], F32) for _ in range(2)]
        winR = Ring([(winst[k], B[("hf_win", k)]) for k in range(2)])
        S.dma(feats[:], cd["feats_" + nm], writes=[B["hf_feats"]])
        w1 = A.alloc([33, 64], F32)
        w2_ = A.alloc([64, 64], F32)
        w3 = A.alloc([64, 512], F32)
        b3 = A.alloc([1, 512], F32)
        S.dma(w1[:], hyf1[l], writes=[B["hf_w"]])
        S.dma(w2_[:], hyf2[l], writes=[B["hf_w"]])
        S.dma(w3[:], hyf3[l], writes=[B["hf_w"]])
        S.dma(b3[:], hyb3[l], writes=[B["hf_w"]])
        bb = A.alloc([64, 2], F32)
        S.op("dve", lambda: V.tensor_scalar(out=bb[:], in0=hyfb_t[:, l, :], scalar1=1.0 / (2 * math.pi), scalar2=0.0,
                                            op0=ALU.mult, op1=ALU.add), reads=[B[("c", "hyfb")]], writes=[B["hf_bb"]])
        h1 = A.alloc([64, Ln], F32)
        h2 = A.alloc([64, Ln], F32)
        tmp = A.alloc([64, 512], F32)
        tmpi = A.alloc([64, 512], mybir.dt.int32)
        tmpk = A.alloc([64, 512], F32)
        nb = max(1, Ln // 512)
        bw = Ln // nb
        for (src, wt, kk, dst, bi, dname, sname) in ((feats, w1, 33, h1, 0, "hf_h1", "hf_feats"), (h1, w2_, 64, h2, 1, "hf_h2", "hf_h1")):
            for b_ in range(nb):
                ps, pb = psR.next()
                mm(ps[0:64, 0:bw], wt[0:kk, :], src[0:kk, b_ * bw:(b_ + 1) * bw], True, True, [B["hf_w"], B[sname]], [pb])
                S.op("dve", lambda: V.tensor_scalar(out=tmp[:, 0:bw], in0=ps[0:64, 0:bw], scalar1=1.0 / (2 * math.pi),
                                                    scalar2=bb[:, bi:bi + 1], op0=ALU.mult, op1=ALU.add),
                     reads=[pb, B["hf_bb"]], writes=[B["hf_tmp"]])
                S.op("dve", lambda: V.tensor_copy(out=tmpi[:, 0:bw], in_=tmp[:, 0:bw]), reads=[B["hf_tmp"]], writes=[B["hf_tmpi"]])
                S.op("dve", lambda: V.tensor_copy(out=tmpk[:, 0:bw], in_=tmpi[:, 0:bw]), reads=[B["hf_tmpi"]], writes=[B["hf_tmpk"]])
                S.op("dve", lambda: V.tensor_tensor(out=tmp[:, 0:bw], in0=tmp[:, 0:bw], in1=tmpk[:, 0:bw], op=ALU.subtract),
                     reads=[B["hf_tmp"], B["hf_tmpk"]], writes=[B["hf_tmp"]])
                S.op("act", lambda: ACT.activation(out=dst[:, b_ * bw:(b_ + 1) * bw], in_=tmp[:, 0:bw], func=AF.Sin,
                                                   bias=epsT[0:64, 3:4], scale=2 * math.pi * (1.0 - 1e-6)),
                     reads=[B["hf_tmp"], B["eps"]], writes=[B[dname]])
        hw = A.alloc([128, ntc, 512], F32)
        ab = A.alloc([128, 512], F32)
        psum_s, pbs = psA.next()
        for tc in range(ntc):
            ps, pb = psR.next()
            wn_, wnb = winR.next()
            S.dma(wn_[:], cd["win_" + nm][:, tc, :], writes=[wnb])
            mm(ps[:, :], h2[:, tc * 128:(tc + 1) * 128], w3[:, :], True, False, [B["hf_h2"], B["hf_w"]], [pb])
            mm(ps[:, :], ones_f[0:1, :], b3[0:1, :], False, True, [B[("c", "ones_f")], B["hf_w"]], [pb])
            S.op("dve", lambda: V.tensor_tensor(out=hw[:, tc, :], in0=ps[:, :], in1=wn_[:], op=ALU.mult),
                 reads=[pb, wnb], writes=[B[("hf_hw", tc)]])
            S.op("act", lambda: ACT.activation(out=ab[:], in_=hw[:, tc, :], func=AF.Abs),
                 reads=[B[("hf_hw", tc)]], writes=[B["hf_ab"]])
            mm(psum_s[:, :], ones_f[:, :], ab[:, :], tc == 0, tc == ntc - 1, [B[("c", "ones_f")], B["hf_ab"]], [pbs])
        rs = A.alloc([128, 512], F32)
        S.op("dve", lambda: V.reciprocal(out=rs[:], in_=psum_s[:, :]), reads=[pbs], writes=[B["hf_rs"]])
        hb = A.alloc([128, ntc, 512], BF16)
        for tc in range(ntc):
            S.op("dve", lambda: V.tensor_tensor(out=hb[:, tc, :], in0=hw[:, tc, :], in1=rs[:], op=ALU.mult),
                 reads=[B[("hf_hw", tc)], B["hf_rs"]], writes=[B["hf_hb"]])
        Fst = [A.alloc([128, ntc, 2, 128], BF16) for _ in range(2)]
        FR = Ring([(Fst[i], B[("hf_F", i)]) for i in range(2)])
        Hst = [A.alloc([128, 2, 2, 256], BF16) for _ in range(2)]
        HR = Ring([(Hst[i], B[("hf_H", i)]) for i in range(2)])
        Hd = Hl if nm == "l" else Hc
        for fc in range(ntc):
            ft, fb = FR.next()
            S.dma(ft[:], cd["F_" + nm][fc], writes=[fb])
            ht, hbuf = HR.next()
            for ri in range(2):
                ps, pb = psR.next()
                for tc in range(ntc):
                    mm(ps[:, :], ft[:, tc, ri, :], hb[:, tc, :], tc == 0, tc == ntc - 1, [fb, B["hf_hb"]], [pb])
                S.op("act", lambda: ACT.copy(out=ht[:, :, ri, :], in_=ps[:, :].rearrange("p (o c) -> p o c", o=2)),
                     reads=[pb], writes=[hbuf])
            S.dma(Hd[l, fc], ht[:], reads=[hbuf], writes=[B[("Hd", nm, l)]], q="pool")
        alt = A.alloc([128, ntc], BF16)
        S.dma(alt[:], cd["alt_" + nm], writes=[B["hf_alt"]])
        ps, pb = psR.next()
        for tc in range(ntc):
            mm(ps[0:1, :], alt[:, tc:tc + 1], hb[:, tc, :], tc == 0, tc == ntc - 1, [B["hf_alt"], B["hf_hb"]], [pb])
        ny = A.alloc([1, 512], BF16)
        S.op("act", lambda: ACT.copy(out=ny[:], in_=ps[0:1, :]), reads=[pb], writes=[B["hf_ny"]])
        S.dma((Hlny if nm == "l" else Hcny)[l], ny[:], reads=[B["hf_ny"]], writes=[B[("Hd", nm, l)]], q="pool")
        if dbg == "hfilt" and l == 0 and nm == "l":
            tmpd = A.alloc([128, ntc, 512], F32)
            S.op("dve", lambda: V.tensor_copy(out=tmpd[:], in_=hb[:]), reads=[B["hf_hb"]], writes=[B["dbgtmp"]])
            for tc in range(4):
                out_evs.append(S.dma(dbgT[tc * 128:(tc + 1) * 128, 0:512], tmpd[:, tc, :], reads=[B["dbgtmp"]]))
        S.barrier()
        A.release(m1)

    for l in range(n_layers):
        hyena_filters(l, "l")
        if l < NL - 1:
            hyena_filters(l, "c")

    xn = A.alloc([128, KC, T], BF16)
    layer_mark = A.mark()

    MN_BYTES = 3 * 1024 + 2 * 2048 + 3 * 2048
    FFN_BYTES = 6 * T * 2 + 2 * (2 * KC * 256 * 2) + 2 * (6 * D * 2) + 2 * 2048

    def modnorm(l, i, top=False):
        m1 = A.mark()
        if top:
            base = A.hi - MN_BYTES
            assert A.mark() + FFN_BYTES <= base, (A.mark(), FFN_BYTES, base)
            offs = [base + 1024 * k for k in range(3)] + [base + 3072 + 2048 * k for k in range(5)]
            al = lambda k, dt: A.alloc([128, 512], dt, at=offs[k])
            sq = [al(k, BF16) for k in range(3)]
            rstd = [al(3 + k, F32) for k in range(2)]
            tm = [al(5 + k, F32) for k in range(3)]
        else:
            sq = [A.alloc([128, 512], BF16) for _ in range(3)]
            rstd = [A.alloc([128, 512], F32) for _ in range(2)]
            tm = [A.alloc([128, 512], F32) for _ in range(3)]
        sqR = Ring([(sq[k], B[("mn_sq", k)]) for k in range(3)])
        rsR = Ring([(rstd[k], B[("mn_rs", k)]) for k in range(2)])
        tmR = Ring([(tm[k], B[("mn_tm", k)]) for k in range(3)])
        for tbi, (t0, n, j) in enumerate(TBS):
            ps, pb = psR.next()
            for kc in range(KC):
                q, qb = sqR.next()
                S.op("act", lambda: ACT.activation(out=q[:, 0:n], in_=hT[:, kc, t0:t0 + n], func=AF.Square),
                     reads=[B[("h", kc, tbi)]], writes=[qb])
                mm(ps[:, 0:n], ones_bf[:, :], q[:, 0:n], kc == 0, kc == KC - 1, [qb, B[("c", "ones_bf")]], [pb])
            r, rb = rsR.next()
            S.op("act", lambda: ACT.activation(out=r[:, 0:n], in_=ps[:, 0:n], func=AF.Sqrt, bias=epsT[:, 0:1], scale=1.0 / D),
                 reads=[pb, B["eps"]], writes=[rb])
            S.op("dve", lambda: V.reciprocal(out=r[:, 0:n], in_=r[:, 0:n]), reads=[rb], writes=[rb])
            for kc in range(KC):
                t_, tb_ = tmR.next()
                S.op("dve", lambda: V.tensor_tensor(out=t_[:, 0:n], in0=hT[:, kc, t0:t0 + n], in1=r[:, 0:n], op=ALU.mult),
                     reads=[B[("h", kc, tbi)], rb], writes=[tb_])
                S.op("act", lambda: ACT.activation(out=xn[:, kc, t0:t0 + n], in_=t_[:, 0:n], func=AF.Identity,
                                                   bias=modT[:, l, 3 * i * 8 + kc, j:j + 1], scale=modA[:, l, i, kc, j:j + 1]),
                     reads=[tb_, B["mod"], B["modA"]], writes=[B[("xn", kc, tbi)]])
        if not top:
            S.barrier()
        A.release(m1)

    GROUPS = [(0, 6), (6, 12), (12, 17), (17, 22)]

    def ffn(l, f, i, skip_ctx=False):
        m1 = A.mark()
        tbs = [(tbi, tb) for tbi, tb in enumerate(TBS) if not (skip_ctx and tbi == 0)]
        g = A.alloc([128, 6, T], BF16)
        wab = [A.alloc([128, 2, KC, 256], BF16) for _ in range(2)]
        wR = Ring([(wab[k], B[("ffn_wab", k)]) for k in range(2)])
        w2t = [A.alloc([128, 6, D], BF16) for _ in range(2)]
        w2R = Ring([(w2t[k], B[("ffn_w2", k)]) for k in range(2)])
        sa = [A.alloc([128, 512], F32) for _ in range(2)]
        saR = Ring([(sa[k], B[("ffn_sa", k)]) for k in range(2)])
        w13v = w13[l, f].rearrange("(k p) n -> p k n", p=128)
        for (j0, j1) in GROUPS:
            nj = j1 - j0
            wt2, wb2 = w2R.next()
            S.dma(wt2[:, 0:nj, :], w2[l, f, j0 * 128:j1 * 128, :].rearrange("(j p) n -> p j n", p=128), writes=[wb2], q="pool")
            jj = j0
            while jj < j1:
                npair = min(2, j1 - jj)
                wt, wb = wR.next()
                S.dma(wt[:, 0, :, 0:npair * 128], w13v[:, :, jj * 128:(jj + npair) * 128], writes=[wb], q="pool")
                S.dma(wt[:, 1, :, 0:npair * 128], w13v[:, :, HID + jj * 128:HID + (jj + npair) * 128], writes=[wb], q="pool")
                for u in range(npair):
                    for tbi, (t0, n, j) in tbs:
                        pa, pab = psR.next()
                        pbb, pbbb = psR.next()
                        for kc in range(KC):
                            mm(pa[:, 0:n], wt[:, 0, kc, u * 128:(u + 1) * 128], xn[:, kc, t0:t0 + n], kc == 0, kc == KC - 1,
                               [wb, B[("xn", kc, tbi)]], [pab])
                        for kc in range(KC):
                            mm(pbb[:, 0:n], wt[:, 1, kc, u * 128:(u + 1) * 128], xn[:, kc, t0:t0 + n], kc == 0, kc == KC - 1,
                               [wb, B[("xn", kc, tbi)]], [pbbb])
                        s_, sb_ = saR.next()
                        S.op("act", lambda: ACT.activation(out=s_[:, 0:n], in_=pa[:, 0:n], func=AF.Silu), reads=[pab], writes=[sb_])
                        S.op("dve", lambda: V.tensor_tensor(out=g[:, jj - j0 + u, t0:t0 + n], in0=pbb[:, 0:n], in1=s_[:, 0:n], op=ALU.mult),
                             reads=[pbbb, sb_], writes=[B[("ffn_g", jj - j0 + u, tbi)]])
                jj += npair
            for tbi, (t0, n, j) in tbs:
                for oc in range(KC):
                    ps, pb = psR.next()
                    for q in range(nj):
                        mm(ps[:, 0:n], wt2[:, q, oc * 128:(oc + 1) * 128], g[:, q, t0:t0 + n], q == 0, q == nj - 1,
                           [wb2, B[("ffn_g", q, tbi)]], [pb])
                    S.op("dve", lambda: V.scalar_tensor_tensor(out=hT[:, oc, t0:t0 + n], in0=ps[:, 0:n], scalar=modG[:, l, i, oc, j:j + 1],
                                                               in1=hT[:, oc, t0:t0 + n], op0=ALU.mult, op1=ALU.add),
                         reads=[pb, B["modG"], B[("h", oc, tbi)]], writes=[B[("h", oc, tbi)]])
        S.barrier()
        A.release(m1)

    def load_w_cols(l, cols_list, name):
        ncols = sum(n for _, n in cols_list)
        t = A.alloc([128, KC, ncols], BF16)
        wv = w_in[l].rearrange("(k p) n -> p k n", p=128)
        o = 0
        for (c0, n) in cols_list:
            S.dma(t[:, :, o:o + n], wv[:, :, c0:c0 + n], writes=[B[("win", name)]], q="pool")
            o += n
        return t, B[("win", name)]

    def proj_fm(wt, wb, c0, m, tbi, skip=None):
        t0, n, j = TBS[tbi]
        ps, pb = psR.next()
        for kc in range(KC):
            mm(ps[0:m, 0:n], wt[:, kc, c0:c0 + m], xn[:, kc, t0:t0 + n], kc == 0, kc == KC - 1, [wb, B[("xn", kc, tbi)]], [pb])
        return ps, pb

    def proj_tm(wt, wb, c0, ncols, sc):
        ps, pb = psR.next()
        tbi = tbi_of_sc(sc)
        for kc in range(KC):
            mm(ps[:, 0:ncols], xn[:, kc, sc * 128:(sc + 1) * 128], wt[:, kc, c0:c0 + ncols], kc == 0, kc == KC - 1,
               [wb, B[("xn", kc, tbi)]], [pb])
        return ps, pb

    def out_proj(l, y, ybufs, g, skip_ctx=False):
        m1 = A.mark()
        wt = A.alloc([128, 2, D], BF16)
        S.dma(wt[:], w_out[l, g * 256:(g + 1) * 256, :].rearrange("(k p) n -> p k n", p=128), writes=[B["wout"]], q="pool")
        for tbi, (t0, n, j) in enumerate(TBS):
            if skip_ctx and tbi == 0:
                continue
            for oc in range(KC):
                ps, pb = psR.next()
                for k in range(2):
                    mm(ps[:, 0:n], wt[:, k, oc * 128:(oc + 1) * 128], y[:, k, t0:t0 + n], k == 0, k == 1, [B["wout"]] + ybufs(k, tbi), [pb])
                S.op("dve", lambda: V.scalar_tensor_tensor(out=hT[:, oc, t0:t0 + n], in0=ps[:, 0:n], scalar=modG[:, l, 1, oc, j:j + 1],
                                                           in1=hT[:, oc, t0:t0 + n], op0=ALU.mult, op1=ALU.add),
                     reads=[pb, B["modG"], B[("h", oc, tbi)]], writes=[B[("h", oc, tbi)]])
        S.barrier()
        A.release(m1)

    def head_rms(src, srcb, n, blk, blkname, dim, dst_rstd, dstb):
        sqt = A.alloc([128, 512], BF16)
        S.op("act", lambda: ACT.activation(out=sqt[:, 0:n], in_=src, func=AF.Square), reads=srcb, writes=[B["hr_sq"]])
        ps, pb = psR.next()
        mm(ps[:, 0:n], blk[:, :], sqt[:, 0:n], True, True, [B["hr_sq"], B[("c", blkname)]], [pb])
        S.op("act", lambda: ACT.activation(out=dst_rstd, in_=ps[:, 0:n], func=AF.Sqrt, bias=epsT[:, 0:1], scale=1.0 / dim),
             reads=[pb, B["eps"]], writes=dstb)
        S.op("dve", lambda: V.reciprocal(out=dst_rstd, in_=dst_rstd), reads=dstb, writes=dstb)

    def attn_core(pairs, kT_of, kbufs, q_of, qbufs, vx_of, vbufs, scale, epilogue, nq=1, depth=3):
        E = [A.alloc([128, 512], BF16) for _ in range(depth + 2)]
        ER = Ring([(E[k], B[("at_E", k)]) for k in range(depth + 2)])
        items = []
        for tbi, scs in pairs:
            accs = None
            for si, sc in enumerate(scs):
                for v in range(nq):
                    items.append((tbi, sc, v, si == 0, si == len(scs) - 1, si == len(scs) - 1 and v == nq - 1))
        state = {}
        pend = []

        def stage_a(it):
            tbi, sc, v, first, last, fin = it
            t0, n, j = TBS[tbi]
            if first and v == 0:
                state[tbi] = [psA.next() for _ in range(nq)]
            ps, pb = psR.next()
            mm(ps[:, 0:n], kT_of(sc), q_of(v, tbi), True, True, kbufs(sc) + qbufs(v, tbi), [pb])
            e, eb = ER.next()
            S.op("act", lambda: ACT.activation(out=e[:, 0:n], in_=ps[:, 0:n], func=AF.Exp, scale=scale), reads=[pb], writes=[eb])
            return (it, e, eb)

        def stage_b(c):
            (tbi, sc, v, first, last, fin), e, eb = c
            t0, n, j = TBS[tbi]
            acc, ab = state[tbi][v]
            mm(acc[:, 0:n], vx_of(sc), e[:, 0:n], first, last, vbufs(sc) + [eb], [ab])
            if fin:
                epilogue(tbi, state[tbi])

        for idx in range(len(items) + depth):
            if idx < len(items):
                pend.append(stage_a(items[idx]))
            if idx >= depth:
                stage_b(pend.pop(0))

    def qk_tmp_ring(nsets=2):
        sets = []
        for k in range(nsets):
            sets.append(dict(raw=A.alloc([128, 512], F32), rs=A.alloc([128, 512], F32), qn=A.alloc([128, 512], F32),
                             cs=A.alloc([128, 2, 512], F32), t1=A.alloc([128, 512], F32), sq=A.alloc([128, 512], BF16), k=k))
        return Ring(sets)

    def qk_norm_rope(l, ps, pb, tbi, gain_ap, blk, blkname, dim, rot, csname, dst, dstb, extra=None, tmpR=None):
        t0, n, j = TBS[tbi]
        ts = tmpR.next()
        k_ = ts["k"]
        raw, rs, qn, cs, t1, sq = ts["raw"], ts["rs"], ts["qn"], ts["cs"], ts["t1"], ts["sq"]
        braw, brs, bqn, bcs, bt1, bsq = (B[("qn_" + nm, k_)] for nm in ("raw", "rs", "qn", "cs", "t1", "sq"))
        S.op("act", lambda: ACT.copy(out=raw[:, 0:n], in_=ps[:, 0:n]), reads=[pb], writes=[braw])
        S.op("act", lambda: ACT.activation(out=sq[:, 0:n], in_=raw[:, 0:n], func=AF.Square), reads=[braw], writes=[bsq])
        pq, pqb = psR.next()
        mm(pq[:, 0:n], blk[:, :], sq[:, 0:n], True, True, [bsq, B[("c", blkname)]], [pqb])
        S.op("act", lambda: ACT.activation(out=rs[:, 0:n], in_=pq[:, 0:n], func=AF.Sqrt, bias=epsT[:, 0:1], scale=1.0 / dim),
             reads=[pqb, B["eps"]], writes=[brs])
        S.op("dve", lambda: V.reciprocal(out=rs[:, 0:n], in_=rs[:, 0:n]), reads=[brs], writes=[brs])
        S.op("dve", lambda: V.scalar_tensor_tensor(out=qn[:, 0:n], in0=raw[:, 0:n], scalar=gain_ap, in1=rs[:, 0:n], op0=ALU.mult, op1=ALU.mult),
             reads=[braw, brs, B[("c", "gqp")], B[("c", "dfp")]], writes=[bqn])
        if tbi != 0:
            pr, prb = psR.next()
            mm(pr[:, 0:n], rot[:, :], qn[:, 0:n], True, True, [bqn, B["rope_c"]], [prb])
            l0 = t0 - CL
            S.dma(cs[:, 0, 0:n], cd["cos" + csname][:, l0:l0 + n], writes=[bcs])
            S.dma(cs[:, 1, 0:n], cd["sin" + csname][:, l0:l0 + n], writes=[bcs])
            S.op("dve", lambda: V.tensor_tensor(out=t1[:, 0:n], in0=pr[:, 0:n], in1=cs[:, 1, 0:n], op=ALU.mult),
                 reads=[prb, bcs], writes=[bt1])
            S.op("dve", lambda: V.tensor_tensor(out=qn[:, 0:n], in0=qn[:, 0:n], in1=cs[:, 0, 0:n], op=ALU.mult),
                 reads=[bqn, bcs], writes=[bqn])
            S.op("dve", lambda: V.tensor_tensor(out=qn[:, 0:n], in0=qn[:, 0:n], in1=t1[:, 0:n], op=ALU.add),
                 reads=[bqn, bt1], writes=[bqn])
        if extra is None:
            S.op("act", lambda: ACT.copy(out=dst, in_=qn[:, 0:n]), reads=[bqn], writes=dstb)
        else:
            extra(qn, bqn, n)

    def vx_build(l, wv, wvb, c0, nh, vx, name):
        S.op("dve", lambda: V.memset(vx[:], 1.0), writes=[B[(name, t)] for t in range(5)])
        for sc in range(NSC):
            ps, pb = proj_tm(wv, wvb, c0, nh * 64, sc)
            S.op("act", lambda: ACT.copy(out=vx[:, sc, :, 0:64], in_=ps[:, 0:nh * 64].rearrange("p (h d) -> p h d", d=64)),
                 reads=[pb], writes=[B[(name, tbi_of_sc(sc))]])

    def lat_pairs(need_ctx):
        pairs = []
        if need_ctx:
            pairs.append((0, [0, 1]))
        for tbi in range(1, 5):
            pairs.append((tbi, list(range(NSC))))
        return pairs

    def gqa_mixer(l, need_ctx):
        ROW0 = 256
        m1 = A.mark()
        y = A.alloc([128, 2, T], BF16)
        rot = A.alloc([128, 128], F32)
        S.dma(rot[:], cd["rot64"], writes=[B["rope_c"]])
        b0 = 1040
        wq, wqb = load_w_cols(l, [(b0, 64), (b0 + 128, 64), (b0 + 64, 64), (b0 + 192, 64)], "gq_q")
        wk, wkb = load_w_cols(l, [(b0 + 256, 128)], "gq_k")
        wv, wvb = load_w_cols(l, [(b0 + 384, 128)], "gq_v")
        qT = A.alloc([128, 2, 2, T], BF16)
        kT = A.alloc([128, T], BF16)
        vx = A.alloc([128, NSC, 2, 128], BF16)
        vx_build(l, wv, wvb, 0, 2, vx, "gq_vx")
        mtmp = A.mark()
        tmpR = qk_tmp_ring()
        for tbi, (t0, n, j) in enumerate(TBS):
            for ch in range(2):
                ps, pb = proj_fm(wq, wqb, ch * 128, 128, tbi)

                def extra_q(fin, finb, n, ch=ch, t0=t0, tbi=tbi):
                    for hf in range(2):
                        S.op("dve", lambda: V.tensor_scalar(out=qT[:, ch, hf, t0:t0 + n], in0=fin[:, 0:n], scalar1=hm64[:, hf:hf + 1],
                                                            scalar2=None, op0=ALU.mult),
                             reads=[finb, B[("c", "hm64")]], writes=[B[("gq_q", ch, tbi)]])

                qk_norm_rope(l, ps, pb, tbi, gqp_t[:, l, 0:1], blk64, "blk64", 64, rot, "64",
                             None, None, extra=extra_q, tmpR=tmpR)
            ps, pb = proj_fm(wk, wkb, 0, 128, tbi)
            qk_norm_rope(l, ps, pb, tbi, gqp_t[:, l, 1:2], blk64, "blk64", 64, rot, "64",
                         kT[:, t0:t0 + n], [B[("gq_k", tbi)]], tmpR=tmpR)
        S.barrier()
        A.release(mtmp)
        rc = A.alloc([128, 512], F32)
        for h in range(4):
            ch, r0 = h % 2, (h // 2) * 64
            kvh = h // 2

            def epi(tbi, accs, h=h):
                t0, n, j = TBS[tbi]
                acc, ab = accs[0]
                S.op("dve", lambda: V.reciprocal(out=rc[64:128, 0:n], in_=acc[64:128, 0:n]), reads=[ab], writes=[B["gq_rc"]])
                o0 = (h % 2) * 64
                S.op("dve", lambda: V.tensor_tensor(out=y[o0:o0 + 64, h // 2, t0:t0 + n], in0=acc[0:64, 0:n], in1=rc[64:128, 0:n], op=ALU.mult),
                     reads=[ab, B["gq_rc"]], writes=[B[("gq_y", h // 2, tbi)]])

            attn_core(lat_pairs(need_ctx),
                      lambda sc: kT[:, sc * 128:(sc + 1) * 128], lambda sc: [B[("gq_k", tbi_of_sc(sc))]],
                      lambda v, tbi: qT[:, ch, kvh, TBS[tbi][0]:TBS[tbi][0] + TBS[tbi][1]], lambda v, tbi: [B[("gq_q", ch, tbi)]],
                      lambda sc: vx[:, sc, kvh, :], lambda sc: [B[("gq_vx", tbi_of_sc(sc))]],
                      0.125, epi)
        S.barrier()
        A.release(m1 + 2 * T * 2)
        if dbg in ("gqa", "allmix") and l == 0:
            yf = A.alloc([128, 2, T], F32)
            S.op("dve", lambda: V.tensor_copy(out=yf[:], in_=y[:]), reads=[B[("gq_y", k, t)] for k in range(2) for t in range(5)], writes=[B["dbgtmp"]])
            out_evs.extend(dump_fm(yf, 2, [B["dbgtmp"]], row0=(ROW0 if dbg == "allmix" else 0)))
        out_proj(l, y, lambda k, tbi: [B[("gq_y", k, tbi)]], 1, skip_ctx=not need_ctx)
        A.release(m1)

    def diff_mixer(l, need_ctx):
        ROW0 = 768
        m1 = A.mark()
        y = A.alloc([128, 2, T], BF16)
        rot = A.alloc([128, 128], F32)
        S.dma(rot[:], cd["rot32"], writes=[B["rope_c"]])
        b0 = 2320
        for ch in range(2):
            mch = A.mark()
            wq, wqb = load_w_cols(l, [(b0 + ch * 128, 128)], "df_q")
            wk, wkb = load_w_cols(l, [(b0 + 256 + ch * 128, 128)], "df_k")
            wv, wvb = load_w_cols(l, [(b0 + 512 + ch * 128, 128)], "df_v")
            qz = A.alloc([128, 4, T], BF16)
            kT = A.alloc([128, T], BF16)
            vx = A.alloc([128, NSC, 2, 128], BF16)
            vx_build(l, wv, wvb, 0, 2, vx, "df_vx")
            mtmp = A.mark()
            tmpR = qk_tmp_ring()
            for tbi, (t0, n, j) in enumerate(TBS):
                ps, pb = proj_fm(wq, wqb, 0, 128, tbi)

                def extra(fin, finb, n, t0=t0, tbi=tbi):
                    for v in range(4):
                        S.op("dve", lambda: V.tensor_scalar(out=qz[:, v, t0:t0 + n], in0=fin[:, 0:n], scalar1=qm32[:, v:v + 1],
                                                            scalar2=None, op0=ALU.mult),
                             reads=[finb, B[("c", "qm32")]], writes=[B[("df_q", v, tbi)]])

                qk_norm_rope(l, ps, pb, tbi, dfp_t[:, l, 0:1], blk32, "blk32", 32, rot, "32", None, None, extra=extra, tmpR=tmpR)
                ps, pb = proj_fm(wk, wkb, 0, 128, tbi)
                qk_norm_rope(l, ps, pb, tbi, dfp_t[:, l, 1:2], blk32, "blk32", 32, rot, "32",
                             kT[:, t0:t0 + n], [B[("df_k", tbi)]], tmpR=tmpR)
            S.barrier()
            A.release(mtmp)
            yraw = A.alloc([128, T], F32)
            r1 = A.alloc([128, 512], F32)
            r2 = A.alloc([128, 512], F32)
            t1 = A.alloc([128, 512], F32)
            t2 = A.alloc([128, 512], F32)
            for hh in range(2):
                r0 = hh * 64

                def epi(tbi, accs, r0=r0):
                    t0, n, j = TBS[tbi]
                    (a1, ab1), (a2, ab2) = accs
                    S.op("dve", lambda: V.reciprocal(out=r1[64:128, 0:n], in_=a1[64:128, 0:n]), reads=[ab1], writes=[B["df_r1"]])
                    S.op("dve", lambda: V.reciprocal(out=r2[64:128, 0:n], in_=a2[64:128, 0:n]), reads=[ab2], writes=[B["df_r2"]])
                    S.op("dve", lambda: V.tensor_tensor(out=t1[0:64, 0:n], in0=a1[0:64, 0:n], in1=r1[64:128, 0:n], op=ALU.mult),
                         reads=[ab1, B["df_r1"]], writes=[B["df_t1"]])
                    S.op("dve", lambda: V.tensor_tensor(out=t2[0:64, 0:n], in0=a2[0:64, 0:n], in1=r2[64:128, 0:n], op=ALU.mult),
                         reads=[ab2, B["df_r2"]], writes=[B["df_t2"]])
                    if r0 == 0:
                        S.op("dve", lambda: V.scalar_tensor_tensor(out=yraw[0:64, t0:t0 + n], in0=t2[0:64, 0:n], scalar=lamT[0:64, l, 2:3],
                                                                   in1=t1[0:64, 0:n], op0=ALU.mult, op1=ALU.add),
                             reads=[B["df_t1"], B["df_t2"], B["lam"]], writes=[B[("df_yraw", tbi)]])
                    else:
                        S.op("dve", lambda: V.scalar_tensor_tensor(out=t1[0:64, 0:n], in0=t2[0:64, 0:n], scalar=lamT[0:64, l, 2:3],
                                                                   in1=t1[0:64, 0:n], op0=ALU.mult, op1=ALU.add),
                             reads=[B["df_t1"], B["df_t2"], B["lam"]], writes=[B["df_t1"]])
                        S.op("act", lambda: ACT.copy(out=yraw[64:128, t0:t0 + n], in_=t1[0:64, 0:n]), reads=[B["df_t1"]], writes=[B[("df_yraw", tbi)]])

                attn_core(lat_pairs(need_ctx),
                          lambda sc: kT[:, sc * 128:(sc + 1) * 128], lambda sc: [B[("df_k", tbi_of_sc(sc))]],
                          lambda v, tbi: qz[:, hh * 2 + v, TBS[tbi][0]:TBS[tbi][0] + TBS[tbi][1]], lambda v, tbi: [B[("df_q", hh * 2 + v, tbi)]],
                          lambda sc: vx[:, sc, hh, :], lambda sc: [B[("df_vx", tbi_of_sc(sc))]],
                          32 ** -0.5, epi, nq=2)
            rs = A.alloc([128, 512], F32)
            for tbi, (t0, n, j) in enumerate(TBS):
                if tbi == 0 and not need_ctx:
                    continue
                m2 = A.mark()
                head_rms(yraw[:, t0:t0 + n], [B[("df_yraw", tbi)]], n, blk64, "blk64", 64, rs[:, 0:n], [B["df_rs"]])
                S.op("dve", lambda: V.scalar_tensor_tensor(out=y[:, ch, t0:t0 + n], in0=yraw[:, t0:t0 + n], scalar=dsub[:, l:l + 1],
                                                           in1=rs[:, 0:n], op0=ALU.mult, op1=ALU.mult),
                     reads=[B[("df_yraw", tbi)], B["df_rs"], B["lam"]], writes=[B[("df_y", ch, tbi)]])
                A.release(m2)
            S.barrier()
            A.release(mch)
        A.release(m1 + 2 * T * 2)
        if dbg in ("diff", "allmix") and l == 0:
            yf = A.alloc([128, 2, T], F32)
            S.op("dve", lambda: V.tensor_copy(out=yf[:], in_=y[:]), reads=[B[("df_y", k, t)] for k in range(2) for t in range(5)], writes=[B["dbgtmp"]])
            out_evs.extend(dump_fm(yf, 2, [B["dbgtmp"]], row0=(ROW0 if dbg == "allmix" else 0)))
        out_proj(l, y, lambda k, tbi: [B[("df_y", k, tbi)]], 3, skip_ctx=not need_ctx)
        A.release(m1)

    PW = 2308

    def pcol(t):
        return 1 + t if t < CL else 3 + t

    def dwconv(praw, prb, wcols, bcol, out_fn):
        for (t0, n) in ((0, CL), (CL, 512), (CL + 512, 512), (CL + 1024, 512), (CL + 1536, 512)):
            c = pcol(t0)
            acc = A.alloc([128, 512], F32)
            S.op("dve", lambda: V.tensor_scalar(out=acc[:, 0:n], in0=praw[:, c - 1:c - 1 + n], scalar1=wcols[0], scalar2=None, op0=ALU.mult),
                 reads=prb, writes=[B["dw_acc"]])
            S.op("dve", lambda: V.scalar_tensor_tensor(out=acc[:, 0:n], in0=praw[:, c:c + n], scalar=wcols[1], in1=acc[:, 0:n],
                                                       op0=ALU.mult, op1=ALU.add), reads=prb + [B["dw_acc"]], writes=[B["dw_acc"]])
            S.op("dve", lambda: V.scalar_tensor_tensor(out=acc[:, 0:n], in0=praw[:, c + 1:c + 1 + n], scalar=wcols[2], in1=acc[:, 0:n],
                                                       op0=ALU.mult, op1=ALU.add), reads=prb + [B["dw_acc"]], writes=[B["dw_acc"]])
            out_fn(t0, n, acc, B["dw_acc"])
            A.release(A.mark() - 512 * 4)

    def proj_to_praw(wt, wb, c0, praw, prbuf):
        S.op("dve", lambda: V.memset(praw[:, 0:1], 0.0), writes=[prbuf])
        S.op("dve", lambda: V.memset(praw[:, 257:259], 0.0), writes=[prbuf])
        S.op("dve", lambda: V.memset(praw[:, 2307:2308], 0.0), writes=[prbuf])
        for tbi, (t0, n, j) in enumerate(TBS):
            ps, pb = proj_fm(wt, wb, c0, 128, tbi)
            c = pcol(t0)
            S.op("act", lambda: ACT.copy(out=praw[:, c:c + n], in_=ps[:, 0:n]), reads=[pb], writes=[prbuf])

    def mlstm_mixer(l, need_ctx):
        ROW0 = 0
        m1 = A.mark()
        y = A.alloc([128, 2, T], BF16)
        potA = [A.alloc([4, T], F32) for _ in range(2)]
        CT = A.alloc([128, NSC, 8], F32)
        mg = A.mark()
        wg, wgb = load_w_cols(l, [(1024, 16)], "ml_g")
        gi = A.alloc([4, T], F32)
        gf = A.alloc([4, T], F32)
        onesr = A.alloc([4, T], F32)
        tot = A.alloc([4, 1], F32)
        S.op("dve", lambda: V.memset(onesr[:], 1.0), writes=[B["ml_ones"]])
        for d_ in range(2):
            ty_i, ty_f = 2 * d_, 2 * d_ + 1
            for (ty, dstt, bn) in ((ty_i, gi, "ml_gi"), (ty_f, gf, "ml_gf")):
                for tbi, (t0, n, j) in enumerate(TBS):
                    ps, pb = proj_fm(wg, wgb, ty * 4, 4, tbi)
                    S.op("act", lambda: ACT.activation(out=dstt[:, t0:t0 + n], in_=ps[0:4, 0:n], func=AF.Identity,
                                                       bias=mlgb_t[:, l, ty:ty + 1], scale=1.0),
                         reads=[pb, B[("c", "mlgb")]], writes=[B[bn]])
            lfb = B["ml_gf"]
            S.op("act", lambda: ACT.activation(out=gf[:], in_=gf[:], func=AF.Exp, scale=-1.0), reads=[lfb], writes=[lfb])
            S.op("act", lambda: ACT.activation(out=gf[:], in_=gf[:], func=AF.Ln, bias=epsT[0:4, 2:3], scale=1.0), reads=[lfb, B["eps"]], writes=[lfb])
            S.op("dve", lambda: V.tensor_scalar(out=gf[:], in0=gf[:], scalar1=-1.0, scalar2=None, op0=ALU.mult), reads=[lfb], writes=[lfb])
            Ap, Ab = potA[d_], B[("ml_potA", d_)]
            if d_ == 0:
                S.op("dve", lambda: V.tensor_tensor_scan(out=Ap[:], data0=onesr[:], data1=gf[:], initial=0.0, op0=ALU.mult, op1=ALU.add),
                     reads=[lfb, B["ml_ones"]], writes=[Ab])
            else:
                for (c0, n) in ((0, CL), (CL, L)):
                    S.op("dve", lambda: V.tensor_tensor_scan(out=Ap[:, c0:c0 + n], data0=onesr[:, c0:c0 + n], data1=gf[:, c0:c0 + n], initial=0.0,
                                                             op0=ALU.mult, op1=ALU.add), reads=[lfb, B["ml_ones"]], writes=[Ab])
                S.op("dve", lambda: V.tensor_copy(out=tot[:], in_=Ap[:, T - 1:T]), reads=[Ab], writes=[B["ml_tot"]])
                S.op("dve", lambda: V.tensor_tensor(out=Ap[:], in0=gf[:], in1=Ap[:], op=ALU.subtract), reads=[lfb, Ab, B["ml_tot"]], writes=[Ab])
                S.op("dve", lambda: V.tensor_scalar(out=Ap[:, CL:T], in0=Ap[:, CL:T], scalar1=tot[:, 0:1], scalar2=None, op0=ALU.add),
                     reads=[Ab, B["ml_tot"]], writes=[Ab])
            S.op("dve", lambda: V.scalar_tensor_tensor(out=gi[:], in0=gi[:], scalar=math.log(0.125), in1=Ap[:], op0=ALU.add, op1=ALU.subtract),
                 reads=[B["ml_gi"], Ab], writes=[B["ml_gi"]])
            ps, pb = psR.next()
            for sc in range(NSC):
                mm(ps[:, sc * 4:sc * 4 + 4], gi[0:4, sc * 128:(sc + 1) * 128], ident_f[0:4, 0:4], True, True,
                   [B["ml_gi"], B[("c", "ident_f")]], [pb])
            S.op("dve", lambda: V.tensor_copy(out=CT[:, :, d_ * 4:(d_ + 1) * 4], in_=ps[:, 0:NSC * 4].rearrange("p (s e) -> p s e", e=4)),
                 reads=[pb], writes=[B["ml_CT"]])
        S.barrier()
        A.release(mg)
        mf = A.alloc([128, 128], BF16)
        mb = A.alloc([128, 128], BF16)
        S.dma(mf[:], cd["mask_f"][:, 384:512], writes=[B["ml_mask"]])
        S.dma(mb[:], cd["mask_b"][:, 384:512], writes=[B["ml_mask"]])
        et = A.alloc([128, 512], F32)
        negr = A.alloc([128, 1], F32)
        rpos = A.alloc([128, 1], F32)
        es = A.alloc([128, NSC], F32)
        Wt = [A.alloc([128, 512], BF16) for _ in range(4)]
        WR = Ring([(Wt[k], B[("ml_W", k)]) for k in range(4)])
        q1 = A.alloc([128, 512], F32)
        q3 = A.alloc([128, 512], F32)
        hd = A.alloc([128, 512], F32)
        sets = []
        for k in range(2):
            st_ = (A.alloc([128, 512], F32), A.alloc([128, 512], F32), A.alloc([128, 4, 4], F32), A.alloc([128, 8], F32), A.alloc([128, 8], F32))
            S.op("dve", lambda: V.memset(st_[3][:], 0.0), writes=[B[("ml_set", k)]])
            sets.append((st_, B[("ml_set", k)]))
        setR = Ring(sets)
        rs, t3, sgt = q1, q3, hd
        for ch in range(2):
            mch = A.mark()
            wv, wvb = load_w_cols(l, [(512 + ch * 128, 128)], "ml_wv")
            wo, wob = load_w_cols(l, [(768 + ch * 128, 128)], "ml_wo")
            qzm = A.alloc([128, 2, T], BF16)
            kc_ = A.alloc([128, T], BF16)
            m2 = A.mark()
            wq, wqb = load_w_cols(l, [(ch * 128, 128)], "ml_wq")
            wk, wkb = load_w_cols(l, [(256 + ch * 128, 128)], "ml_wk")
            qc = A.alloc([128, T], BF16)
            praw = A.alloc([128, PW], F32)
            for (wt_, wtb_, cidx, dstt, bn) in ((wq, wqb, ch, qc, "ml_q"), (wk, wkb, 2 + ch, kc_, "ml_k")):
                proj_to_praw(wt_, wtb_, 0, praw, B["ml_praw"])

                def fin(t0, n, acc, ab, cidx=cidx, dstt=dstt, bn=bn):
                    S.op("act", lambda: ACT.activation(out=dstt[:, t0:t0 + n], in_=acc[:, 0:n], func=AF.Silu,
                                                       bias=mlp_t[:, l, 12 + cidx:13 + cidx], scale=1.0),
                         reads=[ab, B[("c", "mlp")]], writes=[B[bn]])

                dwconv(praw, [B["ml_praw"]], [mlp_t[:, l, k * 4 + cidx:k * 4 + cidx + 1] for k in range(3)], None, fin)
            for (c0_, n_) in ((0, CL), (CL, 1024), (CL + 1024, 1024)):
                for hf in range(2):
                    S.op("dve", lambda: V.tensor_scalar(out=qzm[:, hf, c0_:c0_ + n_], in0=qc[:, c0_:c0_ + n_], scalar1=hm64[:, hf:hf + 1],
                                                        scalar2=None, op0=ALU.mult),
                         reads=[B["ml_q"], B[("c", "hm64")]], writes=[B["ml_qz"]])
            S.barrier()
            A.release(m2)
            vx = A.alloc([128, NSC, 2, 128], BF16)
            vx_build(l, wv, wvb, 0, 2, vx, "ml_vx")
            hsum = A.alloc([128, T], F32)
            jobs = []
            for hh in range(2):
                for d_ in range(2):
                    for tbi in range(5):
                        if tbi == 0 and not need_ctx:
                            continue
                        jobs.append((hh, d_, tbi))
            DEPTH = 2
            pend = []

            def setup(job):
                hh, d_, tbi = job
                t0, n, j = TBS[tbi]
                r0 = hh * 64
                h = 2 * ch + hh
                sc_lo = t0 // 128
                sc_hi = (t0 + n) // 128
                nsub = n // 128
                if tbi == 0:
                    off_scs = []
                elif d_ == 0:
                    off_scs = list(range(0, sc_lo))
                else:
                    off_scs = [0, 1] + list(range(sc_hi, NSC))
                (etd, etm, esd, rr, nrr), sbuf_ = setR.next()
                pbc, pbcb = psR.next()
                mm(pbc[:, 0:n], sel4[0:4, h, :], potA[d_][0:4, t0:t0 + n], True, True, [B[("c", "sel4")], B[("ml_potA", d_)]], [pbcb])
                rcol = 0 if d_ == 0 else n - 1
                S.op("dve", lambda: V.tensor_copy(out=rr[:, 0:1], in_=pbc[:, rcol:rcol + 1]), reads=[pbcb], writes=[sbuf_])
                for tt in range(nsub):
                    cc_ = tt * 128 + (0 if d_ == 0 else 127)
                    S.op("dve", lambda: V.tensor_copy(out=rr[:, 1 + tt:2 + tt], in_=pbc[:, cc_:cc_ + 1]), reads=[pbcb], writes=[sbuf_])
                S.op("dve", lambda: V.tensor_scalar(out=nrr[:, 0:5], in0=rr[:, 0:5], scalar1=-1.0, scalar2=None, op0=ALU.mult),
                     reads=[sbuf_], writes=[sbuf_])
                if off_scs:
                    S.op("act", lambda: ACT.activation(out=et[:, 0:n], in_=pbc[:, 0:n], func=AF.Exp, bias=nrr[:, 0:1], scale=1.0),
                         reads=[pbcb, sbuf_], writes=[B["ml_et"]])
                    S.op("act", lambda: ACT.activation(out=es[:, :], in_=CT[:, :, d_ * 4 + h], func=AF.Exp, bias=rr[:, 0:1], scale=1.0),
                         reads=[B["ml_CT"], sbuf_], writes=[B["ml_es"]])
                tri = mf if d_ == 0 else mb
                for tt in range(nsub):
                    S.op("act", lambda: ACT.activation(out=etd[:, tt * 128:(tt + 1) * 128], in_=pbc[:, tt * 128:(tt + 1) * 128], func=AF.Exp,
                                                       bias=nrr[:, 1 + tt:2 + tt], scale=1.0),
                         reads=[pbcb, sbuf_], writes=[sbuf_])
                    S.op("dve", lambda: V.tensor_tensor(out=etm[:, tt * 128:(tt + 1) * 128], in0=etd[:, tt * 128:(tt + 1) * 128], in1=tri[:, :], op=ALU.mult),
                         reads=[sbuf_, B["ml_mask"]], writes=[sbuf_])
                    S.op("act", lambda: ACT.activation(out=esd[:, tt, 0:nsub], in_=CT[:, sc_lo:sc_lo + nsub, d_ * 4 + h], func=AF.Exp,
                                                       bias=rr[:, 1 + tt:2 + tt], scale=1.0),
                         reads=[B["ml_CT"], sbuf_], writes=[sbuf_])
                acc, accb = psA.next()
                items = []
                first = True
                for sc in off_scs:
                    items.append(dict(sc=sc, c0=0, w=n, es=es[:, sc:sc + 1], tgt=et[:, 0:n], rd=[B["ml_es"], B["ml_et"]], start=first, stop=False))
                    first = False
                for tt in range(nsub):
                    ks = list(range(0, tt + 1)) if d_ == 0 else list(range(tt, nsub))
                    for ki, k in enumerate(ks):
                        items.append(dict(sc=sc_lo + k, c0=tt * 128, w=128, es=esd[:, tt, k:k + 1],
                                          tgt=(etm if k == tt else etd)[:, tt * 128:(tt + 1) * 128], rd=[sbuf_],
                                          start=(first and ki == 0), stop=(ki == len(ks) - 1)))
                for it in items:
                    it.update(job=job, acc=acc, accb=accb, r0=r0, hh=hh, t0=t0, n=n, fin=False)
                items[-1]["fin"] = True
                return items

            def stage_a(it):
                r0, t0, c0, w, sc = it["r0"], it["t0"], it["c0"], it["w"], it["sc"]
                ps, pb = psR.next()
                mm(ps[:, 0:w], kc_[:, sc * 128:(sc + 1) * 128], qzm[:, it["hh"], t0 + c0:t0 + c0 + w], True, True,
                   [B["ml_k"], B["ml_qz"]], [pb])
                w_, wb_ = WR.next()
                S.op("dve", lambda: V.scalar_tensor_tensor(out=w_[:, 0:w], in0=ps[:, 0:w], scalar=it["es"], in1=it["tgt"],
                                                           op0=ALU.mult, op1=ALU.mult),
                     reads=[pb] + it["rd"], writes=[wb_])
                it["w_"], it["wb_"] = w_, wb_
                return it

            def stage_b(it):
                acc, accb, c0, w, sc, hh = it["acc"], it["accb"], it["c0"], it["w"], it["sc"], it["hh"]
                mm(acc[:, c0:c0 + w], vx[:, sc, hh, :], it["w_"][:, 0:w], it["start"], it["stop"], [B[("ml_vx", tbi_of_sc(sc))], it["wb_"]], [accb])
                if not it["fin"]:
                    return
                hh_, d_, tbi = it["job"]
                t0, n, r0 = it["t0"], it["n"], it["r0"]
                S.op("dve", lambda: V.tensor_scalar(out=q3[64:128, 0:n], in0=acc[64:128, 0:n], scalar1=-1.0, scalar2=1.0, op0=ALU.mult, op1=ALU.max),
                     reads=[accb], writes=[B["ml_q3"]])
                S.op("dve", lambda: V.tensor_tensor(out=q3[64:128, 0:n], in0=acc[64:128, 0:n], in1=q3[64:128, 0:n], op=ALU.max),
                     reads=[accb, B["ml_q3"]], writes=[B["ml_q3"]])
                S.op("act", lambda: ACT.activation(out=q3[64:128, 0:n], in_=q3[64:128, 0:n], func=AF.Ln), reads=[B["ml_q3"]], writes=[B["ml_q3"]])
                S.op("act", lambda: ACT.activation(out=q3[64:128, 0:n], in_=q3[64:128, 0:n], func=AF.Exp, scale=-1.0), reads=[B["ml_q3"]], writes=[B["ml_q3"]])
                if d_ == 0:
                    S.op("dve", lambda: V.tensor_tensor(out=hsum[r0:r0 + 64, t0:t0 + n], in0=acc[0:64, 0:n], in1=q3[64:128, 0:n], op=ALU.mult),
                         reads=[accb, B["ml_q3"]], writes=[B[("ml_hs", tbi)]])
                else:
                    S.op("dve", lambda: V.tensor_tensor(out=hd[r0:r0 + 64, 0:n], in0=acc[0:64, 0:n], in1=q3[64:128, 0:n], op=ALU.mult),
                         reads=[accb, B["ml_q3"]], writes=[B["ml_hd"]])
                    S.op("dve", lambda: V.tensor_tensor(out=hsum[r0:r0 + 64, t0:t0 + n], in0=hsum[r0:r0 + 64, t0:t0 + n], in1=hd[r0:r0 + 64, 0:n], op=ALU.add),
                         reads=[B["ml_hd"], B[("ml_hs", tbi)]], writes=[B[("ml_hs", tbi)]])

            for job in jobs:
                for it in setup(job):
                    pend.append(stage_a(it))
                    if len(pend) > DEPTH:
                        stage_b(pend.pop(0))
            while pend:
                stage_b(pend.pop(0))
            S.barrier()
            for tbi, (t0, n, j) in enumerate(TBS):
                if tbi == 0 and not need_ctx:
                    continue
                m3 = A.mark()
                head_rms(hsum[:, t0:t0 + n], [B[("ml_hs", tbi)]], n, blk64, "blk64", 64, rs[:, 0:n], [B["ml_rs"]])
                S.op("dve", lambda: V.scalar_tensor_tensor(out=t3[:, 0:n], in0=hsum[:, t0:t0 + n], scalar=mlp_t[:, l, 16 + ch:17 + ch],
                                                           in1=rs[:, 0:n], op0=ALU.mult, op1=ALU.mult),
                     reads=[B[("ml_hs", tbi)], B["ml_rs"], B[("c", "mlp")]], writes=[B["ml_t3"]])
                ps, pb = proj_fm(wo, wob, 0, 128, tbi)
                S.op("act", lambda: ACT.activation(out=sgt[:, 0:n], in_=ps[:, 0:n], func=AF.Sigmoid), reads=[pb], writes=[B["ml_sg"]])
                S.op("dve", lambda: V.tensor_tensor(out=y[:, ch, t0:t0 + n], in0=t3[:, 0:n], in1=sgt[:, 0:n], op=ALU.mult),
                     reads=[B["ml_t3"], B["ml_sg"]], writes=[B[("ml_y", ch, tbi)]])
                A.release(m3)
            S.barrier()
            A.release(mch)
        S.barrier()
        A.release(m1 + 2 * T * 2)
        if dbg in ("mlstm", "allmix") and l == 0:
            yf = A.alloc([128, 2, T], F32)
            S.op("dve", lambda: V.tensor_copy(out=yf[:], in_=y[:]), reads=[B[("ml_y", k, t)] for k in range(2) for t in range(5)], writes=[B["dbgtmp"]])
            out_evs.extend(dump_fm(yf, 2, [B["dbgtmp"]], row0=(ROW0 if dbg == "allmix" else 0)))
        out_proj(l, y, lambda k, tbi: [B[("ml_y", k, tbi)]], 0, skip_ctx=not need_ctx)
        A.release(m1)

    def hyena_mixer(l, need_ctx):
        ROW0 = 512
        m1 = A.mark()
        y = A.alloc([128, 2, T], BF16)
        b0 = 1552
        vf = A.alloc([128, 2, T], BF16)
        zf = A.alloc([128, 2, T], BF16)
        x1 = A.alloc([128, 2, T], BF16)
        x2 = A.alloc([128, 2, T], BF16)
        m2 = A.mark()
        wh, whb = load_w_cols(l, [(b0, 768)], "hy_w")
        praw = A.alloc([128, PW], F32)
        dsts = [vf, vf, x1, x1, x2, x2]
        for c6 in range(6):
            proj_to_praw(wh, whb, c6 * 128, praw, B["hy_praw"])

            def fin(t0, n, acc, ab, c6=c6):
                S.op("act", lambda: ACT.activation(out=dsts[c6][:, c6 % 2, t0:t0 + n], in_=acc[:, 0:n], func=AF.Identity,
                                                   bias=hyp_t[:, l, 18 + c6:19 + c6], scale=1.0),
                     reads=[ab, B[("c", "hyp")]], writes=[B[("hy_u", c6)]])

            dwconv(praw, [B["hy_praw"]], [hyp_t[:, l, k * 6 + c6:k * 6 + c6 + 1] for k in range(3)], None, fin)
        S.barrier()
        A.release(m2)
        xn_off = 16640 + (base_mark - 16640)
        A2 = Arena(nc, base_mark, base_mark + KC * T * 2)
        A2.n = 100000 + l * 1000

        def conv_seg(nm, order, src, srcb, t_lo, Ln, epi):
            ntc = Ln // 128
            a2m = A2.mark()
            am = A.mark()
            utok = A.alloc([128, ntc, 256], BF16)
            for tc in range(ntc):
                pt, ptb = psR.next()
                for cc_ in range(2):
                    mm(pt[:, cc_ * 128:(cc_ + 1) * 128], src[:, cc_, t_lo + tc * 128:t_lo + (tc + 1) * 128], ident_bf[:, :], True, True,
                       srcb + [B[("c", "ident_bf")]], [ptb])
                S.op("act", lambda: ACT.copy(out=utok[:, tc, :], in_=pt[:, 0:256]), reads=[ptb], writes=[B["hy_utok"]])
            Y = A2.alloc([128, ntc, 2, 256], BF16)
            Yny = A2.alloc([1, 256], BF16)
            Fst = [A2.alloc([128, ntc, 2, 128], BF16) for _ in range(2)]
            FR = Ring([(Fst[k], B[("hy_F", k)]) for k in range(2)])
            Hst = [A2.alloc([128, 2, 256], BF16) for _ in range(2)]
            HR = Ring([(Hst[k], B[("hy_H", k)]) for k in range(2)])
            Hd = Hl if nm == "l" else Hc
            tt = [A.alloc([128, 256], F32) for _ in range(4)]
            for fc in range(ntc):
                ft, fb = FR.next()
                S.dma(ft[:], cd["F_" + nm][fc], writes=[fb])
                ht, hb_ = HR.next()
                S.dma(ht[:], Hd[l, fc, :, order, :, :], reads=[B[("Hd", nm, l)]], writes=[hb_])
                ps, pb = psR.next()
                for ri in range(2):
                    for tc in range(ntc):
                        mm(ps[:, ri * 256:(ri + 1) * 256], ft[:, tc, ri, :], utok[:, tc, :], tc == 0, tc == ntc - 1, [fb, B["hy_utok"]], [pb])
                ure, uim = ps[:, 0:256], ps[:, 256:512]
                S.op("dve", lambda: V.tensor_tensor(out=tt[0][:], in0=ure, in1=ht[:, 0, :], op=ALU.mult), reads=[pb, hb_], writes=[B["hy_tt0"]])
                S.op("dve", lambda: V.tensor_tensor(out=tt[1][:], in0=uim, in1=ht[:, 1, :], op=ALU.mult), reads=[pb, hb_], writes=[B["hy_tt1"]])
                S.op("dve", lambda: V.tensor_tensor(out=Y[:, fc, 0, :], in0=tt[0][:], in1=tt[1][:], op=ALU.subtract),
                     reads=[B["hy_tt0"], B["hy_tt1"]], writes=[B["hy_Y"]])
                S.op("dve", lambda: V.tensor_tensor(out=tt[2][:], in0=ure, in1=ht[:, 1, :], op=ALU.mult), reads=[pb, hb_], writes=[B["hy_tt2"]])
                S.op("dve", lambda: V.tensor_tensor(out=tt[3][:], in0=uim, in1=ht[:, 0, :], op=ALU.mult), reads=[pb, hb_], writes=[B["hy_tt3"]])
                S.op("dve", lambda: V.tensor_tensor(out=Y[:, fc, 1, :], in0=tt[2][:], in1=tt[3][:], op=ALU.add),
                     reads=[B["hy_tt2"], B["hy_tt3"]], writes=[B["hy_Y"]])
            alt = A2.alloc([128, ntc], BF16)
            S.dma(alt[:], cd["alt_" + nm], writes=[B["hy_alt"]])
            hny = A2.alloc([1, 256], BF16)
            S.dma(hny[:], (Hlny if nm == "l" else Hcny)[l, :, order * 256:(order + 1) * 256], reads=[B[("Hd", nm, l)]], writes=[B["hy_hny"]])
            icny = A.alloc([1, Ln], BF16)
            S.dma(icny[:], cd["icny_" + nm], writes=[B["hy_icny"]])
            ps, pb = psR.next()
            for tc in range(ntc):
                mm(ps[0:1, 0:256], alt[:, tc:tc + 1], utok[:, tc, :], tc == 0, tc == ntc - 1, [B["hy_alt"], B["hy_utok"]], [pb])
            S.op("dve", lambda: V.tensor_tensor(out=Yny[:], in0=ps[0:1, 0:256], in1=hny[:], op=ALU.mult), reads=[pb, B["hy_hny"]], writes=[B["hy_Yny"]])
            nbk = max(1, Ln // 512)
            bw = Ln // nbk
            nhalf = 4 if ntc >= 8 else 1
            fh = ntc // nhalf
            Ist = [A.alloc([128, fh, 2, bw], BF16) for _ in range(2)]
            IR = Ring([(Ist[k], B[("hy_I", k)]) for k in range(2)])
            for bk in range(nbk):
                accs = [psA.next() for _ in range(2)]
                for hf in range(nhalf):
                    it, ib = IR.next()
                    S.dma(it[:], cd["I_" + nm][:, hf * fh:(hf + 1) * fh, :, bk * bw:(bk + 1) * bw], writes=[ib])
                    for cc_ in range(2):
                        for fq in range(fh):
                            fc = hf * fh + fq
                            for ri in range(2):
                                mm(accs[cc_][0][:, 0:bw], Y[:, fc, ri, cc_ * 128:(cc_ + 1) * 128], it[:, fq, ri, :],
                                   (fc == 0 and ri == 0), False, [B["hy_Y"], ib], [accs[cc_][1]])
                for cc_ in range(2):
                    mm(accs[cc_][0][:, 0:bw], Yny[0:1, cc_ * 128:(cc_ + 1) * 128], icny[0:1, bk * bw:(bk + 1) * bw], False, True,
                       [B["hy_Yny"], B["hy_icny"]], [accs[cc_][1]])
                    epi(cc_, t_lo + bk * bw, bw, accs[cc_][0], accs[cc_][1])
            S.barrier()
            A.release(am)
            A2.release(a2m)

        tmpc = A.alloc([128, 512], F32)
        for (nm, t_lo, Ln) in ((("c", 0, CL),) if need_ctx else ()) + (("l", CL, L),):
            def epi1(cc_, t0, n, ps, pb):
                S.op("dve", lambda: V.scalar_tensor_tensor(out=tmpc[:, 0:n], in0=vf[:, cc_, t0:t0 + n], scalar=hyp_t[:, l, 24 + cc_:25 + cc_],
                                                           in1=ps[:, 0:n], op0=ALU.mult, op1=ALU.add),
                     reads=[pb, B[("hy_u", cc_)], B[("c", "hyp")]], writes=[B["hy_tmpc"]])
                S.op("dve", lambda: V.tensor_tensor(out=zf[:, cc_, t0:t0 + n], in0=tmpc[:, 0:n], in1=x1[:, cc_, t0:t0 + n], op=ALU.mult),
                     reads=[B["hy_tmpc"], B[("hy_u", 2 + cc_)]], writes=[B["hy_z"]])

            conv_seg(nm, 0, vf, [B[("hy_u", 0)], B[("hy_u", 1)]], t_lo, Ln, epi1)

            def epi2(cc_, t0, n, ps, pb):
                S.op("dve", lambda: V.scalar_tensor_tensor(out=tmpc[:, 0:n], in0=zf[:, cc_, t0:t0 + n], scalar=hyp_t[:, l, 26 + cc_:27 + cc_],
                                                           in1=ps[:, 0:n], op0=ALU.mult, op1=ALU.add),
                     reads=[pb, B["hy_z"], B[("c", "hyp")]], writes=[B["hy_tmpc"]])
                S.op("dve", lambda: V.tensor_tensor(out=y[:, cc_, t0:t0 + n], in0=tmpc[:, 0:n], in1=x2[:, cc_, t0:t0 + n], op=ALU.mult),
                     reads=[B["hy_tmpc"], B[("hy_u", 4 + cc_)]], writes=[B["hy_y"]])

            conv_seg(nm, 1, zf, [B["hy_z"]], t_lo, Ln, epi2)
        if dbg in ("hyena", "allmix") and l == 0:
            yf = A.alloc([128, 2, T], F32)
            S.op("dve", lambda: V.tensor_copy(out=yf[:], in_=y[:]), reads=[B["hy_y"]], writes=[B["dbgtmp"]])
            out_evs.extend(dump_fm(yf, 2, [B["dbgtmp"]], row0=(ROW0 if dbg == "allmix" else 0)))
        S.barrier()
        A.release(m1 + 2 * T * 2)
        out_proj(l, y, lambda k, tbi: [B["hy_y"]], 2, skip_ctx=not need_ctx)
        A.release(m1)

    def finish_dbg(t, nchunks, bufs):
        out_evs.extend(dump_fm(t, nchunks, bufs))

    stop = False

    def scoped(name, fn, *a, **kw):
        with nc.named_scope(name):
            return fn(*a, **kw)

    for l in range(n_layers):
        need_ctx = l < NL - 1
        scoped("L%d_norm0" % l, modnorm, l, 0, top=True)
        scoped("L%d_ffn0" % l, ffn, l, 0, 0)
        if dbg == "ffn1" and l == 0:
            finish_dbg(hT, KC, [B[("h", kc, t)] for kc in range(KC) for t in range(5)])
            stop = True
            break
        scoped("L%d_norm1" % l, modnorm, l, 1)
        scoped("L%d_gqa" % l, gqa_mixer, l, need_ctx)
        scoped("L%d_diff" % l, diff_mixer, l, need_ctx)
        scoped("L%d_mlstm" % l, mlstm_mixer, l, need_ctx)
        scoped("L%d_hyena" % l, hyena_mixer, l, need_ctx)
        if dbg == "allmix":
            stop = True
            break
        scoped("L%d_norm2" % l, modnorm, l, 2, top=True)
        scoped("L%d_ffn1" % l, ffn, l, 1, 2, skip_ctx=not need_ctx)

    for kc in range(KC):
        out_evs.append(S.dma(outT[kc * 128:(kc + 1) * 128, :], hT[:, kc, CL:T], reads=[B[("h", kc, t)] for t in range(1, 5)]))
    for ev in out_evs:
        S._wait("sp", ev)
    for e_ in S.h:
        S._flush(e_)
    S.close()
    return nc


def _tile_rows(v, reps):
    return np.tile(np.asarray(v, np.float32), reps)


def make_in_maps(inputs, consts):
    f = lambda a: np.ascontiguousarray(np.asarray(a, dtype=np.float32))
    x, c, ctx, c_ctx = f(inputs["x"]), f(inputs["c"]), f(inputs["ctx"]), f(inputs["c_ctx"])
    shared = {}
    shared["ada_w"] = f(inputs["ada_w"])
    shared["adabT"] = np.ascontiguousarray(f(inputs["ada_b"]).reshape(NL, 72, 128).transpose(2, 0, 1).reshape(128, NL * 72))
    shared["normgT"] = np.ascontiguousarray(f(inputs["norm_g"]).reshape(NL, 3, KC, 128).transpose(3, 0, 1, 2).reshape(128, NL * 3 * KC))
    shared["ffn_w13"] = f(inputs["ffn_w13"])
    shared["ffn_w2"] = f(inputs["ffn_w2"])
    shared["w_in"] = f(inputs["w_in"])
    shared["w_out"] = f(inputs["w_out"])
    mlp = np.zeros((128, NL, 24), np.float32)
    cw = f(inputs["ml_conv_w"]).reshape(NL, 3, 4, 128)
    mlp[:, :, 0:12] = cw.transpose(3, 0, 1, 2).reshape(128, NL, 12)
    mlp[:, :, 12:16] = f(inputs["ml_conv_b"]).reshape(NL, 4, 128).transpose(2, 0, 1)
    mlp[:, :, 16:18] = f(inputs["ml_norm_g"]).reshape(NL, 2, 128).transpose(2, 0, 1)
    shared["mlp"] = mlp
    shared["mlgb"] = np.ascontiguousarray(f(inputs["ml_gate_b"]).reshape(NL, 4, 4).transpose(2, 0, 1))
    gq = f(inputs["gqa_qk_g"])
    shared["gqp"] = np.ascontiguousarray(np.tile(gq, (1, 1, 2)).transpose(2, 0, 1))
    hyp = np.zeros((128, NL, 28), np.float32)
    hw = f(inputs["hy_conv_w"]).reshape(NL, 3, 6, 128)
    hyp[:, :, 0:18] = hw.transpose(3, 0, 1, 2).reshape(128, NL, 18)
    hyp[:, :, 18:24] = f(inputs["hy_conv_b"]).reshape(NL, 6, 128).transpose(2, 0, 1)
    hyp[:, :, 24:28] = f(inputs["hy_skip"]).reshape(NL, 2, 2, 128).transpose(3, 0, 1, 2).reshape(128, NL, 4)
    shared["hyp"] = hyp
    shared["hyf1"] = f(inputs["hy_filt_w1"])
    shared["hyf2"] = f(inputs["hy_filt_w2"])
    shared["hyf3"] = f(inputs["hy_filt_w3"])
    shared["hyfb"] = np.ascontiguousarray(np.stack([f(inputs["hy_filt_b1"]), f(inputs["hy_filt_b2"])], -1).transpose(1, 0, 2))
    shared["hyb3"] = f(inputs["hy_filt_b3"]).reshape(NL, 1, 512)
    dq = f(inputs["diff_qk_g"])
    dfp = np.zeros((128, NL, 3), np.float32)
    dfp[:, :, 0:2] = np.tile(dq, (1, 1, 4)).transpose(2, 0, 1)
    dfp[:, :, 2] = np.tile(f(inputs["diff_subln_g"]), (1, 2)).T
    shared["dfp"] = dfp
    shared["dlam"] = np.ascontiguousarray(np.broadcast_to(f(inputs["diff_lambda"]).reshape(1, NL, 128), (128, NL, 128)))
    for k, v in consts.items():
        shared["c_" + k] = v
    maps = []
    for b in range(8):
        m = dict(shared)
        m["xT"] = np.ascontiguousarray(x[b].T)
        m["ctxT"] = np.ascontiguousarray(ctx[b].T)
        ccb = np.stack([c[b].reshape(KC, 128).T, c_ctx.reshape(KC, 128).T], -1)
        m["cc"] = np.ascontiguousarray(ccb)
        maps.append(m)
    return maps


def run(inputs, n_layers=NL, dbg=None, cores=8):
    consts = host_consts()
    nc = build(consts, n_layers=n_layers, dbg=dbg)
    maps = make_in_maps(inputs, consts)[:cores]
    res = run_bass_kernel_spmd(nc, maps, core_ids=list(range(cores)))
    return res


def kernel(**inputs):
    res = run(inputs)
    out = np.stack([np.ascontiguousarray(r["outT"].T) for r in res.results], 0)
    return out.astype(np.float32)
```
